# Optimizing a Trainium2 kernel written in Bass

```python
import math
import jax, jax.numpy as jnp
from jax import lax
import numpy as np

D_MODEL = 1024
BATCH = 8
SEQ = 2048
DEPTH = 1

HEAD_DIM = 64
HEADS_PER_GROUP = 8
ATTN_GROUPS = ((128, 1), (512, 4), (2048, 16))
N_ATTN_GROUPS = len(ATTN_GROUPS)
ATTN_WIDTH = HEADS_PER_GROUP * HEAD_DIM
Q_BLOCK = 128
ROPE_THETA = 500000.0
ROPE_DIMS = HEAD_DIM // 4
GMLP_WIDTH = 512
GMLP_GROUPS = 8
GMLP_GROUP_DIM = GMLP_WIDTH // GMLP_GROUPS
CHUNK = 128
EPS = 1e-6

QKV_COLS = N_ATTN_GROUPS * 3 * ATTN_WIDTH
OFF_GATE_A = QKV_COLS
OFF_Z_B = OFF_GATE_A + ATTN_WIDTH
OFF_GATE_B = OFF_Z_B + 2 * GMLP_WIDTH
OFF_MERGE_A = OFF_GATE_B + GMLP_WIDTH
OFF_MERGE_B = OFF_MERGE_A + D_MODEL
IN_COLS = OFF_MERGE_B + D_MODEL

kernel_name = "hybrid_dilated_attn_gmlp_block"


def rms_norm(x, g):
    xf = x.astype(jnp.float32)
    y = xf * lax.rsqrt(jnp.mean(xf * xf, axis=-1, keepdims=True) + EPS)
    return (y * g.astype(jnp.float32)).astype(x.dtype)


def layer_norm(x, g, b):
    xf = x.astype(jnp.float32)
    mu = jnp.mean(xf, axis=-1, keepdims=True)
    var = jnp.mean(jnp.square(xf - mu), axis=-1, keepdims=True)
    y = (xf - mu) * lax.rsqrt(var + EPS) * g.astype(jnp.float32) + b.astype(jnp.float32)
    return y.astype(x.dtype)


def partial_rope(x, positions):
    half = ROPE_DIMS // 2
    freqs = ROPE_THETA ** (-jnp.arange(0, ROPE_DIMS, 2, dtype=jnp.float32) / ROPE_DIMS)
    ang = positions.astype(jnp.float32)[..., None] * freqs
    cos = jnp.cos(ang)[:, :, None, :]
    sin = jnp.sin(ang)[:, :, None, :]
    xf = x.astype(jnp.float32)
    x1 = xf[..., :half]
    x2 = xf[..., half:ROPE_DIMS]
    rot = jnp.concatenate([x1 * cos - x2 * sin, x2 * cos + x1 * sin, xf[..., ROPE_DIMS:]], axis=-1)
    return rot.astype(x.dtype)


def dilated_window_attention(q, k, v, window, dilation):
    B, S, H, E = q.shape
    n_back = window // dilation
    L = S // dilation
    n_blk = -(-L // Q_BLOCK)
    Lp = n_blk * Q_BLOCK
    kb_len = Q_BLOCK + n_back
    qr = q.astype(jnp.float32).reshape(B, L, dilation, H, E)
    kr = k.astype(jnp.float32).reshape(B, L, dilation, H, E)
    vr = v.astype(jnp.float32).reshape(B, L, dilation, H, E)
    qp = jnp.pad(qr, ((0, 0), (0, Lp - L), (0, 0), (0, 0), (0, 0)))
    qb = qp.reshape(B, n_blk, Q_BLOCK, dilation, H, E)
    pad_k = ((0, 0), (n_back, Lp - L), (0, 0), (0, 0), (0, 0))
    kp = jnp.pad(kr, pad_k)
    vp = jnp.pad(vr, pad_k)
    idx = (jnp.arange(n_blk) * Q_BLOCK)[:, None] + jnp.arange(kb_len)[None, :]
    kb = kp[:, idx]
    vb = vp[:, idx]
    s = jnp.einsum('bnqrhe,bnkrhe->brhnqk', qb, kb) * (1.0 / math.sqrt(E))
    qq = jnp.arange(Q_BLOCK)[:, None]
    kk = jnp.arange(kb_len)[None, :]
    blk = jnp.arange(n_blk)[:, None, None]
    diff = qq + n_back - kk
    key_m = blk * Q_BLOCK - n_back + kk
    valid = (diff >= 0) & (diff <= n_back) & (key_m >= 0)
    s = jnp.where(valid, s, -jnp.inf)
    m = jnp.max(s, axis=-1, keepdims=True)
    p = jnp.exp(s - m)
    den = jnp.sum(p, axis=-1)
    lse = m[..., 0] + jnp.log(den)
    o = jnp.einsum('brhnqk,bnkrhe->bnqrhe', p, vb)
    o = o / jnp.transpose(den, (0, 3, 4, 1, 2))[..., None]
    o = o.reshape(B, Lp, dilation, H, E)[:, :L].reshape(B, S, H, E)
    lse = jnp.transpose(lse, (0, 3, 4, 1, 2)).reshape(B, Lp, dilation, H)[:, :L].reshape(B, S, H)
    return o, lse


def setup_inputs(seed: int = 0) -> dict:
    key = jax.random.key(seed)
    ks = jax.random.split(key, 16)
    D = D_MODEL
    f32 = jnp.float32
    x = jax.random.normal(ks[0], (BATCH, SEQ, D), f32)
    c = jax.random.normal(ks[1], (BATCH, D), f32)
    offs = jax.random.randint(ks[2], (BATCH, 1), 0, 1024, dtype=jnp.int32)
    positions = offs + jnp.arange(SEQ, dtype=jnp.int32)[None, :]
    norm_g = 1.0 + 0.02 * jax.random.normal(ks[3], (D,), f32)
    w_ada = 0.2 * jax.random.normal(ks[4], (D, 3 * D), f32) * D ** -0.5
    b_ada = 0.02 * jax.random.normal(ks[5], (3 * D,), f32)
    w_in = jax.random.normal(ks[6], (D, IN_COLS), f32) * D ** -0.5
    q_norm_g = 1.0 + 0.02 * jax.random.normal(ks[7], (N_ATTN_GROUPS, HEAD_DIM), f32)
    k_norm_g = 1.0 + 0.02 * jax.random.normal(ks[8], (N_ATTN_GROUPS, HEAD_DIM), f32)
    sgu_ln_g = 1.0 + 0.02 * jax.random.normal(ks[9], (GMLP_WIDTH,), f32)
    sgu_ln_b = 0.02 * jax.random.normal(ks[10], (GMLP_WIDTH,), f32)
    w_spatial = 0.5 * jax.random.normal(ks[11], (GMLP_GROUPS, CHUNK, CHUNK), f32) * CHUNK ** -0.5
    b_spatial = 1.0 + 0.02 * jax.random.normal(ks[12], (GMLP_GROUPS, CHUNK), f32)
    w_branch_a = jax.random.normal(ks[13], (ATTN_WIDTH, D), f32) * ATTN_WIDTH ** -0.5
    w_branch_b = jax.random.normal(ks[14], (GMLP_WIDTH, D), f32) * GMLP_WIDTH ** -0.5
    w_out = jax.random.normal(ks[15], (D, D), f32) * D ** -0.5
    return {"x": x, "c": c, "positions": positions, "norm_g": norm_g,
            "w_ada": w_ada, "b_ada": b_ada, "w_in": w_in,
            "q_norm_g": q_norm_g, "k_norm_g": k_norm_g,
            "sgu_ln_g": sgu_ln_g, "sgu_ln_b": sgu_ln_b,
            "w_spatial": w_spatial, "b_spatial": b_spatial,
            "w_branch_a": w_branch_a, "w_branch_b": w_branch_b, "w_out": w_out}


def reference(x, c, positions, norm_g, w_ada, b_ada, w_in, q_norm_g, k_norm_g,
              sgu_ln_g, sgu_ln_b, w_spatial, b_spatial, w_branch_a, w_branch_b, w_out):
    B, S, D = x.shape
    causal_chunk = jnp.tril(jnp.ones((CHUNK, CHUNK), dtype=bool))
    for layer in range(DEPTH):
        ada = jax.nn.silu(c) @ w_ada + b_ada
        shift, scale, gate = jnp.split(ada, 3, axis=-1)
        h = rms_norm(x, norm_g) * (1.0 + scale[:, None, :]) + shift[:, None, :]

        z = h @ w_in
        qkv = z[..., :QKV_COLS].reshape(B, S, N_ATTN_GROUPS, 3, HEADS_PER_GROUP, HEAD_DIM)

        outs, lses = [], []
        for g, (window, dilation) in enumerate(ATTN_GROUPS):
            q = partial_rope(rms_norm(qkv[:, :, g, 0], q_norm_g[g]), positions)
            k = partial_rope(rms_norm(qkv[:, :, g, 1], k_norm_g[g]), positions)
            v = qkv[:, :, g, 2]
            o, lse = dilated_window_attention(q, k, v, window, dilation)
            outs.append(o)
            lses.append(lse)
        o_all = jnp.stack(outs, axis=0)
        w_grp = jax.nn.softmax(jnp.stack(lses, axis=0), axis=0)
        attn = jnp.sum(w_grp[..., None] * o_all, axis=0).reshape(B, S, ATTN_WIDTH).astype(x.dtype)
        y_a = attn * jax.nn.silu(z[..., OFF_GATE_A:OFF_Z_B])

        uv = jax.nn.gelu(z[..., OFF_Z_B:OFF_GATE_B], approximate=False)
        u, v = jnp.split(uv, 2, axis=-1)
        v = layer_norm(v, sgu_ln_g, sgu_ln_b)
        n_chunks = S // CHUNK
        vc = v.reshape(B, n_chunks, CHUNK, GMLP_GROUPS, GMLP_GROUP_DIM)
        w_s = jnp.where(causal_chunk[None], w_spatial, 0.0)
        sv = jnp.einsum('gts,bnsgc->bntgc', w_s, vc) + b_spatial.T[None, None, :, :, None]
        y_b = u * sv.reshape(B, S, GMLP_WIDTH)
        y_b = y_b * jax.nn.silu(z[..., OFF_GATE_B:OFF_MERGE_A])

        merged = (jax.nn.sigmoid(z[..., OFF_MERGE_A:OFF_MERGE_B]) * (y_a @ w_branch_a)
                  + jax.nn.sigmoid(z[..., OFF_MERGE_B:IN_COLS]) * (y_b @ w_branch_b))
        out = merged @ w_out
        x = x + gate[:, None, :] * out
    return x
```

```python
import math
from contextlib import ExitStack

import numpy as np
import concourse.bass as bass
import concourse.mybir as mybir
from concourse.bass_utils import run_bass_kernel_spmd

F32 = mybir.dt.float32
BF16 = mybir.dt.bfloat16
I32 = mybir.dt.int32
AF = mybir.ActivationFunctionType
ALU = mybir.AluOpType

D = 1024
S = 2048
NT = 16
KC = 8
IN_COLS = 8704
OFF_GATE_A = 4608
OFF_Z_B = 5120
OFF_GATE_B = 6144
OFF_MERGE_A = 6656
OFF_MERGE_B = 7680
EPS = 1e-6
DIL = (1, 4, 16)
ROPE_THETA = 500000.0

ENGS = ("pe", "act", "dve", "pool", "sp")


class Prog:
    def __init__(self):
        self.ops = {e: [] for e in ENGS}
        self.count = {}
        self.waited = {e: {} for e in ENGS}
        self.lastw = {}
        self.readers = {}
        self.bank_i = 0
        self.nb = 8

    def _need(self, eng, reads, writes):
        need = {}

        def add(sk, v, war=False):
            if sk == eng and eng == "pe":
                return
            if need.get(sk, 0) < v:
                need[sk] = v

        for k in reads:
            w = self.lastw.get(k)
            if w:
                add(*w)
        for k in writes:
            w = self.lastw.get(k)
            if w:
                add(*w)
            for r in self.readers.get(k, ()):
                add(r[0], r[1], war=True)
        out = []
        for sk, v in need.items():
            if self.waited[eng].get(sk, 0) >= v:
                continue
            self.waited[eng][sk] = v
            out.append((sk, v))
        return out

    def op(self, eng, fn, reads=(), writes=(), slot=None):
        waits = self._need(eng, reads, writes)
        if slot is None:
            sk, inc = eng, 1
        else:
            sk, inc = "dma_" + slot, 16
        self.count[sk] = self.count.get(sk, 0) + inc
        val = self.count[sk]
        self.ops[eng].append((waits, fn, sk, inc))
        for k in reads:
            self.readers.setdefault(k, []).append((sk, val))
        for k in writes:
            self.lastw[k] = (sk, val)
            self.readers[k] = []

    def barrier(self):
        for e in ENGS:
            waits = []
            for sk, v in self.count.items():
                if sk == e and e == "pe":
                    continue
                if self.waited[e].get(sk, 0) >= v:
                    continue
                self.waited[e][sk] = v
                waits.append((sk, v))
            if waits:
                self.ops[e].append((waits, None, None, 0))
        self.lastw = {}
        self.readers = {}

    def bank(self):
        i = self.bank_i
        self.bank_i = (i + 1) % self.nb
        return i


class Arena:
    def __init__(self, nc, nbytes):
        self.t = nc.alloc_sbuf_tensor("arena", [128, nbytes // 2], BF16)
        self.ap = self.t.ap()
        self.off = 0
        self.cap = nbytes
        self.peak = 0

    def at(self, off, shape, dtype):
        save, savep = self.off, self.peak
        self.off = off
        v = self.alloc(shape, dtype)
        self.off, self.peak = save, savep
        return v

    def alloc(self, shape, dtype):
        es = 2 if dtype == BF16 else 4
        n = 1
        for s in shape[1:]:
            n *= s
        nb = (n * es + 63) // 64 * 64
        o = self.off
        self.off += nb
        self.peak = max(self.peak, self.off)
        assert self.off <= self.cap, ("SBUF arena overflow", self.off, self.cap)
        v = self.ap[:, o // 2:(o + n * es) // 2]
        if dtype != BF16:
            v = v.bitcast(dtype)
        if len(shape) == 3:
            v = v.rearrange("p (a b) -> p a b", a=shape[1])
        elif len(shape) == 4:
            v = v.rearrange("p (a b c) -> p a b c", a=shape[1], b=shape[2])
        return v


class _Stop(Exception):
    pass


def build_nc(debug=False, stop_after=99):
    nc = bass.Bass("TRN2", target_bir_lowering=False)
    dt = nc.dram_tensor
    x_d = dt("x", [S, D], F32, kind="ExternalInput").ap()
    cT_d = dt("cT", [128, 8], F32, kind="ExternalInput").ap()
    pos_d = dt("pos", [1, S], I32, kind="ExternalInput").ap()
    wada_d = dt("w_ada", [D, 3 * D], F32, kind="ExternalInput").ap()
    adab_d = dt("adab", [128, 24], F32, kind="ExternalInput").ap()
    normg_d = dt("normg", [128, 8], F32, kind="ExternalInput").ap()
    win_d = dt("w_in", [D, IN_COLS], F32, kind="ExternalInput").ap()
    gqk_d = dt("gqk", [128, 6], F32, kind="ExternalInput").ap()
    lng_d = dt("lng", [1, 512], F32, kind="ExternalInput").ap()
    lnb_d = dt("lnb", [1, 512], F32, kind="ExternalInput").ap()
    wsp_d = dt("wsp", [128, 8, 128], F32, kind="ExternalInput").ap()
    bsp_d = dt("bsp", [128, 4, 128], F32, kind="ExternalInput").ap()
    wba_d = dt("w_ba", [512, D], F32, kind="ExternalInput").ap()
    wbb_d = dt("w_bb", [512, D], F32, kind="ExternalInput").ap()
    wout_d = dt("w_out", [D, D], F32, kind="ExternalInput").ap()
    freq_d = dt("freq", [128, 2], F32, kind="ExternalInput").ap()
    cst_d = dt("cst", [128, 704], F32, kind="ExternalInput").ap()
    out_d = dt("out", [S, D], F32, kind="ExternalOutput").ap()
    dbg = {}
    if debug:
        for nm, shp, ty in (("d_hT", [128, 8 * S], BF16), ("d_V", [128, 3 * 16 * 512], BF16),
                            ("d_qk", [128, 6 * S], BF16), ("d_ya", [128, 4 * S], BF16),
                            ("d_yb", [128, 4 * S], BF16), ("d_mg", [128, 8 * S], BF16),
                            ("d_tab", [128, 2 * S], BF16), ("d_ada", [128, 24], F32)):
            dbg[nm] = dt(nm, shp, ty, kind="ExternalOutput").ap()

    win_v = win_d.rearrange("(kc p) c -> p kc c", p=128)
    wada_v = wada_d.rearrange("(kc p) c -> p kc c", p=128)

    P = Prog()
    A = Arena(nc, 206 * 1024)
    banks = [nc.alloc_psum_tensor("bank%d" % i, [128, 512], F32).ap() for i in range(8)]

    def B(i):
        return "B%d" % i

    hT = A.alloc([128, KC, S], BF16)
    cst = A.alloc([128, 704], BF16)
    ident = cst[:, 0:128]
    bones = cst[:, 128:256]
    perm = cst[:, 256:384]
    mD = cst[:, 384:512]
    mP = cst[:, 512:640]
    ones64 = cst[:, 640:704]
    mDP = A.alloc([128, 4, 128], BF16)
    mDD = A.alloc([128, 4, 128], BF16)
    WsT = A.alloc([128, 8, 128], BF16)
    bsp = A.alloc([128, 4, 128], F32)
    Ctab = A.alloc([128, S], BF16)
    Stab = A.alloc([128, S], BF16)
    small = A.alloc([128, 128], F32)
    cT = small[:, 0:8]
    sc = small[:, 8:16]
    adab = small[:, 16:40]
    ada = small[:, 40:64]
    normg = small[:, 64:72]
    Acol = small[:, 72:80]
    gqk = small[:, 80:86]
    freq = small[:, 86:87]
    freq_lo = small[:, 87:88]
    ssq = small[:, 88:104]
    rstd_x = small[:, 104:120]
    bnst = small[:, 120:126]
    bnag = small[:, 126:128]
    small2 = A.alloc([128, 64], F32)
    srt = small2[:, 0:16]
    lnr = small2[:, 16:18]
    Bcol = ada[:, 0:8]
    gatecol = ada[:, 16:24]
    negpi = small2[:, 18:19]
    epsc = small2[:, 19:20]
    mPD = A.alloc([128, 4, 128], BF16)
    wchunk = [A.alloc([128, KC, 128], BF16) for _ in range(4)]
    y_aT = A.alloc([128, 4, S], BF16)
    TOP8 = A.cap - 8192
    TOP16 = A.cap - 8192 - 16384
    wbig = [A.at(TOP8, [128, KC, 512], BF16), None]
    TWO_PI = 2.0 * math.pi
    SHR = 1.0 - 2e-6
    PI_LO = 3.1415925
    INV2PI_HI = float(np.float32(1.0 / TWO_PI))
    INV2PI_LO = float(np.float32(1.0 / TWO_PI - INV2PI_HI))
    P.op("dve", lambda e: e.memset(negpi, -math.pi * SHR), writes=["negpi"])
    P.op("dve", lambda e: e.memset(epsc, EPS), writes=["epsc"])

    wc_i = [0]
    wb_i = [0]

    def load_chunk(c0):
        s = wc_i[0] % 4
        wc_i[0] += 1
        key = "wc%d" % s
        P.op("pool", lambda e: e.dma_start(out=wchunk[s], in_=win_v[:, :, c0:c0 + 128]),
             writes=[key], slot=key)
        return wchunk[s], key

    def load_big(c0):
        s = wb_i[0] % 2
        wb_i[0] += 1
        key = "wb%d" % s
        dst = wbig[s]
        P.op("pool", lambda e: e.dma_start(out=dst, in_=win_v[:, :, c0:c0 + 512]),
             writes=[key], slot=key)
        return dst, key

    def mm_group(out_ap, pairs, reads, writes):
        def fn(e):
            n = len(pairs)
            ins = None
            for i, (l, r) in enumerate(pairs):
                ins = e.matmul(out_ap, l, r, start=(i == 0), stop=(i == n - 1))
            return ins
        P.op("pe", fn, reads=reads, writes=writes)

    def zT_block(w, wkey, tb):
        b = P.bank()
        mm_group(banks[b], [(w[:, kc, :], hT[:, kc, tb * 512:(tb + 1) * 512]) for kc in range(KC)],
                 reads=[wkey], writes=[B(b)])
        return b

    try:
        P.op("pool", lambda e: e.dma_start(out=cst, in_=cst_d), writes=["cst"], slot="cst")
        P.op("pool", lambda e: e.dma_start(out=WsT, in_=wsp_d), writes=["WsT"], slot="wsp")
        for nm, dst, src in (("cT", cT, cT_d), ("adab", adab, adab_d), ("normg", normg, normg_d),
                             ("gqk", gqk, gqk_d), ("freq", small[:, 86:88], freq_d), ("bsp", bsp, bsp_d)):
            P.op("sp", (lambda d_, s_: (lambda e: e.dma_start(out=d_, in_=s_)))(dst, src),
                 writes=[nm], slot=nm)
        for j in range(4):
            P.op("dve", (lambda j_: (lambda e: e.tensor_copy(out=mDP[:, j_, :], in_=(mD if j_ % 2 == 0 else mP))))(j),
                 reads=["cst"], writes=["mDP%d" % j])
            P.op("dve", (lambda j_: (lambda e: e.tensor_copy(out=mDD[:, j_, :], in_=mD)))(j),
                 reads=["cst"], writes=["mDD%d" % j])
            P.op("dve", (lambda j_: (lambda e: e.tensor_copy(out=mPD[:, j_, :], in_=(mP if j_ % 2 == 0 else mD))))(j),
                 reads=["cst"], writes=["mPD%d" % j])
        for g in range(8):
            P.op("dve", (lambda g_: (lambda e: e.tensor_tensor(out=WsT[:, g_, :], in0=WsT[:, g_, :], in1=mD, op=ALU.mult)))(g),
                 reads=["cst", "WsT"], writes=["WsT"])

        ph0 = A.off
        xn_all = A.alloc([128, NT, D], BF16)
        xs = [A.alloc([128, D], F32) for _ in range(3)]
        scr = A.alloc([128, D], BF16)
        wada_s = [A.alloc([128, KC, 512], F32) for _ in range(2)]
        rowb = A.alloc([128, 3 * D], F32)
        wada_s.append(A.alloc([128, KC, 512], F32))
        one11 = small2[:, 20:21]
        P.op("dve", lambda e: e.memset(one11, 1.0), writes=["one11"])
        P.op("act", lambda e: e.activation(out=sc, in_=cT, func=AF.Silu), reads=["cT"], writes=["sc"])

        def x_load(t):
            s = t % 3
            P.op("pool", (lambda s_, t_: (lambda e: e.dma_start(out=xs[s_], in_=x_d[t_ * 128:(t_ + 1) * 128, :])))(s, t),
                 writes=["xs%d" % s], slot="xs%d" % s)
        for t in range(3):
            x_load(t)
        early_wv0 = []
        for t in range(NT):
            s = t % 3
            P.op("act", (lambda s_, t_: (lambda e: e.activation(out=scr, in_=xs[s_], func=AF.Square,
                                                                 accum_out=ssq[:, t_:t_ + 1])))(s, t),
                 reads=["xs%d" % s], writes=["scr", "ssq%d" % t])
            P.op("act", (lambda t_: (lambda e: e.activation(out=srt[:, t_:t_ + 1], in_=ssq[:, t_:t_ + 1], func=AF.Sqrt,
                                                             bias=epsc, scale=1.0 / D)))(t),
                 reads=["ssq%d" % t, "epsc"], writes=["srt%d" % t])
            P.op("dve", (lambda t_: (lambda e: e.reciprocal(out=rstd_x[:, t_:t_ + 1], in_=srt[:, t_:t_ + 1])))(t),
                 reads=["srt%d" % t], writes=["rstdx%d" % t])
            P.op("act", (lambda s_, t_: (lambda e: e.activation(out=xn_all[:, t_, :], in_=xs[s_], func=AF.Copy,
                                                                 scale=rstd_x[:, t_:t_ + 1])))(s, t),
                 reads=["xs%d" % s, "rstdx%d" % t], writes=["xn%d" % t])
            if t + 3 < NT:
                x_load(t + 3)
            elif not early_wv0:
                early_wv0.append(load_big(0 * 1536 + 1024))

        for j in range(6):
            s = j % 3
            key = "wada%d" % s
            P.op("sp", (lambda s_, j_: (lambda e: e.dma_start(out=wada_s[s_], in_=wada_v[:, :, j_ * 512:(j_ + 1) * 512])))(s, j),
                 writes=[key], slot=key)
            br_ = P.bank()
            mm_group(banks[br_][0:1, :], [(sc[:, kc:kc + 1], wada_s[s][:, kc, :]) for kc in range(KC)],
                     reads=[key, "sc"], writes=[B(br_)])
            P.op("dve", (lambda j_, b_: (lambda e: e.tensor_copy(out=rowb[0:1, j_ * 512:(j_ + 1) * 512], in_=banks[b_][0:1, :])))(j, br_),
                 reads=[B(br_)], writes=["rowb%d" % j])
        b_ada = P.bank()

        def adaT_fn(e):
            ins = None
            for col in range(24):
                ins = e.matmul(banks[b_ada][:, col:col + 1], rowb[0:1, col * 128:(col + 1) * 128], one11[0:1, 0:1],
                               start=True, stop=True)
            return ins
        P.op("pe", adaT_fn, reads=["rowb%d" % j for j in range(6)] + ["one11"], writes=[B(b_ada)])
        P.op("dve", lambda e: e.tensor_tensor(out=ada, in0=banks[b_ada][:, 0:24], in1=adab, op=ALU.add),
             reads=[B(b_ada), "adab"], writes=["ada"])
        P.op("dve", lambda e: e.scalar_tensor_tensor(out=Acol, in0=ada[:, 8:16], scalar=1.0, in1=normg,
                                                     op0=ALU.add, op1=ALU.mult),
             reads=["ada", "normg"], writes=["Acol"])

        for t in range(NT):
            b = P.bank()
            b2 = P.bank()
            psA = banks[b].rearrange("p (k t) -> p k t", k=4)
            psB = banks[b2].rearrange("p (k t) -> p k t", k=4)

            def tr_fn(e, t_=t, psA_=psA, psB_=psB):
                ins = None
                for kc in range(KC):
                    dst_ = (psA_ if kc < 4 else psB_)[:, kc % 4, :]
                    ins = e.matmul(dst_, xn_all[:, t_, kc * 128:(kc + 1) * 128], ident, start=True, stop=True)
                return ins
            P.op("pe", tr_fn, reads=["xn%d" % t, "cst"], writes=[B(b), B(b2)])
            for kc in range(KC):
                dst = hT[:, kc, t * 128:(t + 1) * 128]
                psT = psA if kc < 4 else psB
                bk = b if kc < 4 else b2
                if kc < 4:
                    P.op("dve", (lambda d_, p_, k_: (lambda e: e.tensor_scalar(out=d_, in0=p_, scalar1=Acol[:, k_:k_ + 1],
                                                                              scalar2=Bcol[:, k_:k_ + 1],
                                                                              op0=ALU.mult, op1=ALU.add)))(dst, psT[:, kc % 4, :], kc),
                         reads=[B(bk), "Acol", "ada"], writes=["hT%d_%d" % (t, kc)])
                else:
                    P.op("act", (lambda d_, p_, k_: (lambda e: e.activation(out=d_, in_=p_, func=AF.Identity,
                                                                           bias=Bcol[:, k_:k_ + 1],
                                                                           scale=Acol[:, k_:k_ + 1])))(dst, psT[:, kc % 4, :], kc),
                         reads=[B(bk), "Acol", "ada"], writes=["hT%d_%d" % (t, kc)])
        if debug:
            P.barrier()
            P.op("sp", lambda e: e.dma_start(out=dbg["d_hT"], in_=hT.rearrange("p k t -> p (k t)")), slot="dbg0")
            P.op("sp", lambda e: e.dma_start(out=dbg["d_ada"], in_=ada), slot="dbg3")
        P.barrier()
        A.off = ph0

        if stop_after == 1:
            raise _Stop()
        off_wbig = A.off
        wbig[1] = A.alloc([128, KC, 512], BF16)
        V = [A.alloc([128, 16, 512], BF16) for _ in range(3)]
        off_qk = A.off
        qk = [[A.alloc([128, S], BF16) for _ in range(2)] for _ in range(3)]
        acc_n = A.alloc([128, S], F32)
        acc_d = A.alloc([128, S], F32)
        pT = [[A.alloc([128, 4, 128], BF16) for _ in range(2)] for _ in range(2)]
        bones_g = A.alloc([128, 6, 128], BF16)
        ginv = A.alloc([128, 8], F32)

        def qk_slice(g, blk):
            d = DIL[g]
            nb = 16 // d
            r, n = blk // nb, blk % nb
            st = r * (S // d) + 128 * n
            return slice(st, st + 128)

        def tok_slice(g, blk):
            d = DIL[g]
            nb = 16 // d
            r, n = blk // nb, blk % nb
            st = 128 * n * d + r
            return slice(st, st + 127 * d + 1, d)

        P.op("dve", lambda e: e.reciprocal(out=ginv[:, 0:6], in_=gqk), reads=["gqk"], writes=["ginv"])
        P.op("dve", lambda e: e.tensor_tensor(out=ginv[:, 0:6], in0=ginv[:, 0:6], in1=ginv[:, 0:6], op=ALU.mult),
             reads=["ginv"], writes=["ginv"])
        for j in range(6):
            P.op("dve", (lambda j_: (lambda e: e.tensor_scalar(out=bones_g[:, j_, :], in0=bones, scalar1=ginv[:, j_:j_ + 1],
                                                               scalar2=None, op0=ALU.mult)))(j),
                 reads=["ginv", "cst"], writes=["bones_g%d" % j])

        seq = [(hp, g) for hp in range(4) for g in range(3)]
        worder = []
        for idx, (hp, g) in enumerate(seq):
            if idx == 0:
                worder += [(hp, g, 0), (hp, g, 1)]
            if idx + 1 < len(seq):
                nh, ng = seq[idx + 1]
                worder += [(nh, ng, 0), (nh, ng, 1)]
            if g == 2:
                worder += [(hp, -1, 0)]
        wloaded = {}
        wnext = [0]

        def wcol(item):
            hp_, g_, role_ = item
            if g_ < 0:
                return OFF_GATE_A + hp_ * 128
            return g_ * 1536 + role_ * 512 + hp_ * 128

        def get_w(item):
            k = worder.index(item)
            while wnext[0] <= min(k + 2, len(worder) - 1):
                wloaded[worder[wnext[0]]] = load_chunk(wcol(worder[wnext[0]]))
                wnext[0] += 1
            return wloaded[item]

        get_w(worder[0])
        posi = A.at(off_qk, [128, S], I32)
        ang = A.at(off_qk + 8192, [128, S], F32)
        yy = A.at(off_qk + 16384, [128, S], F32)
        ki = A.at(off_qk + 24576, [128, S], I32)
        kf = A.at(off_qk + 32768, [128, S], F32)
        P.op("sp", lambda e: e.dma_start(out=posi, in_=pos_d.partition_broadcast(128)),
             writes=["posi"], slot="posi")
        tab_ops = []
        tab_ops.append(lambda: P.op("dve", lambda e: e.tensor_copy(out=ang, in_=posi), reads=["posi"], writes=["ang"]))
        tab_ops.append(lambda: P.op("dve", lambda e: e.tensor_scalar(out=kf, in0=ang, scalar1=freq_lo, scalar2=None, op0=ALU.mult),
                                    reads=["ang", "freq"], writes=["kf"]))
        tab_ops.append(lambda: P.op("dve", lambda e: e.scalar_tensor_tensor(out=ang, in0=ang, scalar=freq, in1=kf,
                                                                            op0=ALU.mult, op1=ALU.add),
                                    reads=["ang", "kf", "freq"], writes=["ang"]))
        for tab, offs in ((Stab, 0.5), (Ctab, 0.75)):
            tab_ops.append((lambda o_: (lambda: P.op("dve", lambda e: e.tensor_scalar(out=yy, in0=ang, scalar1=INV2PI_HI, scalar2=o_,
                                                                                      op0=ALU.mult, op1=ALU.add),
                                                     reads=["ang"], writes=["yy"])))(offs))
            tab_ops.append(lambda: P.op("dve", lambda e: e.scalar_tensor_tensor(out=yy, in0=ang, scalar=INV2PI_LO, in1=yy,
                                                                                op0=ALU.mult, op1=ALU.add),
                                        reads=["ang", "yy"], writes=["yy"]))
            tab_ops.append(lambda: P.op("dve", lambda e: e.tensor_copy(out=ki, in_=yy), reads=["yy"], writes=["ki"]))
            tab_ops.append(lambda: P.op("dve", lambda e: e.tensor_copy(out=kf, in_=ki), reads=["ki"], writes=["kf"]))
            tab_ops.append(lambda: P.op("dve", lambda e: e.tensor_tensor(out=yy, in0=yy, in1=kf, op=ALU.subtract),
                                        reads=["yy", "kf"], writes=["yy"]))
            tab_ops.append(lambda: P.op("dve", lambda e: e.tensor_single_scalar(out=kf, in_=yy, scalar=0.0, op=ALU.is_lt),
                                        reads=["yy"], writes=["kf"]))
            tab_ops.append(lambda: P.op("dve", lambda e: e.tensor_tensor(out=yy, in0=yy, in1=kf, op=ALU.add),
                                        reads=["yy", "kf"], writes=["yy"]))
            tab_ops.append(lambda: P.op("dve", lambda e: e.tensor_scalar(out=yy, in0=yy, scalar1=TWO_PI, scalar2=-math.pi,
                                                                          op0=ALU.mult, op1=ALU.add),
                                        reads=["yy"], writes=["yy"]))
            tab_ops.append(lambda: P.op("dve", lambda e: e.tensor_scalar(out=yy, in0=yy, scalar1=-PI_LO, scalar2=PI_LO,
                                                                          op0=ALU.max, op1=ALU.min),
                                        reads=["yy"], writes=["yy"]))
            tab_ops.append((lambda t_: (lambda: P.op("act", lambda e: e.activation(out=t_, in_=yy, func=AF.Sin),
                                                     reads=["yy"], writes=["tab"])))(tab))

        for g in range(3):
            if g == 0:
                w, wkey = early_wv0[0]
            else:
                w, wkey = load_big(g * 1536 + 1024)
            for blk in range(16):
                sl = tok_slice(g, blk)
                b = P.bank()
                mm_group(banks[b], [(hT[:, kc, sl], w[:, kc, :]) for kc in range(KC)],
                         reads=[wkey], writes=[B(b)])
                if blk % 2 == 0:
                    P.op("act", (lambda g_, k_, b_: (lambda e: e.copy(out=V[g_][:, k_, :], in_=banks[b_])))(g, blk, b),
                         reads=[B(b)], writes=["V%d_%d" % (g, blk)])
                else:
                    P.op("dve", (lambda g_, k_, b_: (lambda e: e.tensor_copy(out=V[g_][:, k_, :], in_=banks[b_])))(g, blk, b),
                         reads=[B(b)], writes=["V%d_%d" % (g, blk)])
                if tab_ops:
                    tab_ops.pop(0)()
        while tab_ops:
            tab_ops.pop(0)()
        P.barrier()
        NBUF = 3
        save_off = A.off
        A.off = off_wbig
        def talloc(nbytes, dtype):
            if A.off < save_off and A.off + nbytes > off_wbig + 8192:
                A.off = save_off
            return A.alloc([128, 512], dtype)
        tmps = []
        for i in range(NBUF):
            tset = dict(zg=talloc(1024, BF16), sq=talloc(1024, BF16), zc=talloc(1024, BF16), zs=talloc(1024, BF16),
                        ln=talloc(2048, F32))
            tset["rs"] = tset["ln"]
            tmps.append(tset)
        ftmp = dict(t1=talloc(2048, F32), t2=talloc(2048, F32))
        if A.off < save_off:
            A.off = save_off
        assert A.off <= TOP8, ("phase 2 overlaps top slot", A.off, TOP8)
        wv = wbig[0]
        P.op("pool", lambda e: e.dma_start(out=wv, in_=win_v[:, :, OFF_Z_B + 512:OFF_Z_B + 1024]), writes=["wv"], slot="wb0")
        P.nb = 6
        P.bank_i = 0
        BO, BD = 6, 7

        cb_i = [0]

        def chunk_block(hp, g, role, tb):
            st = {}

            def stageA():
                w, wkey = get_w((hp, g, role))
                i = cb_i[0] % NBUF
                cb_i[0] += 1
                T = tmps[i]
                sf = "_%d" % i
                gcol = gqk[:, g * 2 + role:g * 2 + role + 1]
                bz = zT_block(w, wkey, tb)
                P.op("act", lambda e: e.activation(out=T["zg"], in_=banks[bz], func=AF.Copy, scale=gcol),
                     reads=[B(bz), "gqk"], writes=["zg" + sf])
                P.op("dve", lambda e: e.tensor_tensor(out=T["sq"], in0=T["zg"], in1=T["zg"], op=ALU.mult),
                     reads=["zg" + sf], writes=["sq" + sf])
                ts_ = slice(tb * 512, (tb + 1) * 512)
                P.op("dve", lambda e: e.tensor_tensor(out=T["zc"], in0=T["zg"], in1=Ctab[:, ts_], op=ALU.mult),
                     reads=["zg" + sf], writes=["zc" + sf])
                P.op("dve", lambda e: e.tensor_tensor(out=T["zs"], in0=T["zg"], in1=Stab[:, ts_], op=ALU.mult),
                     reads=["zg" + sf], writes=["zs" + sf])
                st["T"], st["sf"] = T, sf

            def stageB():
                T, sf = st["T"], st["sf"]
                dstT = qk[g][role]
                ts_ = slice(tb * 512, (tb + 1) * 512)
                bs = P.bank()
                br = P.bank()

                def sr_fn(e):
                    e.matmul(banks[bs], bones_g[:, g * 2 + role, :], T["sq"], start=True, stop=True)
                    e.matmul(banks[br], ident, T["zc"], start=True, stop=False)
                    return e.matmul(banks[br], perm, T["zs"], start=False, stop=True)
                P.op("pe", sr_fn, reads=["sq" + sf, "zc" + sf, "zs" + sf, "bones_g%d" % (g * 2 + role), "cst"], writes=[B(bs), B(br)])
                P.op("act", lambda e: e.activation(out=T["ln"], in_=banks[bs], func=AF.Ln, bias=epsc, scale=1.0),
                     reads=[B(bs), "epsc"], writes=["ln" + sf])
                P.op("act", lambda e: e.activation(out=T["rs"], in_=T["ln"], func=AF.Exp, scale=-0.5),
                     reads=["ln" + sf], writes=["ln" + sf])
                dg = DIL[g]
                if dg == 1:
                    o_ap, i0_ap, i1_ap = dstT[:, ts_], banks[br], T["rs"]
                else:
                    mloc = 512 // dg
                    o_ap = dstT.rearrange("p (r m) -> p m r", r=dg)[:, tb * mloc:(tb + 1) * mloc, :]
                    i0_ap = banks[br].rearrange("p (m r) -> p m r", r=dg)
                    i1_ap = T["rs"].rearrange("p (m r) -> p m r", r=dg)
                P.op("dve", lambda e: e.tensor_tensor(out=o_ap, in0=i0_ap, in1=i1_ap, op=ALU.mult),
                     reads=[B(br), "ln" + sf], writes=["qk%d%d_%d" % (g, role, tb)])
            return stageA, stageB

        def chunk_steps(hp, g):
            return [chunk_block(hp, g, ro, tb) for tb in range(4) for ro in range(2)]

        unit_ctr = [0]

        def att_units(hp, g):
            d = DIL[g]
            nb = 16 // d
            qT, kT = qk[g][0], qk[g][1]
            qkeys = ["qk%d%d_%d" % (g, ro, tb) for ro in range(2) for tb in range(4)]
            if g == 2:
                qgroups = [[r for r in range(4 * i, 4 * i + 4)] for i in range(4)]
            else:
                qgroups = [[r * nb + n for n in range(4 * i, 4 * i + 4)] for r in range(d) for i in range(nb // 4)]
            out = []
            for qg in qgroups:
                subs = []
                for slot_, blk in enumerate(qg):
                    n = blk % nb
                    if g != 2 and n > 0:
                        subs.append((blk - 1, blk, slot_, True, False, 1))
                        subs.append((blk, blk, slot_, False, True, 0))
                    else:
                        subs.append((blk, blk, slot_, True, True, 0))
                units = [subs[i:i + 4] for i in range(0, len(subs), 4)]
                for ui, un in enumerate(units):
                    par = unit_ctr[0] % 2
                    unit_ctr[0] += 1
                    last_of_group = (ui == len(units) - 1)

                    def S_stage(un=un, par=par):
                        types = [u_[5] for u_ in un]
                        nj = len(un)
                        bS = [P.bank(), P.bank()]
                        pss = [banks[bS[e_]].rearrange("p (j q) -> p j q", j=4) for e_ in range(2)]

                        def s_fn(e):
                            ins = None
                            for j, (kb, qb, _s, _f, _l, _t) in enumerate(un):
                                for e_ in range(2):
                                    rows = slice(64 * e_, 64 * e_ + 64)
                                    ins = e.matmul(pss[e_][:, j, :], kT[rows, qk_slice(g, kb)], qT[rows, qk_slice(g, qb)],
                                                   start=True, stop=True)
                            return ins
                        if g == 2:
                            rk = qkeys
                        else:
                            tbs_k = set((kb // 4) if g == 0 else (kb % nb) for (kb, qb, _s, _f, _l, _t) in un)
                            tbs_q = set((qb // 4) if g == 0 else (qb % nb) for (kb, qb, _s, _f, _l, _t) in un)
                            rk = ["qk%d1_%d" % (g, t_) for t_ in tbs_k] + ["qk%d0_%d" % (g, t_) for t_ in tbs_q]
                        P.op("pe", s_fn, reads=rk, writes=[B(bS[0]), B(bS[1])])
                        if all(t_ == 0 for t_ in types):
                            mk = mDD
                        elif all(types[j] == (j % 2) for j in range(nj)):
                            mk = mDP
                        elif all(types[j] == ((j + 1) % 2) for j in range(nj)):
                            mk = mPD
                        else:
                            raise AssertionError("mask pattern")
                        for e_ in range(2):
                            pt = pT[par][e_]
                            pkey = "pT%d%d" % (par, e_)
                            P.op("act", (lambda ps=pss[e_], pt=pt: (lambda e: e.activation(out=pt[:, 0:nj, :], in_=ps[:, 0:nj, :],
                                                                                           func=AF.Exp, scale=0.125)))(),
                                 reads=[B(bS[e_])], writes=[pkey])
                            P.op("dve",
                                 (lambda pt=pt, mk=mk: (lambda e: e.tensor_tensor(out=pt[:, 0:nj, :], in0=pt[:, 0:nj, :],
                                                                                  in1=mk[:, 0:nj, :], op=ALU.mult)))(),
                                 reads=[pkey], writes=[pkey])

                    def PV_stage(un=un, par=par):
                        po = banks[BO].rearrange("p (j q) -> p j q", j=4)
                        pd = banks[BD].rearrange("p (j q) -> p j q", j=4)

                        def pv_fn(e):
                            ins = None
                            for j, (kb, qb, sl_, f_, l_, _t) in enumerate(un):
                                for e_ in range(2):
                                    orow = slice(64 * e_, 64 * e_ + 64)
                                    hcol = slice((hp * 2 + e_) * 64, (hp * 2 + e_) * 64 + 64)
                                    e.matmul(po[orow, sl_, :], V[g][:, kb, hcol], pT[par][e_][:, j, :], start=f_, stop=l_)
                                for e_ in range(2):
                                    orow = slice(64 * e_, 64 * e_ + 64)
                                    ins = e.matmul(pd[orow, sl_, :], ones64, pT[par][e_][:, j, :], start=f_, stop=l_)
                            return ins
                        P.op("pe", pv_fn, reads=["pT%d0" % par, "pT%d1" % par, "cst"], writes=[B(BO), B(BD)])

                    def post(qg=qg):
                        if g == 0:
                            t0 = (qg[0] % nb) * 128
                            dn = acc_n[:, t0:t0 + 512]
                            dd = acc_d[:, t0:t0 + 512]
                            P.op("act", lambda e: e.copy(out=dn, in_=banks[BO]), reads=[B(BO)], writes=["accn"])
                            P.op("dve", lambda e: e.tensor_copy(out=dd, in_=banks[BD]), reads=[B(BD)], writes=["accd"])
                        else:
                            if g == 1:
                                r = qg[0] // nb
                                dn = acc_n[:, r:S:4]
                                dd = acc_d[:, r:S:4]
                                sn = banks[BO]
                                sd = banks[BD]
                            else:
                                r0 = qg[0]
                                dn = acc_n.rearrange("p (i r) -> p r i", r=16)[:, r0:r0 + 4, :]
                                dd = acc_d.rearrange("p (i r) -> p r i", r=16)[:, r0:r0 + 4, :]
                                sn = banks[BO].rearrange("p (j q) -> p j q", j=4)
                                sd = banks[BD].rearrange("p (j q) -> p j q", j=4)
                            P.op("dve", lambda e: e.tensor_tensor(out=dn, in0=sn, in1=dn, op=ALU.add),
                                 reads=[B(BO), "accn"], writes=["accn"])
                            P.op("dve", lambda e: e.tensor_tensor(out=dd, in0=sd, in1=dd, op=ALU.add),
                                 reads=[B(BD), "accd"], writes=["accd"])
                    out.append((S_stage, PV_stage, post if last_of_group else None))
            return out

        def finalize(hp):
            w, wkey = get_w((hp, -1, 0))
            for tb in range(4):
                i = cb_i[0] % NBUF
                cb_i[0] += 1
                T = dict(tmps[i])
                T.update(ftmp)
                sf = "_%d" % i
                ts_ = slice(tb * 512, (tb + 1) * 512)
                bz = zT_block(w, wkey, tb)
                P.op("act", (lambda T=T, bz=bz: (lambda e: e.activation(out=T["t2"], in_=banks[bz], func=AF.Exp, scale=-1.0)))(),
                     reads=[B(bz)], writes=["ft2"])
                P.op("dve", (lambda T=T, ts_=ts_: (lambda e: e.scalar_tensor_tensor(out=T["t1"], in0=T["t2"], scalar=1.0, in1=acc_d[:, ts_],
                                                                                    op0=ALU.add, op1=ALU.mult)))(),
                     reads=["ft2", "accd"], writes=["ft1"])
                P.op("act", (lambda T=T: (lambda e: e.activation(out=T["ln"], in_=T["t1"], func=AF.Ln)))(),
                     reads=["ft1"], writes=["ln" + sf])
                P.op("act", (lambda T=T: (lambda e: e.activation(out=T["rs"], in_=T["ln"], func=AF.Exp, scale=-1.0)))(),
                     reads=["ln" + sf], writes=["ln" + sf])
                P.op("dve", (lambda T=T, ts_=ts_: (lambda e: e.tensor_tensor(out=T["t1"], in0=acc_n[:, ts_], in1=T["rs"], op=ALU.mult)))(),
                     reads=["accn", "ln" + sf, "ft1"], writes=["ft1"])
                P.op("dve", (lambda T=T, ts_=ts_, bz=bz: (lambda e: e.tensor_tensor(out=y_aT[:, hp, ts_], in0=banks[bz], in1=T["t1"], op=ALU.mult)))(),
                     reads=[B(bz), "ft1"], writes=["ya%d_%d" % (hp, tb)])

        pendB = [None]

        def emit_chunk(ab):
            ab[0]()
            if pendB[0] is not None:
                pendB[0]()
            pendB[0] = ab[1]

        def flushB():
            if pendB[0] is not None:
                pendB[0]()
                pendB[0] = None

        for ab in chunk_steps(*seq[0]):
            emit_chunk(ab)
        flushB()
        for idx, (hp, g) in enumerate(seq):
            units = att_units(hp, g)
            nxt = chunk_steps(*seq[idx + 1]) if idx + 1 < len(seq) else []
            ci = 0
            units[0][0]()
            for u in range(len(units)):
                if u + 1 < len(units):
                    units[u + 1][0]()
                k = -(-(u + 1) * len(nxt) // len(units)) - ci
                for _ in range(k):
                    emit_chunk(nxt[ci])
                    ci += 1
                units[u][1]()
                if units[u][2] is not None:
                    units[u][2]()
            flushB()
            if g == 2:
                finalize(hp)
            if debug and hp == 0 and g == 2:
                P.barrier()
                for g2 in range(3):
                    P.op("sp", (lambda g_: (lambda e: e.dma_start(out=dbg["d_V"][:, g_ * 8192:(g_ + 1) * 8192],
                                                                  in_=V[g_].rearrange("p a b -> p (a b)"))))(g2), slot="dbgV%d" % g2)
                P.barrier()

        P.barrier()
        P.nb = 8
        P.bank_i = 0
        A.off = ph0

        if stop_after == 2:
            raise _Stop()
        y_bT = A.alloc([128, 4, S], BF16)
        wug = A.alloc([128, 8, KC, 128], BF16)
        sv_all = A.alloc([128, 4, S], F32)
        lng = A.alloc([128, 512], F32)
        lnb = A.alloc([128, 512], F32)
        vg1 = A.alloc([128, 4, 512], F32)
        vg = [vg1, vg1]
        vn_t = [A.alloc([128, 512], F32) for _ in range(2)]
        vl_t = [A.alloc([128, 512], BF16) for _ in range(4)]
        gu_t = [A.alloc([128, 512], F32) for _ in range(2)]
        gs_t = [A.alloc([128, 512], F32) for _ in range(2)]
        st3 = A.alloc([128, 64], F32)
        bst = [st3[:, 0:24].rearrange("p (j k) -> p j k", j=4), st3[:, 24:48].rearrange("p (j k) -> p j k", j=4)]
        mv = [st3[:, 48:56].rearrange("p (j k) -> p j k", j=4), st3[:, 56:64].rearrange("p (j k) -> p j k", j=4)]
        st3b = A.alloc([128, 16], F32)
        rsd = [st3b[:, 0:4], st3b[:, 4:8]]
        assert A.off <= TOP16, ("phase 3 overlaps top16", A.off, TOP16)
        for j in (0, 4, 1, 5, 2, 6, 3, 7):
            c0 = OFF_Z_B + j * 128 if j < 4 else OFF_GATE_B + (j - 4) * 128
            P.op("pool", (lambda j_, c_: (lambda e: e.dma_start(out=wug[:, j_], in_=win_v[:, :, c_:c_ + 128])))(j, c0),
                 writes=["wug%d" % j], slot="wug%d" % j)
        wba = A.at(TOP16, [128, 4, D], BF16)
        wbb = A.at(TOP16 + 8192, [128, 4, D], BF16)
        P.op("pool", lambda e: e.dma_start(out=wba, in_=wba_d.rearrange("(kc p) c -> p kc c", p=128)), writes=["wba"], slot="wba")
        P.op("pool", lambda e: e.dma_start(out=wbb, in_=wbb_d.rearrange("(kc p) c -> p kc c", p=128)), writes=["wbb"], slot="wbb")
        P.op("sp", lambda e: e.dma_start(out=lng, in_=lng_d.partition_broadcast(128)), writes=["lng"], slot="lng")
        P.op("sp", lambda e: e.dma_start(out=lnb, in_=lnb_d.partition_broadcast(128)), writes=["lnb"], slot="lnb")

        gate_bc = A.at(TOP8, [128, D], F32)
        gate_bf = A.at(TOP8 + 4096, [128, KC, 128], BF16)
        onesT = A.at(TOP8 + 6144, [128, 128], BF16)
        gres = A.at(TOP8 + 6400, [128, KC], F32)
        ghi = A.at(TOP8 + 6464, [128, KC], BF16)

        def gate_steps():
            bgt = [P.bank(), P.bank()]
            steps = []

            def prep0():
                P.op("dve", lambda e: e.memset(onesT, 1.0), writes=["onesT", "wv"])
                P.op("dve", lambda e: e.tensor_copy(out=ghi, in_=gatecol), writes=["ghi"])
                P.op("dve", lambda e: e.tensor_tensor(out=gres, in0=gatecol, in1=ghi, op=ALU.subtract), reads=["ghi"], writes=["gres"])
            steps.append(prep0)
            for half, (src, key) in enumerate(((ghi, "ghi"), (gres, "gres"))):
                def prep(src=src, key=key):
                    for kc in range(KC):
                        P.op("dve", (lambda k_: (lambda e: e.tensor_scalar(out=gate_bf[:, k_, :], in0=ident, scalar1=src[:, k_:k_ + 1],
                                                                           scalar2=None, op0=ALU.mult)))(kc),
                             reads=[key], writes=["gbf%d" % kc])
                steps.append(prep)

                def pe_ev(half=half):
                    for hf in range(2):
                        bb_ = bgt[hf]

                        def g_fn(e, hf_=hf, bb__=bb_):
                            ins = None
                            for j in range(4):
                                kc = hf_ * 4 + j
                                ins = e.matmul(banks[bb__][:, j * 128:(j + 1) * 128], onesT, gate_bf[:, kc, :], start=True, stop=True)
                            return ins
                        P.op("pe", g_fn, reads=["onesT"] + ["gbf%d" % k for k in range(KC)], writes=[B(bb_)])
                        dst = gate_bc[:, hf * 512:(hf + 1) * 512]
                        if half == 0:
                            P.op("dve", (lambda d_, b_: (lambda e: e.tensor_copy(out=d_, in_=banks[b_])))(dst, bb_),
                                 reads=[B(bb_)], writes=["gbc%d" % hf])
                        else:
                            P.op("dve", (lambda d_, b_: (lambda e: e.tensor_tensor(out=d_, in0=banks[b_], in1=d_, op=ALU.add)))(dst, bb_),
                                 reads=[B(bb_), "gbc%d" % hf], writes=["gbc%d" % hf])
                steps.append(pe_ev)
            return steps

        mw = {0: (load_chunk(OFF_MERGE_A), load_chunk(OFF_MERGE_B)),
              1: (load_chunk(OFF_MERGE_A + 128), load_chunk(OFF_MERGE_B + 128))}
        def V1(t):
            tbp = (t // 4) % 2
            j = t % 4
            b = P.bank()
            mm_group(banks[b], [(hT[:, kc, t * 128:(t + 1) * 128], wv[:, kc, :]) for kc in range(KC)],
                     reads=["wv"], writes=[B(b)])
            P.op("act", lambda e: e.activation(out=vg[tbp][:, j, :], in_=banks[b], func=AF.Gelu),
                 reads=[B(b)], writes=["vg_%d" % j])
            P.op("dve", lambda e: e.bn_stats(out=bst[tbp][:, j, :], in_=vg[tbp][:, j, :]),
                 reads=["vg_%d" % j], writes=["bst%d_%d" % (tbp, j)])
            P.op("dve", lambda e: e.bn_aggr(out=mv[tbp][:, j, :], in_=bst[tbp][:, j, :]),
                 reads=["bst%d_%d" % (tbp, j)], writes=["mv%d_%d" % (tbp, j)])

        def V2(tb):
            tbp = tb % 2
            P.op("act", lambda e: e.activation(out=rsd[tbp], in_=mv[tbp][:, :, 1], func=AF.Sqrt, bias=epsc, scale=1.0),
                 reads=["mv%d_%d" % (tbp, j) for j in range(4)] + ["epsc"], writes=["rsd%d" % tbp])
            P.op("dve", lambda e: e.reciprocal(out=rsd[tbp], in_=rsd[tbp]), reads=["rsd%d" % tbp], writes=["rsd%d" % tbp])

        def V3a(t):
            tbp = (t // 4) % 2
            j = t % 4
            s = t % 2
            P.op("dve", lambda e: e.tensor_scalar(out=vn_t[s], in0=vg[tbp][:, j, :], scalar1=mv[tbp][:, j, 0:1],
                                                  scalar2=rsd[tbp][:, j:j + 1], op0=ALU.subtract, op1=ALU.mult),
                 reads=["vg_%d" % j, "mv%d_%d" % (tbp, j), "rsd%d" % tbp], writes=["vn%d" % s])
            P.op("dve", lambda e: e.tensor_tensor(out=vn_t[s], in0=vn_t[s], in1=lng, op=ALU.mult),
                 reads=["vn%d" % s, "lng"], writes=["vn%d" % s])
            P.op("pool", lambda e: e.tensor_tensor(out=vl_t[j], in0=vn_t[s], in1=lnb, op=ALU.add),
                 reads=["vn%d" % s, "lnb"], writes=["vl%d" % j])

        def V3b(t):
            j = t % 4
            bsv = P.bank()
            psv = banks[bsv].rearrange("p (c q) -> p c q", c=4)

            def sv_fn(e):
                ins = None
                for gg in range(8):
                    ins = e.matmul(psv[64 * (gg % 2):64 * (gg % 2) + 64, gg // 2, :], vl_t[j][:, gg * 64:(gg + 1) * 64],
                                   WsT[:, gg, :], start=True, stop=True)
                return ins
            P.op("pe", sv_fn, reads=["vl%d" % j, "WsT"], writes=[B(bsv)])
            P.op("dve", lambda e: e.tensor_tensor(out=sv_all[:, :, t * 128:(t + 1) * 128], in0=psv, in1=bsp, op=ALU.add),
                 reads=[B(bsv), "bsp"], writes=["sv%d" % t])

        ug_i = [0]

        def UG(c, tb):
            i = ug_i[0] % 2
            ug_i[0] += 1
            ts_ = slice(tb * 512, (tb + 1) * 512)
            bu = P.bank()
            mm_group(banks[bu], [(wug[:, c, kc, :], hT[:, kc, ts_]) for kc in range(KC)], reads=["wug%d" % c], writes=[B(bu)])
            bg = P.bank()
            mm_group(banks[bg], [(wug[:, 4 + c, kc, :], hT[:, kc, ts_]) for kc in range(KC)], reads=["wug%d" % (4 + c)], writes=[B(bg)])
            P.op("act", lambda e: e.activation(out=gu_t[i], in_=banks[bu], func=AF.Gelu), reads=[B(bu)], writes=["gu%d" % i])

            def a2():
                P.op("act", lambda e: e.activation(out=gs_t[i], in_=banks[bg], func=AF.Silu), reads=[B(bg)], writes=["gs%d" % i])
                P.op("dve", lambda e: e.tensor_tensor(out=gu_t[i], in0=gu_t[i], in1=gs_t[i], op=ALU.mult),
                     reads=["gu%d" % i, "gs%d" % i], writes=["gu%d" % i])

            def b_():
                P.op("pool", lambda e: e.tensor_tensor(out=y_bT[:, c, ts_], in0=gu_t[i], in1=sv_all[:, c, ts_], op=ALU.mult),
                     reads=["gu%d" % i] + ["sv%d" % t for t in range(4 * tb, 4 * tb + 4)], writes=["yb%d_%d" % (c, tb)])
            return a2, b_

        for t in range(4):
            V1(t)
        V2(0)
        for t in range(4):
            V3a(t)
        for tb in range(4):
            pend = []
            if tb == 3:
                gsteps = gate_steps()
                gsteps[0]()
                gsteps[1]()
            for c in range(4):
                a2, b_ = UG(c, tb)
                if tb + 1 < 4:
                    V1(4 * (tb + 1) + c)
                a2()
                if c < 2:
                    pend.append(b_)
                    if c == 1:
                        for t in range(4 * tb, 4 * tb + 4):
                            V3b(t)
                        for f_ in pend:
                            f_()
                else:
                    b_()
                if tb == 3 and c == 1:
                    gsteps[2]()
                if tb == 3 and c == 2:
                    gsteps[3]()
                if tb == 3 and c == 3:
                    gsteps[4]()
            if tb + 1 < 4:
                V2(tb + 1)
                for t in range(4 * (tb + 1), 4 * (tb + 1) + 4):
                    V3a(t)
        if debug:
            P.barrier()
            P.op("sp", lambda e: e.dma_start(out=dbg["d_ya"], in_=y_aT.rearrange("p k t -> p (k t)")), slot="dbg4")
            P.op("sp", lambda e: e.dma_start(out=dbg["d_yb"], in_=y_bT.rearrange("p k t -> p (k t)")), slot="dbg5")
        P.barrier()
        A.off = ph0

        if stop_after == 3:
            raise _Stop()
        y_bT = A.alloc([128, 4, S], BF16)
        wout = A.alloc([128, KC, D], BF16)
        mg = A.alloc([128, KC, S], BF16)
        xs2 = [A.alloc([128, D], F32) for _ in range(2)]
        ot = [A.alloc([128, D], F32) for _ in range(2)]
        sga = A.alloc([128, 512], F32)
        sgb = A.alloc([128, 512], F32)
        m1 = A.alloc([128, 512], F32)
        m2 = A.alloc([128, 512], F32)
        for h in range(2):
            P.op("pool", (lambda h_: (lambda e: e.dma_start(out=wout[:, :, h_ * 512:(h_ + 1) * 512],
                                                            in_=wout_d.rearrange("(kc p) c -> p kc c", p=128)[:, :, h_ * 512:(h_ + 1) * 512])))(h),
                 writes=["wout%d" % h], slot="wout%d" % h)
        mtmp = [dict(sga=sga, sgb=sgb, m1=m1, m2=m2),
                dict(sga=A.alloc([128, 512], F32), sgb=A.alloc([128, 512], F32),
                     m1=A.alloc([128, 512], F32), m2=A.alloc([128, 512], F32))]
        assert A.off <= TOP16, ("phase 4 overlaps top16", A.off, TOP16)
        it4 = 0
        for fc in range(KC):
            if fc + 1 < KC and (fc + 1) not in mw:
                mw[fc + 1] = (load_chunk(OFF_MERGE_A + (fc + 1) * 128), load_chunk(OFF_MERGE_B + (fc + 1) * 128))
            (wma, wmakey), (wmb, wmbkey) = mw[fc]
            for tb in range(4):
                ts_ = slice(tb * 512, (tb + 1) * 512)
                M = mtmp[it4 % 2]
                sf = "_%d" % (it4 % 2)
                it4 += 1
                bga = zT_block(wma, wmakey, tb)
                bgb = zT_block(wmb, wmbkey, tb)
                bpa = P.bank()
                mm_group(banks[bpa], [(wba[:, kc, fc * 128:(fc + 1) * 128], y_aT[:, kc, ts_]) for kc in range(4)],
                         reads=["wba"], writes=[B(bpa)])
                bpb = P.bank()
                mm_group(banks[bpb], [(wbb[:, kc, fc * 128:(fc + 1) * 128], y_bT[:, kc, ts_]) for kc in range(4)],
                         reads=["wbb"], writes=[B(bpb)])
                P.op("act", (lambda b_, M=M: (lambda e: e.activation(out=M["sga"], in_=banks[b_], func=AF.Sigmoid)))(bga),
                     reads=[B(bga)], writes=["sga" + sf])
                P.op("act", (lambda b_, M=M: (lambda e: e.activation(out=M["sgb"], in_=banks[b_], func=AF.Sigmoid)))(bgb),
                     reads=[B(bgb)], writes=["sgb" + sf])
                P.op("dve", (lambda b_, M=M: (lambda e: e.tensor_tensor(out=M["m1"], in0=banks[b_], in1=M["sga"], op=ALU.mult)))(bpa),
                     reads=[B(bpa), "sga" + sf], writes=["m1" + sf])
                P.op("dve", (lambda b_, M=M: (lambda e: e.tensor_tensor(out=M["m2"], in0=banks[b_], in1=M["sgb"], op=ALU.mult)))(bpb),
                     reads=[B(bpb), "sgb" + sf], writes=["m2" + sf])
                P.op("pool", (lambda f_, s_, M=M: (lambda e: e.tensor_tensor(out=mg[:, f_, s_], in0=M["m1"], in1=M["m2"], op=ALU.add)))(fc, ts_),
                     reads=["m1" + sf, "m2" + sf], writes=["mg%d_%d" % (fc, tb)])
        if debug:
            P.barrier()
            P.op("sp", lambda e: e.dma_start(out=dbg["d_mg"], in_=mg.rearrange("p k t -> p (k t)")), slot="dbg6")
            P.barrier()
        xs2 = xs2 + [A.at(TOP16, [128, D], F32), A.at(TOP16 + 8192, [128, D], F32)]
        ot = ot + [A.at(TOP16 + 4096, [128, D], F32), A.at(TOP16 + 12288, [128, D], F32)]
        alias = {2: "wba", 3: "wbb"}
        first_x = set()
        first_o = set()

        def x2_load(t):
            s = t % 4
            wr = ["xs2_%d" % s]
            if s in alias and s not in first_x:
                first_x.add(s)
                wr.append(alias[s])
            P.op("sp", (lambda s_, t_: (lambda e: e.dma_start(out=xs2[s_], in_=x_d[t_ * 128:(t_ + 1) * 128, :])))(s, t),
                 writes=wr, slot="xs2_%d" % s)
        for t in range(4):
            x2_load(t)
        for t in range(NT):
            s = t % 4
            for hf in range(2):
                b = P.bank()
                mm_group(banks[b], [(mg[:, fc, t * 128:(t + 1) * 128], wout[:, fc, hf * 512:(hf + 1) * 512]) for fc in range(KC)],
                         reads=["wout%d" % hf] + ["mg%d_%d" % (fc, t // 4) for fc in range(KC)], writes=[B(b)])
                hs = slice(hf * 512, (hf + 1) * 512)
                wr = ["ot%d_%d" % (s, hf)]
                if s in alias and (s, hf) not in first_o:
                    first_o.add((s, hf))
                    wr.append(alias[s])
                P.op("dve", (lambda s_, b_, h_: (lambda e: e.tensor_tensor(out=ot[s_][:, h_], in0=banks[b_], in1=gate_bc[:, h_], op=ALU.mult)))(s, b, hs),
                     reads=[B(b), "gbc%d" % hf], writes=wr)
                P.op("pool", (lambda s_, h_: (lambda e: e.tensor_tensor(out=ot[s_][:, h_], in0=ot[s_][:, h_], in1=xs2[s_][:, h_], op=ALU.add)))(s, hs),
                     reads=["ot%d_%d" % (s, hf), "xs2_%d" % s], writes=["ot%d_%d" % (s, hf)])
            P.op("act", (lambda s_, t_: (lambda e: e.dma_start(out=out_d[t_ * 128:(t_ + 1) * 128, :], in_=ot[s_])))(s, t),
                 reads=["ot%d_0" % s, "ot%d_1" % s], writes=["osb%d" % s], slot="osb%d" % s)
            if t + 4 < NT:
                x2_load(t + 4)
        P.barrier()

    except _Stop:
        P.barrier()

    with ExitStack() as es:
        sems = {}
        for sk in P.count:
            sems[sk] = es.enter_context(nc.semaphore("s_" + sk))
        block = es.enter_context(nc.Block())

        def replay(engname, e):
            for (waits, fn, sk, inc) in P.ops[engname]:
                for (wk, wv) in waits:
                    e.wait_ge(sems[wk], wv)
                if fn is None:
                    continue
                ins = fn(e)
                ins.then_inc(sems[sk], inc)

        @block.tensor
        def _(e):
            replay("pe", e)

        @block.scalar
        def _(e):
            replay("act", e)

        @block.vector
        def _(e):
            replay("dve", e)

        @block.gpsimd
        def _(e):
            replay("pool", e)

        @block.sync
        def _(e):
            replay("sp", e)
    return nc


def _consts():
    c = np.zeros((128, 704), np.float32)
    p = np.arange(128)
    c[p, p] = 1.0
    c[:, 128:256] = ((p[:, None] // 64) == (p[None, :] // 64)).astype(np.float32) / 64.0
    perm = np.zeros((128, 128), np.float32)
    for h in range(2):
        for d in range(8):
            perm[64 * h + d + 8, 64 * h + d] = -1.0
            perm[64 * h + d, 64 * h + d + 8] = 1.0
    c[:, 256:384] = perm
    c[:, 384:512] = (p[None, :] >= p[:, None]).astype(np.float32)
    c[:, 512:640] = (p[:, None] >= p[None, :]).astype(np.float32)
    c[:, 640:704] = 1.0
    freq = np.zeros((128, 2), np.float32)
    fr64 = ROPE_THETA ** (-np.arange(0, 16, 2, dtype=np.float64) / 16.0)
    fr = fr64.astype(np.float32)
    frl = (fr64 - fr.astype(np.float64)).astype(np.float32)
    for h in range(2):
        for d in range(16):
            freq[64 * h + d, 0] = fr[d % 8]
            freq[64 * h + d, 1] = frl[d % 8]
    return c, freq


_NC_CACHE = {}


def make_in_maps(x, c, positions, norm_g, w_ada, b_ada, w_in, q_norm_g, k_norm_g,
                 sgu_ln_g, sgu_ln_b, w_spatial, b_spatial, w_branch_a, w_branch_b, w_out):
    f = np.float32
    cst, freq = _consts()
    gqk = np.zeros((128, 6), f)
    for g in range(3):
        gqk[:, 2 * g] = np.tile(np.asarray(q_norm_g[g], f), 2)
        gqk[:, 2 * g + 1] = np.tile(np.asarray(k_norm_g[g], f), 2)
    wsp = np.ascontiguousarray(np.transpose(np.asarray(w_spatial, f), (2, 0, 1)))
    bsp = np.repeat(np.asarray(b_spatial, f).reshape(4, 2, 128), 64, axis=1)
    bsp = np.ascontiguousarray(np.transpose(bsp, (1, 0, 2)))
    shared = {
        "w_ada": np.ascontiguousarray(w_ada, f),
        "adab": np.ascontiguousarray(np.asarray(b_ada, f).reshape(24, 128).T),
        "normg": np.ascontiguousarray(np.asarray(norm_g, f).reshape(8, 128).T),
        "w_in": np.ascontiguousarray(w_in, f),
        "gqk": gqk,
        "lng": np.ascontiguousarray(np.asarray(sgu_ln_g, f).reshape(1, 512)),
        "lnb": np.ascontiguousarray(np.asarray(sgu_ln_b, f).reshape(1, 512)),
        "wsp": wsp, "bsp": bsp,
        "w_ba": np.ascontiguousarray(w_branch_a, f),
        "w_bb": np.ascontiguousarray(w_branch_b, f),
        "w_out": np.ascontiguousarray(w_out, f),
        "freq": freq, "cst": cst,
    }
    maps = []
    for b in range(8):
        m = dict(shared)
        m["x"] = np.ascontiguousarray(x[b], f)
        m["cT"] = np.ascontiguousarray(np.asarray(c[b], f).reshape(8, 128).T)
        m["pos"] = np.ascontiguousarray(np.asarray(positions[b], np.int32).reshape(1, S))
        maps.append(m)
    return maps


def kernel(**inputs):
    inputs = {k: np.asarray(v) for k, v in inputs.items()}
    if "nc" not in _NC_CACHE:
        _NC_CACHE["nc"] = build_nc()
    nc = _NC_CACHE["nc"]
    in_maps = make_in_maps(**inputs)
    res = run_bass_kernel_spmd(nc, in_maps, core_ids=list(range(8)))
    out = np.stack([np.asarray(r["out"], np.float32) for r in res.results], axis=0)
    return out
```

```python
import math
from contextlib import ExitStack

import numpy as np
import concourse.bass as bass
import concourse.mybir as mybir
from concourse.bass_utils import run_bass_kernel_spmd

F32 = mybir.dt.float32
BF16 = mybir.dt.bfloat16
I32 = mybir.dt.int32
AF = mybir.ActivationFunctionType
ALU = mybir.AluOpType

D = 1024
S = 2048
NT = 16
KC = 8
IN_COLS = 8704
OFF_GATE_A = 4608
OFF_Z_B = 5120
OFF_GATE_B = 6144
OFF_MERGE_A = 6656
OFF_MERGE_B = 7680
EPS = 1e-6
DIL = (1, 4, 16)
ROPE_THETA = 500000.0

ENGS = ("pe", "act", "dve", "pool", "sp")


class Prog:
    def __init__(self):
        self.ops = {e: [] for e in ENGS}
        self.count = {}
        self.waited = {e: {} for e in ENGS}
        self.lastw = {}
        self.readers = {}
        self.bank_i = 0
        self.nb = 8

    def _need(self, eng, reads, writes):
        need = {}

        def add(sk, v, war=False):
            if sk == eng and eng == "pe":
                return
            if need.get(sk, 0) < v:
                need[sk] = v

        for k in reads:
            w = self.lastw.get(k)
            if w:
                add(*w)
        for k in writes:
            w = self.lastw.get(k)
            if w:
                add(*w)
            for r in self.readers.get(k, ()):
                add(r[0], r[1], war=True)
        out = []
        for sk, v in need.items():
            if self.waited[eng].get(sk, 0) >= v:
                continue
            self.waited[eng][sk] = v
            out.append((sk, v))
        return out

    def op(self, eng, fn, reads=(), writes=(), slot=None):
        waits = self._need(eng, reads, writes)
        if slot is None:
            sk, inc = eng, 1
        else:
            sk, inc = "dma_" + slot, 16
        self.count[sk] = self.count.get(sk, 0) + inc
        val = self.count[sk]
        self.ops[eng].append((waits, fn, sk, inc))
        for k in reads:
            self.readers.setdefault(k, []).append((sk, val))
        for k in writes:
            self.lastw[k] = (sk, val)
            self.readers[k] = []

    def barrier(self):
        for e in ENGS:
            waits = []
            for sk, v in self.count.items():
                if sk == e and e == "pe":
                    continue
                if self.waited[e].get(sk, 0) >= v:
                    continue
                self.waited[e][sk] = v
                waits.append((sk, v))
            if waits:
                self.ops[e].append((waits, None, None, 0))
        self.lastw = {}
        self.readers = {}

    def bank(self):
        i = self.bank_i
        self.bank_i = (i + 1) % self.nb
        return i


class Arena:
    def __init__(self, nc, nbytes):
        self.t = nc.alloc_sbuf_tensor("arena", [128, nbytes // 2], BF16)
        self.ap = self.t.ap()
        self.off = 0
        self.cap = nbytes
        self.peak = 0

    def at(self, off, shape, dtype):
        save, savep = self.off, self.peak
        self.off = off
        v = self.alloc(shape, dtype)
        self.off, self.peak = save, savep
        return v

    def alloc(self, shape, dtype):
        es = 2 if dtype == BF16 else 4
        n = 1
        for s in shape[1:]:
            n *= s
        nb = (n * es + 63) // 64 * 64
        o = self.off
        self.off += nb
        self.peak = max(self.peak, self.off)
        assert self.off <= self.cap, ("SBUF arena overflow", self.off, self.cap)
        v = self.ap[:, o // 2:(o + n * es) // 2]
        if dtype != BF16:
            v = v.bitcast(dtype)
        if len(shape) == 3:
            v = v.rearrange("p (a b) -> p a b", a=shape[1])
        elif len(shape) == 4:
            v = v.rearrange("p (a b c) -> p a b c", a=shape[1], b=shape[2])
        return v


class _Stop(Exception):
    pass


def build_nc(debug=False, stop_after=99):
    nc = bass.Bass("TRN2", target_bir_lowering=False)
    dt = nc.dram_tensor
    x_d = dt("x", [S, D], F32, kind="ExternalInput").ap()
    cT_d = dt("cT", [128, 8], F32, kind="ExternalInput").ap()
    pos_d = dt("pos", [1, S], I32, kind="ExternalInput").ap()
    wada_d = dt("w_ada", [D, 3 * D], F32, kind="ExternalInput").ap()
    adab_d = dt("adab", [128, 24], F32, kind="ExternalInput").ap()
    normg_d = dt("normg", [128, 8], F32, kind="ExternalInput").ap()
    win_d = dt("w_in", [D, IN_COLS], F32, kind="ExternalInput").ap()
    gqk_d = dt("gqk", [128, 6], F32, kind="ExternalInput").ap()
    lng_d = dt("lng", [1, 512], F32, kind="ExternalInput").ap()
    lnb_d = dt("lnb", [1, 512], F32, kind="ExternalInput").ap()
    wsp_d = dt("wsp", [128, 8, 128], F32, kind="ExternalInput").ap()
    bsp_d = dt("bsp", [128, 4, 128], F32, kind="ExternalInput").ap()
    wba_d = dt("w_ba", [512, D], F32, kind="ExternalInput").ap()
    wbb_d = dt("w_bb", [512, D], F32, kind="ExternalInput").ap()
    wout_d = dt("w_out", [D, D], F32, kind="ExternalInput").ap()
    freq_d = dt("freq", [128, 2], F32, kind="ExternalInput").ap()
    cst_d = dt("cst", [128, 704], F32, kind="ExternalInput").ap()
    out_d = dt("out", [S, D], F32, kind="ExternalOutput").ap()
    dbg = {}
    if debug:
        for nm, shp, ty in (("d_hT", [128, 8 * S], BF16), ("d_V", [128, 3 * 16 * 512], BF16),
                            ("d_qk", [128, 6 * S], BF16), ("d_ya", [128, 4 * S], BF16),
                            ("d_yb", [128, 4 * S], BF16), ("d_mg", [128, 8 * S], BF16),
                            ("d_tab", [128, 2 * S], BF16), ("d_ada", [128, 24], F32)):
            dbg[nm] = dt(nm, shp, ty, kind="ExternalOutput").ap()

    win_v = win_d.rearrange("(kc p) c -> p kc c", p=128)
    wada_v = wada_d.rearrange("(kc p) c -> p kc c", p=128)

    P = Prog()
    A = Arena(nc, 206 * 1024)
    banks = [nc.alloc_psum_tensor("bank%d" % i, [128, 512], F32).ap() for i in range(8)]

    def B(i):
        return "B%d" % i

    hT = A.alloc([128, KC, S], BF16)
    cst = A.alloc([128, 704], BF16)
    ident = cst[:, 0:128]
    bones = cst[:, 128:256]
    perm = cst[:, 256:384]
    mD = cst[:, 384:512]
    mP = cst[:, 512:640]
    ones64 = cst[:, 640:704]
    mDP = A.alloc([128, 4, 128], BF16)
    mDD = A.alloc([128, 4, 128], BF16)
    WsT = A.alloc([128, 8, 128], BF16)
    bsp = A.alloc([128, 4, 128], F32)
    Ctab = A.alloc([128, S], BF16)
    Stab = A.alloc([128, S], BF16)
    small = A.alloc([128, 128], F32)
    cT = small[:, 0:8]
    sc = small[:, 8:16]
    adab = small[:, 16:40]
    ada = small[:, 40:64]
    normg = small[:, 64:72]
    Acol = small[:, 72:80]
    gqk = small[:, 80:86]
    freq = small[:, 86:87]
    freq_lo = small[:, 87:88]
    ssq = small[:, 88:104]
    rstd_x = small[:, 104:120]
    bnst = small[:, 120:126]
    bnag = small[:, 126:128]
    small2 = A.alloc([128, 64], F32)
    srt = small2[:, 0:16]
    lnr = small2[:, 16:18]
    Bcol = ada[:, 0:8]
    gatecol = ada[:, 16:24]
    negpi = small2[:, 18:19]
    epsc = small2[:, 19:20]
    mPD = A.alloc([128, 4, 128], BF16)
    wchunk = [A.alloc([128, KC, 128], BF16) for _ in range(4)]
    y_aT = A.alloc([128, 4, S], BF16)
    TOP8 = A.cap - 8192
    TOP16 = A.cap - 8192 - 16384
    wbig = [A.at(TOP8, [128, KC, 512], BF16), None]
    TWO_PI = 2.0 * math.pi
    SHR = 1.0 - 2e-6
    PI_LO = 3.1415925
    INV2PI_HI = float(np.float32(1.0 / TWO_PI))
    INV2PI_LO = float(np.float32(1.0 / TWO_PI - INV2PI_HI))
    P.op("dve", lambda e: e.memset(negpi, -math.pi * SHR), writes=["negpi"])
    P.op("dve", lambda e: e.memset(epsc, EPS), writes=["epsc"])

    wc_i = [0]
    wb_i = [0]

    def load_chunk(c0):
        s = wc_i[0] % 4
        wc_i[0] += 1
        key = "wc%d" % s
        P.op("pool", lambda e: e.dma_start(out=wchunk[s], in_=win_v[:, :, c0:c0 + 128]),
             writes=[key], slot=key)
        return wchunk[s], key

    def load_big(c0):
        s = wb_i[0] % 2
        wb_i[0] += 1
        key = "wb%d" % s
        dst = wbig[s]
        P.op("pool", lambda e: e.dma_start(out=dst, in_=win_v[:, :, c0:c0 + 512]),
             writes=[key], slot=key)
        return dst, key

    def mm_group(out_ap, pairs, reads, writes):
        def fn(e):
            n = len(pairs)
            ins = None
            for i, (l, r) in enumerate(pairs):
                ins = e.matmul(out_ap, l, r, start=(i == 0), stop=(i == n - 1))
            return ins
        P.op("pe", fn, reads=reads, writes=writes)

    def zT_block(w, wkey, tb):
        b = P.bank()
        mm_group(banks[b], [(w[:, kc, :], hT[:, kc, tb * 512:(tb + 1) * 512]) for kc in range(KC)],
                 reads=[wkey], writes=[B(b)])
        return b

    try:
        P.op("pool", lambda e: e.dma_start(out=cst, in_=cst_d), writes=["cst"], slot="cst")
        P.op("pool", lambda e: e.dma_start(out=WsT, in_=wsp_d), writes=["WsT"], slot="wsp")
        for nm, dst, src in (("cT", cT, cT_d), ("adab", adab, adab_d), ("normg", normg, normg_d),
                             ("gqk", gqk, gqk_d), ("freq", small[:, 86:88], freq_d), ("bsp", bsp, bsp_d)):
            P.op("sp", (lambda d_, s_: (lambda e: e.dma_start(out=d_, in_=s_)))(dst, src),
                 writes=[nm], slot=nm)
        for j in range(4):
            P.op("dve", (lambda j_: (lambda e: e.tensor_copy(out=mDP[:, j_, :], in_=(mD if j_ % 2 == 0 else mP))))(j),
                 reads=["cst"], writes=["mDP%d" % j])
            P.op("dve", (lambda j_: (lambda e: e.tensor_copy(out=mDD[:, j_, :], in_=mD)))(j),
                 reads=["cst"], writes=["mDD%d" % j])
            P.op("dve", (lambda j_: (lambda e: e.tensor_copy(out=mPD[:, j_, :], in_=(mP if j_ % 2 == 0 else mD))))(j),
                 reads=["cst"], writes=["mPD%d" % j])
        for g in range(8):
            P.op("dve", (lambda g_: (lambda e: e.tensor_tensor(out=WsT[:, g_, :], in0=WsT[:, g_, :], in1=mD, op=ALU.mult)))(g),
                 reads=["cst", "WsT"], writes=["WsT"])

        ph0 = A.off
        xn_all = A.alloc([128, NT, D], BF16)
        xs = [A.alloc([128, D], F32) for _ in range(3)]
        scr = A.alloc([128, D], BF16)
        wada_s = [A.alloc([128, KC, 512], F32) for _ in range(2)]
        rowb = A.alloc([128, 3 * D], F32)
        wada_s.append(A.alloc([128, KC, 512], F32))
        one11 = small2[:, 20:21]
        P.op("dve", lambda e: e.memset(one11, 1.0), writes=["one11"])
        P.op("act", lambda e: e.activation(out=sc, in_=cT, func=AF.Silu), reads=["cT"], writes=["sc"])

        def x_load(t):
            s = t % 3
            P.op("pool", (lambda s_, t_: (lambda e: e.dma_start(out=xs[s_], in_=x_d[t_ * 128:(t_ + 1) * 128, :])))(s, t),
                 writes=["xs%d" % s], slot="xs%d" % s)
        for t in range(3):
            x_load(t)
        early_wv0 = []
        for t in range(NT):
            s = t % 3
            P.op("act", (lambda s_, t_: (lambda e: e.activation(out=scr, in_=xs[s_], func=AF.Square,
                                                                 accum_out=ssq[:, t_:t_ + 1])))(s, t),
                 reads=["xs%d" % s], writes=["scr", "ssq%d" % t])
            P.op("act", (lambda t_: (lambda e: e.activation(out=srt[:, t_:t_ + 1], in_=ssq[:, t_:t_ + 1], func=AF.Sqrt,
                                                             bias=epsc, scale=1.0 / D)))(t),
                 reads=["ssq%d" % t, "epsc"], writes=["srt%d" % t])
            P.op("dve", (lambda t_: (lambda e: e.reciprocal(out=rstd_x[:, t_:t_ + 1], in_=srt[:, t_:t_ + 1])))(t),
                 reads=["srt%d" % t], writes=["rstdx%d" % t])
            P.op("act", (lambda s_, t_: (lambda e: e.activation(out=xn_all[:, t_, :], in_=xs[s_], func=AF.Copy,
                                                                 scale=rstd_x[:, t_:t_ + 1])))(s, t),
                 reads=["xs%d" % s, "rstdx%d" % t], writes=["xn%d" % t])
            if t + 3 < NT:
                x_load(t + 3)
            elif not early_wv0:
                early_wv0.append(load_big(0 * 1536 + 1024))

        for j in range(6):
            s = j % 3
            key = "wada%d" % s
            P.op("sp", (lambda s_, j_: (lambda e: e.dma_start(out=wada_s[s_], in_=wada_v[:, :, j_ * 512:(j_ + 1) * 512])))(s, j),
                 writes=[key], slot=key)
            br_ = P.bank()
            mm_group(banks[br_][0:1, :], [(sc[:, kc:kc + 1], wada_s[s][:, kc, :]) for kc in range(KC)],
                     reads=[key, "sc"], writes=[B(br_)])
            P.op("dve", (lambda j_, b_: (lambda e: e.tensor_copy(out=rowb[0:1, j_ * 512:(j_ + 1) * 512], in_=banks[b_][0:1, :])))(j, br_),
                 reads=[B(br_)], writes=["rowb%d" % j])
        b_ada = P.bank()

        def adaT_fn(e):
            ins = None
            for col in range(24):
                ins = e.matmul(banks[b_ada][:, col:col + 1], rowb[0:1, col * 128:(col + 1) * 128], one11[0:1, 0:1],
                               start=True, stop=True)
            return ins
        P.op("pe", adaT_fn, reads=["rowb%d" % j for j in range(6)] + ["one11"], writes=[B(b_ada)])
        P.op("dve", lambda e: e.tensor_tensor(out=ada, in0=banks[b_ada][:, 0:24], in1=adab, op=ALU.add),
             reads=[B(b_ada), "adab"], writes=["ada"])
        P.op("dve", lambda e: e.scalar_tensor_tensor(out=Acol, in0=ada[:, 8:16], scalar=1.0, in1=normg,
                                                     op0=ALU.add, op1=ALU.mult),
             reads=["ada", "normg"], writes=["Acol"])

        for t in range(NT):
            b = P.bank()
            b2 = P.bank()
            psA = banks[b].rearrange("p (k t) -> p k t", k=4)
            psB = banks[b2].rearrange("p (k t) -> p k t", k=4)

            def tr_fn(e, t_=t, psA_=psA, psB_=psB):
                ins = None
                for kc in range(KC):
                    dst_ = (psA_ if kc < 4 else psB_)[:, kc % 4, :]
                    ins = e.matmul(dst_, xn_all[:, t_, kc * 128:(kc + 1) * 128], ident, start=True, stop=True)
                return ins
            P.op("pe", tr_fn, reads=["xn%d" % t, "cst"], writes=[B(b), B(b2)])
            for kc in range(KC):
                dst = hT[:, kc, t * 128:(t + 1) * 128]
                psT = psA if kc < 4 else psB
                bk = b if kc < 4 else b2
                if kc < 4:
                    P.op("dve", (lambda d_, p_, k_: (lambda e: e.tensor_scalar(out=d_, in0=p_, scalar1=Acol[:, k_:k_ + 1],
                                                                              scalar2=Bcol[:, k_:k_ + 1],
                                                                              op0=ALU.mult, op1=ALU.add)))(dst, psT[:, kc % 4, :], kc),
                         reads=[B(bk), "Acol", "ada"], writes=["hT%d_%d" % (t, kc)])
                else:
                    P.op("act", (lambda d_, p_, k_: (lambda e: e.activation(out=d_, in_=p_, func=AF.Identity,
                                                                           bias=Bcol[:, k_:k_ + 1],
                                                                           scale=Acol[:, k_:k_ + 1])))(dst, psT[:, kc % 4, :], kc),
                         reads=[B(bk), "Acol", "ada"], writes=["hT%d_%d" % (t, kc)])
        if debug:
            P.barrier()
            P.op("sp", lambda e: e.dma_start(out=dbg["d_hT"], in_=hT.rearrange("p k t -> p (k t)")), slot="dbg0")
            P.op("sp", lambda e: e.dma_start(out=dbg["d_ada"], in_=ada), slot="dbg3")
        P.barrier()
        A.off = ph0

        if stop_after == 1:
            raise _Stop()
        off_wbig = A.off
        wbig[1] = A.alloc([128, KC, 512], BF16)
        V = [A.alloc([128, 16, 512], BF16) for _ in range(3)]
        off_qk = A.off
        qk = [[A.alloc([128, S], BF16) for _ in range(2)] for _ in range(3)]
        acc_n = A.alloc([128, S], F32)
        acc_d = A.alloc([128, S], F32)
        pT = [[A.alloc([128, 4, 128], BF16) for _ in range(2)] for _ in range(2)]
        bones_g = A.alloc([128, 6, 128], BF16)
        ginv = A.alloc([128, 8], F32)

        def qk_slice(g, blk):
            d = DIL[g]
            nb = 16 // d
            r, n = blk // nb, blk % nb
            st = r * (S // d) + 128 * n
            return slice(st, st + 128)

        def tok_slice(g, blk):
            d = DIL[g]
            nb = 16 // d
            r, n = blk // nb, blk % nb
            st = 128 * n * d + r
            return slice(st, st + 127 * d + 1, d)

        P.op("dve", lambda e: e.reciprocal(out=ginv[:, 0:6], in_=gqk), reads=["gqk"], writes=["ginv"])
        P.op("dve", lambda e: e.tensor_tensor(out=ginv[:, 0:6], in0=ginv[:, 0:6], in1=ginv[:, 0:6], op=ALU.mult),
             reads=["ginv"], writes=["ginv"])
        for j in range(6):
            P.op("dve", (lambda j_: (lambda e: e.tensor_scalar(out=bones_g[:, j_, :], in0=bones, scalar1=ginv[:, j_:j_ + 1],
                                                               scalar2=None, op0=ALU.mult)))(j),
                 reads=["ginv", "cst"], writes=["bones_g%d" % j])

        seq = [(hp, g) for hp in range(4) for g in range(3)]
        worder = []
        for idx, (hp, g) in enumerate(seq):
            if idx == 0:
                worder += [(hp, g, 0), (hp, g, 1)]
            if idx + 1 < len(seq):
                nh, ng = seq[idx + 1]
                worder += [(nh, ng, 0), (nh, ng, 1)]
            if g == 2:
                worder += [(hp, -1, 0)]
        wloaded = {}
        wnext = [0]

        def wcol(item):
            hp_, g_, role_ = item
            if g_ < 0:
                return OFF_GATE_A + hp_ * 128
            return g_ * 1536 + role_ * 512 + hp_ * 128

        def get_w(item):
            k = worder.index(item)
            while wnext[0] <= min(k + 2, len(worder) - 1):
                wloaded[worder[wnext[0]]] = load_chunk(wcol(worder[wnext[0]]))
                wnext[0] += 1
            return wloaded[item]

        get_w(worder[0])
        posi = A.at(off_qk, [128, S], I32)
        ang = A.at(off_qk + 8192, [128, S], F32)
        yy = A.at(off_qk + 16384, [128, S], F32)
        ki = A.at(off_qk + 24576, [128, S], I32)
        kf = A.at(off_qk + 32768, [128, S], F32)
        P.op("sp", lambda e: e.dma_start(out=posi, in_=pos_d.partition_broadcast(128)),
             writes=["posi"], slot="posi")
        tab_ops = []
        tab_ops.append(lambda: P.op("dve", lambda e: e.tensor_copy(out=ang, in_=posi), reads=["posi"], writes=["ang"]))
        tab_ops.append(lambda: P.op("dve", lambda e: e.tensor_scalar(out=kf, in0=ang, scalar1=freq_lo, scalar2=None, op0=ALU.mult),
                                    reads=["ang", "freq"], writes=["kf"]))
        tab_ops.append(lambda: P.op("dve", lambda e: e.scalar_tensor_tensor(out=ang, in0=ang, scalar=freq, in1=kf,
                                                                            op0=ALU.mult, op1=ALU.add),
                                    reads=["ang", "kf", "freq"], writes=["ang"]))
        for tab, offs in ((Stab, 0.5), (Ctab, 0.75)):
            tab_ops.append((lambda o_: (lambda: P.op("dve", lambda e: e.tensor_scalar(out=yy, in0=ang, scalar1=INV2PI_HI, scalar2=o_,
                                                                                      op0=ALU.mult, op1=ALU.add),
                                                     reads=["ang"], writes=["yy"])))(offs))
            tab_ops.append(lambda: P.op("dve", lambda e: e.scalar_tensor_tensor(out=yy, in0=ang, scalar=INV2PI_LO, in1=yy,
                                                                                op0=ALU.mult, op1=ALU.add),
                                        reads=["ang", "yy"], writes=["yy"]))
            tab_ops.append(lambda: P.op("dve", lambda e: e.tensor_copy(out=ki, in_=yy), reads=["yy"], writes=["ki"]))
            tab_ops.append(lambda: P.op("dve", lambda e: e.tensor_copy(out=kf, in_=ki), reads=["ki"], writes=["kf"]))
            tab_ops.append(lambda: P.op("dve", lambda e: e.tensor_tensor(out=yy, in0=yy, in1=kf, op=ALU.subtract),
                                        reads=["yy", "kf"], writes=["yy"]))
            tab_ops.append(lambda: P.op("dve", lambda e: e.tensor_single_scalar(out=kf, in_=yy, scalar=0.0, op=ALU.is_lt),
                                        reads=["yy"], writes=["kf"]))
            tab_ops.append(lambda: P.op("dve", lambda e: e.tensor_tensor(out=yy, in0=yy, in1=kf, op=ALU.add),
                                        reads=["yy", "kf"], writes=["yy"]))
            tab_ops.append(lambda: P.op("dve", lambda e: e.tensor_scalar(out=yy, in0=yy, scalar1=TWO_PI, scalar2=-math.pi,
                                                                          op0=ALU.mult, op1=ALU.add),
                                        reads=["yy"], writes=["yy"]))
            tab_ops.append(lambda: P.op("dve", lambda e: e.tensor_scalar(out=yy, in0=yy, scalar1=-PI_LO, scalar2=PI_LO,
                                                                          op0=ALU.max, op1=ALU.min),
                                        reads=["yy"], writes=["yy"]))
            tab_ops.append((lambda t_: (lambda: P.op("act", lambda e: e.activation(out=t_, in_=yy, func=AF.Sin),
                                                     reads=["yy"], writes=["tab"])))(tab))

        for g in range(3):
            if g == 0:
                w, wkey = early_wv0[0]
            else:
                w, wkey = load_big(g * 1536 + 1024)
            for blk in range(16):
                sl = tok_slice(g, blk)
                b = P.bank()
                mm_group(banks[b], [(hT[:, kc, sl], w[:, kc, :]) for kc in range(KC)],
                         reads=[wkey], writes=[B(b)])
                if blk % 2 == 0:
                    P.op("act", (lambda g_, k_, b_: (lambda e: e.copy(out=V[g_][:, k_, :], in_=banks[b_])))(g, blk, b),
                         reads=[B(b)], writes=["V%d_%d" % (g, blk)])
                else:
                    P.op("dve", (lambda g_, k_, b_: (lambda e: e.tensor_copy(out=V[g_][:, k_, :], in_=banks[b_])))(g, blk, b),
                         reads=[B(b)], writes=["V%d_%d" % (g, blk)])
                if tab_ops:
                    tab_ops.pop(0)()
        while tab_ops:
            tab_ops.pop(0)()
        P.barrier()
        NBUF = 3
        save_off = A.off
        A.off = off_wbig
        def talloc(nbytes, dtype):
            if A.off < save_off and A.off + nbytes > off_wbig + 8192:
                A.off = save_off
            return A.alloc([128, 512], dtype)
        tmps = []
        for i in range(NBUF):
            tset = dict(zg=talloc(1024, BF16), sq=talloc(1024, BF16), zc=talloc(1024, BF16), zs=talloc(1024, BF16),
                        ln=talloc(2048, F32))
            tset["rs"] = tset["ln"]
            tmps.append(tset)
        ftmp = dict(t1=talloc(2048, F32), t2=talloc(2048, F32))
        if A.off < save_off:
            A.off = save_off
        assert A.off <= TOP8, ("phase 2 overlaps top slot", A.off, TOP8)
        wv = wbig[0]
        P.op("pool", lambda e: e.dma_start(out=wv, in_=win_v[:, :, OFF_Z_B + 512:OFF_Z_B + 1024]), writes=["wv"], slot="wb0")
        P.nb = 6
        P.bank_i = 0
        BO, BD = 6, 7

        cb_i = [0]

        def chunk_block(hp, g, role, tb):
            st = {}

            def stageA():
                w, wkey = get_w((hp, g, role))
                i = cb_i[0] % NBUF
                cb_i[0] += 1
                T = tmps[i]
                sf = "_%d" % i
                gcol = gqk[:, g * 2 + role:g * 2 + role + 1]
                bz = zT_block(w, wkey, tb)
                P.op("act", lambda e: e.activation(out=T["zg"], in_=banks[bz], func=AF.Copy, scale=gcol),
                     reads=[B(bz), "gqk"], writes=["zg" + sf])
                P.op("dve", lambda e: e.tensor_tensor(out=T["sq"], in0=T["zg"], in1=T["zg"], op=ALU.mult),
                     reads=["zg" + sf], writes=["sq" + sf])
                ts_ = slice(tb * 512, (tb + 1) * 512)
                P.op("dve", lambda e: e.tensor_tensor(out=T["zc"], in0=T["zg"], in1=Ctab[:, ts_], op=ALU.mult),
                     reads=["zg" + sf], writes=["zc" + sf])
                P.op("dve", lambda e: e.tensor_tensor(out=T["zs"], in0=T["zg"], in1=Stab[:, ts_], op=ALU.mult),
                     reads=["zg" + sf], writes=["zs" + sf])
                st["T"], st["sf"] = T, sf

            def stageB():
                T, sf = st["T"], st["sf"]
                dstT = qk[g][role]
                ts_ = slice(tb * 512, (tb + 1) * 512)
                bs = P.bank()
                br = P.bank()

                def sr_fn(e):
                    e.matmul(banks[bs], bones_g[:, g * 2 + role, :], T["sq"], start=True, stop=True)
                    e.matmul(banks[br], ident, T["zc"], start=True, stop=False)
                    return e.matmul(banks[br], perm, T["zs"], start=False, stop=True)
                P.op("pe", sr_fn, reads=["sq" + sf, "zc" + sf, "zs" + sf, "bones_g%d" % (g * 2 + role), "cst"], writes=[B(bs), B(br)])
                P.op("act", lambda e: e.activation(out=T["ln"], in_=banks[bs], func=AF.Ln, bias=epsc, scale=1.0),
                     reads=[B(bs), "epsc"], writes=["ln" + sf])
                P.op("act", lambda e: e.activation(out=T["rs"], in_=T["ln"], func=AF.Exp, scale=-0.5),
                     reads=["ln" + sf], writes=["ln" + sf])
                dg = DIL[g]
                if dg == 1:
                    o_ap, i0_ap, i1_ap = dstT[:, ts_], banks[br], T["rs"]
                else:
                    mloc = 512 // dg
                    o_ap = dstT.rearrange("p (r m) -> p r m", r=dg)[:, :, tb * mloc:(tb + 1) * mloc]
                    i0_ap = banks[br].rearrange("p (m r) -> p r m", r=dg)
                    i1_ap = T["rs"].rearrange("p (m r) -> p r m", r=dg)
                P.op("dve", lambda e: e.tensor_tensor(out=o_ap, in0=i0_ap, in1=i1_ap, op=ALU.mult),
                     reads=[B(br), "ln" + sf], writes=["qk%d%d_%d" % (g, role, tb)])
            return stageA, stageB

        def chunk_steps(hp, g):
            return [chunk_block(hp, g, ro, tb) for tb in range(4) for ro in range(2)]

        unit_ctr = [0]

        def att_units(hp, g):
            d = DIL[g]
            nb = 16 // d
            qT, kT = qk[g][0], qk[g][1]
            qkeys = ["qk%d%d_%d" % (g, ro, tb) for ro in range(2) for tb in range(4)]
            if g == 2:
                qgroups = [[r for r in range(4 * i, 4 * i + 4)] for i in range(4)]
            else:
                qgroups = [[r * nb + n for n in range(4 * i, 4 * i + 4)] for r in range(d) for i in range(nb // 4)]
            out = []
            for qg in qgroups:
                subs = []
                for slot_, blk in enumerate(qg):
                    n = blk % nb
                    if g != 2 and n > 0:
                        subs.append((blk - 1, blk, slot_, True, False, 1))
                        subs.append((blk, blk, slot_, False, True, 0))
                    else:
                        subs.append((blk, blk, slot_, True, True, 0))
                units = [subs[i:i + 4] for i in range(0, len(subs), 4)]
                for ui, un in enumerate(units):
                    par = unit_ctr[0] % 2
                    unit_ctr[0] += 1
                    last_of_group = (ui == len(units) - 1)

                    def S_stage(un=un, par=par):
                        types = [u_[5] for u_ in un]
                        nj = len(un)
                        bS = [P.bank(), P.bank()]
                        pss = [banks[bS[e_]].rearrange("p (j q) -> p j q", j=4) for e_ in range(2)]

                        def s_fn(e):
                            ins = None
                            for j, (kb, qb, _s, _f, _l, _t) in enumerate(un):
                                for e_ in range(2):
                                    rows = slice(64 * e_, 64 * e_ + 64)
                                    ins = e.matmul(pss[e_][:, j, :], kT[rows, qk_slice(g, kb)], qT[rows, qk_slice(g, qb)],
                                                   start=True, stop=True)
                            return ins
                        if g == 2:
                            rk = qkeys
                        else:
                            tbs_k = set((kb // 4) if g == 0 else (kb % nb) for (kb, qb, _s, _f, _l, _t) in un)
                            tbs_q = set((qb // 4) if g == 0 else (qb % nb) for (kb, qb, _s, _f, _l, _t) in un)
                            rk = ["qk%d1_%d" % (g, t_) for t_ in tbs_k] + ["qk%d0_%d" % (g, t_) for t_ in tbs_q]
                        P.op("pe", s_fn, reads=rk, writes=[B(bS[0]), B(bS[1])])
                        if all(t_ == 0 for t_ in types):
                            mk = mDD
                        elif all(types[j] == (j % 2) for j in range(nj)):
                            mk = mDP
                        elif all(types[j] == ((j + 1) % 2) for j in range(nj)):
                            mk = mPD
                        else:
                            raise AssertionError("mask pattern")
                        for e_ in range(2):
                            pt = pT[par][e_]
                            pkey = "pT%d%d" % (par, e_)
                            P.op("act", (lambda ps=pss[e_], pt=pt: (lambda e: e.activation(out=pt[:, 0:nj, :], in_=ps[:, 0:nj, :],
                                                                                           func=AF.Exp, scale=0.125)))(),
                                 reads=[B(bS[e_])], writes=[pkey])
                            P.op("dve",
                                 (lambda pt=pt, mk=mk: (lambda e: e.tensor_tensor(out=pt[:, 0:nj, :], in0=pt[:, 0:nj, :],
                                                                                  in1=mk[:, 0:nj, :], op=ALU.mult)))(),
                                 reads=[pkey], writes=[pkey])

                    def PV_stage(un=un, par=par):
                        po = banks[BO].rearrange("p (j q) -> p j q", j=4)
                        pd = banks[BD].rearrange("p (j q) -> p j q", j=4)

                        def pv_fn(e):
                            ins = None
                            for j, (kb, qb, sl_, f_, l_, _t) in enumerate(un):
                                for e_ in range(2):
                                    orow = slice(64 * e_, 64 * e_ + 64)
                                    hcol = slice((hp * 2 + e_) * 64, (hp * 2 + e_) * 64 + 64)
                                    e.matmul(po[orow, sl_, :], V[g][:, kb, hcol], pT[par][e_][:, j, :], start=f_, stop=l_)
                                for e_ in range(2):
                                    orow = slice(64 * e_, 64 * e_ + 64)
                                    ins = e.matmul(pd[orow, sl_, :], ones64, pT[par][e_][:, j, :], start=f_, stop=l_)
                            return ins
                        P.op("pe", pv_fn, reads=["pT%d0" % par, "pT%d1" % par, "cst"], writes=[B(BO), B(BD)])

                    def post(qg=qg):
                        if g == 0:
                            t0 = (qg[0] % nb) * 128
                            dn = acc_n[:, t0:t0 + 512]
                            dd = acc_d[:, t0:t0 + 512]
                            P.op("act", lambda e: e.copy(out=dn, in_=banks[BO]), reads=[B(BO)], writes=["accn"])
                            P.op("dve", lambda e: e.tensor_copy(out=dd, in_=banks[BD]), reads=[B(BD)], writes=["accd"])
                        else:
                            if g == 1:
                                r = qg[0] // nb
                                dn = acc_n[:, r:S:4]
                                dd = acc_d[:, r:S:4]
                                sn = banks[BO]
                                sd = banks[BD]
                            else:
                                r0 = qg[0]
                                dn = acc_n.rearrange("p (i r) -> p r i", r=16)[:, r0:r0 + 4, :]
                                dd = acc_d.rearrange("p (i r) -> p r i", r=16)[:, r0:r0 + 4, :]
                                sn = banks[BO].rearrange("p (j q) -> p j q", j=4)
                                sd = banks[BD].rearrange("p (j q) -> p j q", j=4)
                            P.op("dve", lambda e: e.tensor_tensor(out=dn, in0=sn, in1=dn, op=ALU.add),
                                 reads=[B(BO), "accn"], writes=["accn"])
                            P.op("dve", lambda e: e.tensor_tensor(out=dd, in0=sd, in1=dd, op=ALU.add),
                                 reads=[B(BD), "accd"], writes=["accd"])
                    out.append((S_stage, PV_stage, post if last_of_group else None))
            return out

        def finalize(hp):
            w, wkey = get_w((hp, -1, 0))
            for tb in range(4):
                i = cb_i[0] % NBUF
                cb_i[0] += 1
                T = dict(tmps[i])
                T.update(ftmp)
                sf = "_%d" % i
                ts_ = slice(tb * 512, (tb + 1) * 512)
                bz = zT_block(w, wkey, tb)
                P.op("act", (lambda T=T, bz=bz: (lambda e: e.activation(out=T["t2"], in_=banks[bz], func=AF.Exp, scale=-1.0)))(),
                     reads=[B(bz)], writes=["ft2"])
                P.op("dve", (lambda T=T, ts_=ts_: (lambda e: e.scalar_tensor_tensor(out=T["t1"], in0=T["t2"], scalar=1.0, in1=acc_d[:, ts_],
                                                                                    op0=ALU.add, op1=ALU.mult)))(),
                     reads=["ft2", "accd"], writes=["ft1"])
                P.op("act", (lambda T=T: (lambda e: e.activation(out=T["ln"], in_=T["t1"], func=AF.Ln)))(),
                     reads=["ft1"], writes=["ln" + sf])
                P.op("act", (lambda T=T: (lambda e: e.activation(out=T["rs"], in_=T["ln"], func=AF.Exp, scale=-1.0)))(),
                     reads=["ln" + sf], writes=["ln" + sf])
                P.op("dve", (lambda T=T, ts_=ts_: (lambda e: e.tensor_tensor(out=T["t1"], in0=acc_n[:, ts_], in1=T["rs"], op=ALU.mult)))(),
                     reads=["accn", "ln" + sf, "ft1"], writes=["ft1"])
                P.op("dve", (lambda T=T, ts_=ts_, bz=bz: (lambda e: e.tensor_tensor(out=y_aT[:, hp, ts_], in0=banks[bz], in1=T["t1"], op=ALU.mult)))(),
                     reads=[B(bz), "ft1"], writes=["ya%d_%d" % (hp, tb)])

        pendB = [None]

        def emit_chunk(ab):
            ab[0]()
            if pendB[0] is not None:
                pendB[0]()
            pendB[0] = ab[1]

        def flushB():
            if pendB[0] is not None:
                pendB[0]()
                pendB[0] = None

        for ab in chunk_steps(*seq[0]):
            emit_chunk(ab)
        flushB()
        for idx, (hp, g) in enumerate(seq):
            units = att_units(hp, g)
            nxt = chunk_steps(*seq[idx + 1]) if idx + 1 < len(seq) else []
            ci = 0
            units[0][0]()
            for u in range(len(units)):
                if u + 1 < len(units):
                    units[u + 1][0]()
                k = -(-(u + 1) * len(nxt) // len(units)) - ci
                for _ in range(k):
                    emit_chunk(nxt[ci])
                    ci += 1
                units[u][1]()
                if units[u][2] is not None:
                    units[u][2]()
            flushB()
            if g == 2:
                finalize(hp)
            if debug and hp == 0 and g == 2:
                P.barrier()
                for g2 in range(3):
                    P.op("sp", (lambda g_: (lambda e: e.dma_start(out=dbg["d_V"][:, g_ * 8192:(g_ + 1) * 8192],
                                                                  in_=V[g_].rearrange("p a b -> p (a b)"))))(g2), slot="dbgV%d" % g2)
                P.barrier()

        P.barrier()
        P.nb = 8
        P.bank_i = 0
        A.off = ph0

        if stop_after == 2:
            raise _Stop()
        y_bT = A.alloc([128, 4, S], BF16)
        wug = A.alloc([128, 8, KC, 128], BF16)
        sv_all = A.alloc([128, 4, S], F32)
        lng = A.alloc([128, 512], F32)
        lnb = A.alloc([128, 512], F32)
        vg1 = A.alloc([128, 4, 512], F32)
        vg = [vg1, vg1]
        vn_t = [A.alloc([128, 512], F32) for _ in range(2)]
        vl_t = [A.alloc([128, 512], BF16) for _ in range(4)]
        gu_t = [A.alloc([128, 512], F32) for _ in range(2)]
        gs_t = [A.alloc([128, 512], F32) for _ in range(2)]
        st3 = A.alloc([128, 64], F32)
        bst = [st3[:, 0:24].rearrange("p (j k) -> p j k", j=4), st3[:, 24:48].rearrange("p (j k) -> p j k", j=4)]
        mv = [st3[:, 48:56].rearrange("p (j k) -> p j k", j=4), st3[:, 56:64].rearrange("p (j k) -> p j k", j=4)]
        st3b = A.alloc([128, 16], F32)
        rsd = [st3b[:, 0:4], st3b[:, 4:8]]
        assert A.off <= TOP16, ("phase 3 overlaps top16", A.off, TOP16)
        for j in (0, 4, 1, 5, 2, 6, 3, 7):
            c0 = OFF_Z_B + j * 128 if j < 4 else OFF_GATE_B + (j - 4) * 128
            P.op("pool", (lambda j_, c_: (lambda e: e.dma_start(out=wug[:, j_], in_=win_v[:, :, c_:c_ + 128])))(j, c0),
                 writes=["wug%d" % j], slot="wug%d" % j)
        wba = A.at(TOP16, [128, 4, D], BF16)
        wbb = A.at(TOP16 + 8192, [128, 4, D], BF16)
        P.op("pool", lambda e: e.dma_start(out=wba, in_=wba_d.rearrange("(kc p) c -> p kc c", p=128)), writes=["wba"], slot="wba")
        P.op("pool", lambda e: e.dma_start(out=wbb, in_=wbb_d.rearrange("(kc p) c -> p kc c", p=128)), writes=["wbb"], slot="wbb")
        P.op("sp", lambda e: e.dma_start(out=lng, in_=lng_d.partition_broadcast(128)), writes=["lng"], slot="lng")
        P.op("sp", lambda e: e.dma_start(out=lnb, in_=lnb_d.partition_broadcast(128)), writes=["lnb"], slot="lnb")

        gate_bc = A.at(TOP8, [128, D], F32)
        gate_bf = A.at(TOP8 + 4096, [128, KC, 128], BF16)
        onesT = A.at(TOP8 + 6144, [128, 128], BF16)
        gres = A.at(TOP8 + 6400, [128, KC], F32)
        ghi = A.at(TOP8 + 6464, [128, KC], BF16)

        def gate_steps():
            bgt = [P.bank(), P.bank()]
            steps = []

            def prep0():
                P.op("dve", lambda e: e.memset(onesT, 1.0), writes=["onesT", "wv"])
                P.op("dve", lambda e: e.tensor_copy(out=ghi, in_=gatecol), writes=["ghi"])
                P.op("dve", lambda e: e.tensor_tensor(out=gres, in0=gatecol, in1=ghi, op=ALU.subtract), reads=["ghi"], writes=["gres"])
            steps.append(prep0)
            for half, (src, key) in enumerate(((ghi, "ghi"), (gres, "gres"))):
                def prep(src=src, key=key):
                    for kc in range(KC):
                        P.op("dve", (lambda k_: (lambda e: e.tensor_scalar(out=gate_bf[:, k_, :], in0=ident, scalar1=src[:, k_:k_ + 1],
                                                                           scalar2=None, op0=ALU.mult)))(kc),
                             reads=[key], writes=["gbf%d" % kc])
                steps.append(prep)

                def pe_ev(half=half):
                    for hf in range(2):
                        bb_ = bgt[hf]

                        def g_fn(e, hf_=hf, bb__=bb_):
                            ins = None
                            for j in range(4):
                                kc = hf_ * 4 + j
                                ins = e.matmul(banks[bb__][:, j * 128:(j + 1) * 128], onesT, gate_bf[:, kc, :], start=True, stop=True)
                            return ins
                        P.op("pe", g_fn, reads=["onesT"] + ["gbf%d" % k for k in range(KC)], writes=[B(bb_)])
                        dst = gate_bc[:, hf * 512:(hf + 1) * 512]
                        if half == 0:
                            P.op("dve", (lambda d_, b_: (lambda e: e.tensor_copy(out=d_, in_=banks[b_])))(dst, bb_),
                                 reads=[B(bb_)], writes=["gbc%d" % hf])
                        else:
                            P.op("dve", (lambda d_, b_: (lambda e: e.tensor_tensor(out=d_, in0=banks[b_], in1=d_, op=ALU.add)))(dst, bb_),
                                 reads=[B(bb_), "gbc%d" % hf], writes=["gbc%d" % hf])
                steps.append(pe_ev)
            return steps

        mw = {0: (load_chunk(OFF_MERGE_A), load_chunk(OFF_MERGE_B)),
              1: (load_chunk(OFF_MERGE_A + 128), load_chunk(OFF_MERGE_B + 128))}
        def V1(t):
            tbp = (t // 4) % 2
            j = t % 4
            b = P.bank()
            mm_group(banks[b], [(hT[:, kc, t * 128:(t + 1) * 128], wv[:, kc, :]) for kc in range(KC)],
                     reads=["wv"], writes=[B(b)])
            P.op("act", lambda e: e.activation(out=vg[tbp][:, j, :], in_=banks[b], func=AF.Gelu),
                 reads=[B(b)], writes=["vg_%d" % j])
            P.op("dve", lambda e: e.bn_stats(out=bst[tbp][:, j, :], in_=vg[tbp][:, j, :]),
                 reads=["vg_%d" % j], writes=["bst%d_%d" % (tbp, j)])
            P.op("dve", lambda e: e.bn_aggr(out=mv[tbp][:, j, :], in_=bst[tbp][:, j, :]),
                 reads=["bst%d_%d" % (tbp, j)], writes=["mv%d_%d" % (tbp, j)])

        def V2(tb):
            tbp = tb % 2
            P.op("act", lambda e: e.activation(out=rsd[tbp], in_=mv[tbp][:, :, 1], func=AF.Sqrt, bias=epsc, scale=1.0),
                 reads=["mv%d_%d" % (tbp, j) for j in range(4)] + ["epsc"], writes=["rsd%d" % tbp])
            P.op("dve", lambda e: e.reciprocal(out=rsd[tbp], in_=rsd[tbp]), reads=["rsd%d" % tbp], writes=["rsd%d" % tbp])

        def V3a(t):
            tbp = (t // 4) % 2
            j = t % 4
            s = t % 2
            P.op("dve", lambda e: e.tensor_scalar(out=vn_t[s], in0=vg[tbp][:, j, :], scalar1=mv[tbp][:, j, 0:1],
                                                  scalar2=rsd[tbp][:, j:j + 1], op0=ALU.subtract, op1=ALU.mult),
                 reads=["vg_%d" % j, "mv%d_%d" % (tbp, j), "rsd%d" % tbp], writes=["vn%d" % s])
            P.op("dve", lambda e: e.tensor_tensor(out=vn_t[s], in0=vn_t[s], in1=lng, op=ALU.mult),
                 reads=["vn%d" % s, "lng"], writes=["vn%d" % s])
            P.op("pool", lambda e: e.tensor_tensor(out=vl_t[j], in0=vn_t[s], in1=lnb, op=ALU.add),
                 reads=["vn%d" % s, "lnb"], writes=["vl%d" % j])

        def V3b(t):
            j = t % 4
            bsv = P.bank()
            psv = banks[bsv].rearrange("p (c q) -> p c q", c=4)

            def sv_fn(e):
                ins = None
                for gg in range(8):
                    ins = e.matmul(psv[64 * (gg % 2):64 * (gg % 2) + 64, gg // 2, :], vl_t[j][:, gg * 64:(gg + 1) * 64],
                                   WsT[:, gg, :], start=True, stop=True)
                return ins
            P.op("pe", sv_fn, reads=["vl%d" % j, "WsT"], writes=[B(bsv)])
            P.op("dve", lambda e: e.tensor_tensor(out=sv_all[:, :, t * 128:(t + 1) * 128], in0=psv, in1=bsp, op=ALU.add),
                 reads=[B(bsv), "bsp"], writes=["sv%d" % t])

        ug_i = [0]

        def UG(c, tb):
            i = ug_i[0] % 2
            ug_i[0] += 1
            ts_ = slice(tb * 512, (tb + 1) * 512)
            bu = P.bank()
            mm_group(banks[bu], [(wug[:, c, kc, :], hT[:, kc, ts_]) for kc in range(KC)], reads=["wug%d" % c], writes=[B(bu)])
            bg = P.bank()
            mm_group(banks[bg], [(wug[:, 4 + c, kc, :], hT[:, kc, ts_]) for kc in range(KC)], reads=["wug%d" % (4 + c)], writes=[B(bg)])
            P.op("act", lambda e: e.activation(out=gu_t[i], in_=banks[bu], func=AF.Gelu), reads=[B(bu)], writes=["gu%d" % i])

            def a2():
                P.op("act", lambda e: e.activation(out=gs_t[i], in_=banks[bg], func=AF.Silu), reads=[B(bg)], writes=["gs%d" % i])
                P.op("dve", lambda e: e.tensor_tensor(out=gu_t[i], in0=gu_t[i], in1=gs_t[i], op=ALU.mult),
                     reads=["gu%d" % i, "gs%d" % i], writes=["gu%d" % i])

            def b_():
                P.op("pool", lambda e: e.tensor_tensor(out=y_bT[:, c, ts_], in0=gu_t[i], in1=sv_all[:, c, ts_], op=ALU.mult),
                     reads=["gu%d" % i] + ["sv%d" % t for t in range(4 * tb, 4 * tb + 4)], writes=["yb%d_%d" % (c, tb)])
            return a2, b_

        for t in range(4):
            V1(t)
        V2(0)
        for t in range(4):
            V3a(t)
        for tb in range(4):
            pend = []
            if tb == 3:
                gsteps = gate_steps()
                gsteps[0]()
                gsteps[1]()
            for c in range(4):
                a2, b_ = UG(c, tb)
                if tb + 1 < 4:
                    V1(4 * (tb + 1) + c)
                a2()
                if c < 2:
                    pend.append(b_)
                    if c == 1:
                        for t in range(4 * tb, 4 * tb + 4):
                            V3b(t)
                        for f_ in pend:
                            f_()
                else:
                    b_()
                if tb == 3 and c == 1:
                    gsteps[2]()
                if tb == 3 and c == 2:
                    gsteps[3]()
                if tb == 3 and c == 3:
                    gsteps[4]()
            if tb + 1 < 4:
                V2(tb + 1)
                for t in range(4 * (tb + 1), 4 * (tb + 1) + 4):
                    V3a(t)
        if debug:
            P.barrier()
            P.op("sp", lambda e: e.dma_start(out=dbg["d_ya"], in_=y_aT.rearrange("p k t -> p (k t)")), slot="dbg4")
            P.op("sp", lambda e: e.dma_start(out=dbg["d_yb"], in_=y_bT.rearrange("p k t -> p (k t)")), slot="dbg5")
        P.barrier()
        A.off = ph0

        if stop_after == 3:
            raise _Stop()
        y_bT = A.alloc([128, 4, S], BF16)
        wout = A.alloc([128, KC, D], BF16)
        mg = A.alloc([128, KC, S], BF16)
        xs2 = [A.alloc([128, D], F32) for _ in range(2)]
        ot = [A.alloc([128, D], F32) for _ in range(2)]
        sga = A.alloc([128, 512], F32)
        sgb = A.alloc([128, 512], F32)
        m1 = A.alloc([128, 512], F32)
        m2 = A.alloc([128, 512], F32)
        for h in range(2):
            P.op("pool", (lambda h_: (lambda e: e.dma_start(out=wout[:, :, h_ * 512:(h_ + 1) * 512],
                                                            in_=wout_d.rearrange("(kc p) c -> p kc c", p=128)[:, :, h_ * 512:(h_ + 1) * 512])))(h),
                 writes=["wout%d" % h], slot="wout%d" % h)
        mtmp = [dict(sga=sga, sgb=sgb, m1=m1, m2=m2),
                dict(sga=A.alloc([128, 512], F32), sgb=A.alloc([128, 512], F32),
                     m1=A.alloc([128, 512], F32), m2=A.alloc([128, 512], F32))]
        assert A.off <= TOP16, ("phase 4 overlaps top16", A.off, TOP16)
        it4 = 0
        for fc in range(KC):
            if fc + 1 < KC and (fc + 1) not in mw:
                mw[fc + 1] = (load_chunk(OFF_MERGE_A + (fc + 1) * 128), load_chunk(OFF_MERGE_B + (fc + 1) * 128))
            (wma, wmakey), (wmb, wmbkey) = mw[fc]
            for tb in range(4):
                ts_ = slice(tb * 512, (tb + 1) * 512)
                M = mtmp[it4 % 2]
                sf = "_%d" % (it4 % 2)
                it4 += 1
                bga = zT_block(wma, wmakey, tb)
                bgb = zT_block(wmb, wmbkey, tb)
                bpa = P.bank()
                mm_group(banks[bpa], [(wba[:, kc, fc * 128:(fc + 1) * 128], y_aT[:, kc, ts_]) for kc in range(4)],
                         reads=["wba"], writes=[B(bpa)])
                bpb = P.bank()
                mm_group(banks[bpb], [(wbb[:, kc, fc * 128:(fc + 1) * 128], y_bT[:, kc, ts_]) for kc in range(4)],
                         reads=["wbb"], writes=[B(bpb)])
                P.op("act", (lambda b_, M=M: (lambda e: e.activation(out=M["sga"], in_=banks[b_], func=AF.Sigmoid)))(bga),
                     reads=[B(bga)], writes=["sga" + sf])
                P.op("act", (lambda b_, M=M: (lambda e: e.activation(out=M["sgb"], in_=banks[b_], func=AF.Sigmoid)))(bgb),
                     reads=[B(bgb)], writes=["sgb" + sf])
                P.op("dve", (lambda b_, M=M: (lambda e: e.tensor_tensor(out=M["m1"], in0=banks[b_], in1=M["sga"], op=ALU.mult)))(bpa),
                     reads=[B(bpa), "sga" + sf], writes=["m1" + sf])
                P.op("dve", (lambda b_, M=M: (lambda e: e.tensor_tensor(out=M["m2"], in0=banks[b_], in1=M["sgb"], op=ALU.mult)))(bpb),
                     reads=[B(bpb), "sgb" + sf], writes=["m2" + sf])
                P.op("pool", (lambda f_, s_, M=M: (lambda e: e.tensor_tensor(out=mg[:, f_, s_], in0=M["m1"], in1=M["m2"], op=ALU.add)))(fc, ts_),
                     reads=["m1" + sf, "m2" + sf], writes=["mg%d_%d" % (fc, tb)])
        if debug:
            P.barrier()
            P.op("sp", lambda e: e.dma_start(out=dbg["d_mg"], in_=mg.rearrange("p k t -> p (k t)")), slot="dbg6")
            P.barrier()
        xs2 = xs2 + [A.at(TOP16, [128, D], F32), A.at(TOP16 + 8192, [128, D], F32)]
        ot = ot + [A.at(TOP16 + 4096, [128, D], F32), A.at(TOP16 + 12288, [128, D], F32)]
        alias = {2: "wba", 3: "wbb"}
        first_x = set()
        first_o = set()

        def x2_load(t):
            s = t % 4
            wr = ["xs2_%d" % s]
            if s in alias and s not in first_x:
                first_x.add(s)
                wr.append(alias[s])
            P.op("sp", (lambda s_, t_: (lambda e: e.dma_start(out=xs2[s_], in_=x_d[t_ * 128:(t_ + 1) * 128, :])))(s, t),
                 writes=wr, slot="xs2_%d" % s)
        for t in range(4):
            x2_load(t)
        for t in range(NT):
            s = t % 4
            for hf in range(2):
                b = P.bank()
                mm_group(banks[b], [(mg[:, fc, t * 128:(t + 1) * 128], wout[:, fc, hf * 512:(hf + 1) * 512]) for fc in range(KC)],
                         reads=["wout%d" % hf] + ["mg%d_%d" % (fc, t // 4) for fc in range(KC)], writes=[B(b)])
                hs = slice(hf * 512, (hf + 1) * 512)
                wr = ["ot%d_%d" % (s, hf)]
                if s in alias and (s, hf) not in first_o:
                    first_o.add((s, hf))
                    wr.append(alias[s])
                P.op("dve", (lambda s_, b_, h_: (lambda e: e.tensor_tensor(out=ot[s_][:, h_], in0=banks[b_], in1=gate_bc[:, h_], op=ALU.mult)))(s, b, hs),
                     reads=[B(b), "gbc%d" % hf], writes=wr)
                P.op("pool", (lambda s_, h_: (lambda e: e.tensor_tensor(out=ot[s_][:, h_], in0=ot[s_][:, h_], in1=xs2[s_][:, h_], op=ALU.add)))(s, hs),
                     reads=["ot%d_%d" % (s, hf), "xs2_%d" % s], writes=["ot%d_%d" % (s, hf)])
            P.op("act", (lambda s_, t_: (lambda e: e.dma_start(out=out_d[t_ * 128:(t_ + 1) * 128, :], in_=ot[s_])))(s, t),
                 reads=["ot%d_0" % s, "ot%d_1" % s], writes=["osb%d" % s], slot="osb%d" % s)
            if t + 4 < NT:
                x2_load(t + 4)
        P.barrier()

    except _Stop:
        P.barrier()

    with ExitStack() as es:
        sems = {}
        for sk in P.count:
            sems[sk] = es.enter_context(nc.semaphore("s_" + sk))
        block = es.enter_context(nc.Block())

        def replay(engname, e):
            for (waits, fn, sk, inc) in P.ops[engname]:
                for (wk, wv) in waits:
                    e.wait_ge(sems[wk], wv)
                if fn is None:
                    continue
                ins = fn(e)
                ins.then_inc(sems[sk], inc)

        @block.tensor
        def _(e):
            replay("pe", e)

        @block.scalar
        def _(e):
            replay("act", e)

        @block.vector
        def _(e):
            replay("dve", e)

        @block.gpsimd
        def _(e):
            replay("pool", e)

        @block.sync
        def _(e):
            replay("sp", e)
    return nc


def _consts():
    c = np.zeros((128, 704), np.float32)
    p = np.arange(128)
    c[p, p] = 1.0
    c[:, 128:256] = ((p[:, None] // 64) == (p[None, :] // 64)).astype(np.float32) / 64.0
    perm = np.zeros((128, 128), np.float32)
    for h in range(2):
        for d in range(8):
            perm[64 * h + d + 8, 64 * h + d] = -1.0
            perm[64 * h + d, 64 * h + d + 8] = 1.0
    c[:, 256:384] = perm
    c[:, 384:512] = (p[None, :] >= p[:, None]).astype(np.float32)
    c[:, 512:640] = (p[:, None] >= p[None, :]).astype(np.float32)
    c[:, 640:704] = 1.0
    freq = np.zeros((128, 2), np.float32)
    fr64 = ROPE_THETA ** (-np.arange(0, 16, 2, dtype=np.float64) / 16.0)
    fr = fr64.astype(np.float32)
    frl = (fr64 - fr.astype(np.float64)).astype(np.float32)
    for h in range(2):
        for d in range(16):
            freq[64 * h + d, 0] = fr[d % 8]
            freq[64 * h + d, 1] = frl[d % 8]
    return c, freq


_NC_CACHE = {}


def make_in_maps(x, c, positions, norm_g, w_ada, b_ada, w_in, q_norm_g, k_norm_g,
                 sgu_ln_g, sgu_ln_b, w_spatial, b_spatial, w_branch_a, w_branch_b, w_out):
    f = np.float32
    cst, freq = _consts()
    gqk = np.zeros((128, 6), f)
    for g in range(3):
        gqk[:, 2 * g] = np.tile(np.asarray(q_norm_g[g], f), 2)
        gqk[:, 2 * g + 1] = np.tile(np.asarray(k_norm_g[g], f), 2)
    wsp = np.ascontiguousarray(np.transpose(np.asarray(w_spatial, f), (2, 0, 1)))
    bsp = np.repeat(np.asarray(b_spatial, f).reshape(4, 2, 128), 64, axis=1)
    bsp = np.ascontiguousarray(np.transpose(bsp, (1, 0, 2)))
    shared = {
        "w_ada": np.ascontiguousarray(w_ada, f),
        "adab": np.ascontiguousarray(np.asarray(b_ada, f).reshape(24, 128).T),
        "normg": np.ascontiguousarray(np.asarray(norm_g, f).reshape(8, 128).T),
        "w_in": np.ascontiguousarray(w_in, f),
        "gqk": gqk,
        "lng": np.ascontiguousarray(np.asarray(sgu_ln_g, f).reshape(1, 512)),
        "lnb": np.ascontiguousarray(np.asarray(sgu_ln_b, f).reshape(1, 512)),
        "wsp": wsp, "bsp": bsp,
        "w_ba": np.ascontiguousarray(w_branch_a, f),
        "w_bb": np.ascontiguousarray(w_branch_b, f),
        "w_out": np.ascontiguousarray(w_out, f),
        "freq": freq, "cst": cst,
    }
    maps = []
    for b in range(8):
        m = dict(shared)
        m["x"] = np.ascontiguousarray(x[b], f)
        m["cT"] = np.ascontiguousarray(np.asarray(c[b], f).reshape(8, 128).T)
        m["pos"] = np.ascontiguousarray(np.asarray(positions[b], np.int32).reshape(1, S))
        maps.append(m)
    return maps


def kernel(**inputs):
    inputs = {k: np.asarray(v) for k, v in inputs.items()}
    if "nc" not in _NC_CACHE:
        _NC_CACHE["nc"] = build_nc()
    nc = _NC_CACHE["nc"]
    in_maps = make_in_maps(**inputs)
    res = run_bass_kernel_spmd(nc, in_maps, core_ids=list(range(8)))
    out = np.stack([np.asarray(r["out"], np.float32) for r in res.results], axis=0)
    return out
```

```python
import math
from contextlib import ExitStack

import numpy as np
import concourse.bass as bass
import concourse.mybir as mybir
from concourse.bass_utils import run_bass_kernel_spmd

F32 = mybir.dt.float32
BF16 = mybir.dt.bfloat16
I32 = mybir.dt.int32
AF = mybir.ActivationFunctionType
ALU = mybir.AluOpType

D = 1024
S = 2048
NT = 16
KC = 8
IN_COLS = 8704
OFF_GATE_A = 4608
OFF_Z_B = 5120
OFF_GATE_B = 6144
OFF_MERGE_A = 6656
OFF_MERGE_B = 7680
EPS = 1e-6
DIL = (1, 4, 16)
ROPE_THETA = 500000.0

ENGS = ("pe", "act", "dve", "pool", "sp")


class Prog:
    def __init__(self):
        self.ops = {e: [] for e in ENGS}
        self.count = {}
        self.waited = {e: {} for e in ENGS}
        self.lastw = {}
        self.readers = {}
        self.bank_i = 0
        self.nb = 8

    def _need(self, eng, reads, writes):
        need = {}

        def add(sk, v, war=False):
            if sk == eng and eng == "pe":
                return
            if need.get(sk, 0) < v:
                need[sk] = v

        for k in reads:
            w = self.lastw.get(k)
            if w:
                add(*w)
        for k in writes:
            w = self.lastw.get(k)
            if w:
                add(*w)
            for r in self.readers.get(k, ()):
                add(r[0], r[1], war=True)
        out = []
        for sk, v in need.items():
            if self.waited[eng].get(sk, 0) >= v:
                continue
            self.waited[eng][sk] = v
            out.append((sk, v))
        return out

    def op(self, eng, fn, reads=(), writes=(), slot=None):
        waits = self._need(eng, reads, writes)
        if slot is None:
            sk, inc = eng, 1
        else:
            sk, inc = "dma_" + slot, 16
        self.count[sk] = self.count.get(sk, 0) + inc
        val = self.count[sk]
        self.ops[eng].append((waits, fn, sk, inc))
        for k in reads:
            self.readers.setdefault(k, []).append((sk, val))
        for k in writes:
            self.lastw[k] = (sk, val)
            self.readers[k] = []

    def barrier(self):
        for e in ENGS:
            waits = []
            for sk, v in self.count.items():
                if sk == e and e == "pe":
                    continue
                if self.waited[e].get(sk, 0) >= v:
                    continue
                self.waited[e][sk] = v
                waits.append((sk, v))
            if waits:
                self.ops[e].append((waits, None, None, 0))
        self.lastw = {}
        self.readers = {}

    def bank(self):
        i = self.bank_i
        self.bank_i = (i + 1) % self.nb
        return i


class Arena:
    def __init__(self, nc, nbytes):
        self.t = nc.alloc_sbuf_tensor("arena", [128, nbytes // 2], BF16)
        self.ap = self.t.ap()
        self.off = 0
        self.cap = nbytes
        self.peak = 0

    def at(self, off, shape, dtype):
        save, savep = self.off, self.peak
        self.off = off
        v = self.alloc(shape, dtype)
        self.off, self.peak = save, savep
        return v

    def alloc(self, shape, dtype):
        es = 2 if dtype == BF16 else 4
        n = 1
        for s in shape[1:]:
            n *= s
        nb = (n * es + 63) // 64 * 64
        o = self.off
        self.off += nb
        self.peak = max(self.peak, self.off)
        assert self.off <= self.cap, ("SBUF arena overflow", self.off, self.cap)
        v = self.ap[:, o // 2:(o + n * es) // 2]
        if dtype != BF16:
            v = v.bitcast(dtype)
        if len(shape) == 3:
            v = v.rearrange("p (a b) -> p a b", a=shape[1])
        elif len(shape) == 4:
            v = v.rearrange("p (a b c) -> p a b c", a=shape[1], b=shape[2])
        return v


class _Stop(Exception):
    pass


def build_nc(debug=False, stop_after=99):
    nc = bass.Bass("TRN2", target_bir_lowering=False)
    dt = nc.dram_tensor
    x_d = dt("x", [S, D], F32, kind="ExternalInput").ap()
    cT_d = dt("cT", [128, 8], F32, kind="ExternalInput").ap()
    pos_d = dt("pos", [1, S], I32, kind="ExternalInput").ap()
    wada_d = dt("w_ada", [D, 3 * D], F32, kind="ExternalInput").ap()
    adab_d = dt("adab", [128, 24], F32, kind="ExternalInput").ap()
    normg_d = dt("normg", [128, 8], F32, kind="ExternalInput").ap()
    win_d = dt("w_in", [D, IN_COLS], F32, kind="ExternalInput").ap()
    gqk_d = dt("gqk", [128, 6], F32, kind="ExternalInput").ap()
    lng_d = dt("lng", [1, 512], F32, kind="ExternalInput").ap()
    lnb_d = dt("lnb", [1, 512], F32, kind="ExternalInput").ap()
    wsp_d = dt("wsp", [128, 8, 128], F32, kind="ExternalInput").ap()
    bsp_d = dt("bsp", [128, 4, 128], F32, kind="ExternalInput").ap()
    wba_d = dt("w_ba", [512, D], F32, kind="ExternalInput").ap()
    wbb_d = dt("w_bb", [512, D], F32, kind="ExternalInput").ap()
    wout_d = dt("w_out", [D, D], F32, kind="ExternalInput").ap()
    freq_d = dt("freq", [128, 2], F32, kind="ExternalInput").ap()
    cst_d = dt("cst", [128, 704], F32, kind="ExternalInput").ap()
    out_d = dt("out", [S, D], F32, kind="ExternalOutput").ap()
    dbg = {}
    if debug:
        for nm, shp, ty in (("d_hT", [128, 8 * S], BF16), ("d_V", [128, 3 * 16 * 512], BF16),
                            ("d_qk", [128, 6 * S], BF16), ("d_ya", [128, 4 * S], BF16),
                            ("d_yb", [128, 4 * S], BF16), ("d_mg", [128, 8 * S], BF16),
                            ("d_tab", [128, 2 * S], BF16), ("d_ada", [128, 24], F32)):
            dbg[nm] = dt(nm, shp, ty, kind="ExternalOutput").ap()

    win_v = win_d.rearrange("(kc p) c -> p kc c", p=128)
    wada_v = wada_d.rearrange("(kc p) c -> p kc c", p=128)

    P = Prog()
    A = Arena(nc, 206 * 1024)
    banks = [nc.alloc_psum_tensor("bank%d" % i, [128, 512], F32).ap() for i in range(8)]

    def B(i):
        return "B%d" % i

    hT = A.alloc([128, KC, S], BF16)
    cst = A.alloc([128, 704], BF16)
    ident = cst[:, 0:128]
    bones = cst[:, 128:256]
    perm = cst[:, 256:384]
    mD = cst[:, 384:512]
    mP = cst[:, 512:640]
    ones64 = cst[:, 640:704]
    mDP = A.alloc([128, 4, 128], BF16)
    mDD = A.alloc([128, 4, 128], BF16)
    WsT = A.alloc([128, 8, 128], BF16)
    bsp = A.alloc([128, 4, 128], F32)
    Ctab = A.alloc([128, S], BF16)
    Stab = A.alloc([128, S], BF16)
    small = A.alloc([128, 128], F32)
    cT = small[:, 0:8]
    sc = small[:, 8:16]
    adab = small[:, 16:40]
    ada = small[:, 40:64]
    normg = small[:, 64:72]
    Acol = small[:, 72:80]
    gqk = small[:, 80:86]
    freq = small[:, 86:87]
    freq_lo = small[:, 87:88]
    ssq = small[:, 88:104]
    rstd_x = small[:, 104:120]
    bnst = small[:, 120:126]
    bnag = small[:, 126:128]
    small2 = A.alloc([128, 64], F32)
    srt = small2[:, 0:16]
    lnr = small2[:, 16:18]
    Bcol = ada[:, 0:8]
    gatecol = ada[:, 16:24]
    negpi = small2[:, 18:19]
    epsc = small2[:, 19:20]
    mPD = A.alloc([128, 4, 128], BF16)
    wchunk = [A.alloc([128, KC, 128], BF16) for _ in range(4)]
    y_aT = A.alloc([128, 4, S], BF16)
    TOP8 = A.cap - 8192
    TOP16 = A.cap - 8192 - 16384
    wbig = [A.at(TOP8, [128, KC, 512], BF16), None]
    TWO_PI = 2.0 * math.pi
    SHR = 1.0 - 2e-6
    PI_LO = 3.1415925
    INV2PI_HI = float(np.float32(1.0 / TWO_PI))
    INV2PI_LO = float(np.float32(1.0 / TWO_PI - INV2PI_HI))
    P.op("dve", lambda e: e.memset(negpi, -math.pi * SHR), writes=["negpi"])
    P.op("dve", lambda e: e.memset(epsc, EPS), writes=["epsc"])

    wc_i = [0]
    wb_i = [0]

    def load_chunk(c0):
        s = wc_i[0] % 4
        wc_i[0] += 1
        key = "wc%d" % s
        P.op("pool", lambda e: e.dma_start(out=wchunk[s], in_=win_v[:, :, c0:c0 + 128]),
             writes=[key], slot=key)
        return wchunk[s], key

    def load_big(c0):
        s = wb_i[0] % 2
        wb_i[0] += 1
        key = "wb%d" % s
        dst = wbig[s]
        P.op("pool", lambda e: e.dma_start(out=dst, in_=win_v[:, :, c0:c0 + 512]),
             writes=[key], slot=key)
        return dst, key

    def mm_group(out_ap, pairs, reads, writes):
        def fn(e):
            n = len(pairs)
            ins = None
            for i, (l, r) in enumerate(pairs):
                ins = e.matmul(out_ap, l, r, start=(i == 0), stop=(i == n - 1))
            return ins
        P.op("pe", fn, reads=reads, writes=writes)

    def zT_block(w, wkey, tb):
        b = P.bank()
        mm_group(banks[b], [(w[:, kc, :], hT[:, kc, tb * 512:(tb + 1) * 512]) for kc in range(KC)],
                 reads=[wkey], writes=[B(b)])
        return b

    try:
        P.op("pool", lambda e: e.dma_start(out=cst, in_=cst_d), writes=["cst"], slot="cst")
        P.op("pool", lambda e: e.dma_start(out=WsT, in_=wsp_d), writes=["WsT"], slot="wsp")
        for nm, dst, src in (("cT", cT, cT_d), ("adab", adab, adab_d), ("normg", normg, normg_d),
                             ("gqk", gqk, gqk_d), ("freq", small[:, 86:88], freq_d), ("bsp", bsp, bsp_d)):
            P.op("sp", (lambda d_, s_: (lambda e: e.dma_start(out=d_, in_=s_)))(dst, src),
                 writes=[nm], slot=nm)
        for j in range(4):
            P.op("dve", (lambda j_: (lambda e: e.tensor_copy(out=mDP[:, j_, :], in_=(mD if j_ % 2 == 0 else mP))))(j),
                 reads=["cst"], writes=["mDP%d" % j])
            P.op("dve", (lambda j_: (lambda e: e.tensor_copy(out=mDD[:, j_, :], in_=mD)))(j),
                 reads=["cst"], writes=["mDD%d" % j])
            P.op("dve", (lambda j_: (lambda e: e.tensor_copy(out=mPD[:, j_, :], in_=(mP if j_ % 2 == 0 else mD))))(j),
                 reads=["cst"], writes=["mPD%d" % j])
        for g in range(8):
            P.op("dve", (lambda g_: (lambda e: e.tensor_tensor(out=WsT[:, g_, :], in0=WsT[:, g_, :], in1=mD, op=ALU.mult)))(g),
                 reads=["cst", "WsT"], writes=["WsT"])

        ph0 = A.off
        xn_all = A.alloc([128, NT, D], BF16)
        xs = [A.alloc([128, D], F32) for _ in range(3)]
        scr = A.alloc([128, D], BF16)
        wada_s = [A.alloc([128, KC, 512], F32) for _ in range(2)]
        rowb = A.alloc([128, 3 * D], F32)
        wada_s.append(A.alloc([128, KC, 512], F32))
        one11 = small2[:, 20:21]
        P.op("dve", lambda e: e.memset(one11, 1.0), writes=["one11"])
        P.op("act", lambda e: e.activation(out=sc, in_=cT, func=AF.Silu), reads=["cT"], writes=["sc"])

        def x_load(t):
            s = t % 3
            P.op("pool", (lambda s_, t_: (lambda e: e.dma_start(out=xs[s_], in_=x_d[t_ * 128:(t_ + 1) * 128, :])))(s, t),
                 writes=["xs%d" % s], slot="xs%d" % s)
        for t in range(3):
            x_load(t)
        early_wv0 = []
        for t in range(NT):
            s = t % 3
            P.op("act", (lambda s_, t_: (lambda e: e.activation(out=scr, in_=xs[s_], func=AF.Square,
                                                                 accum_out=ssq[:, t_:t_ + 1])))(s, t),
                 reads=["xs%d" % s], writes=["scr", "ssq%d" % t])
            P.op("act", (lambda t_: (lambda e: e.activation(out=srt[:, t_:t_ + 1], in_=ssq[:, t_:t_ + 1], func=AF.Sqrt,
                                                             bias=epsc, scale=1.0 / D)))(t),
                 reads=["ssq%d" % t, "epsc"], writes=["srt%d" % t])
            P.op("dve", (lambda t_: (lambda e: e.reciprocal(out=rstd_x[:, t_:t_ + 1], in_=srt[:, t_:t_ + 1])))(t),
                 reads=["srt%d" % t], writes=["rstdx%d" % t])
            P.op("act", (lambda s_, t_: (lambda e: e.activation(out=xn_all[:, t_, :], in_=xs[s_], func=AF.Copy,
                                                                 scale=rstd_x[:, t_:t_ + 1])))(s, t),
                 reads=["xs%d" % s, "rstdx%d" % t], writes=["xn%d" % t])
            if t + 3 < NT:
                x_load(t + 3)
            elif not early_wv0:
                early_wv0.append(load_big(0 * 1536 + 1024))

        for j in range(6):
            s = j % 3
            key = "wada%d" % s
            P.op("sp", (lambda s_, j_: (lambda e: e.dma_start(out=wada_s[s_], in_=wada_v[:, :, j_ * 512:(j_ + 1) * 512])))(s, j),
                 writes=[key], slot=key)
            br_ = P.bank()
            mm_group(banks[br_][0:1, :], [(sc[:, kc:kc + 1], wada_s[s][:, kc, :]) for kc in range(KC)],
                     reads=[key, "sc"], writes=[B(br_)])
            P.op("dve", (lambda j_, b_: (lambda e: e.tensor_copy(out=rowb[0:1, j_ * 512:(j_ + 1) * 512], in_=banks[b_][0:1, :])))(j, br_),
                 reads=[B(br_)], writes=["rowb%d" % j])
        b_ada = P.bank()

        def adaT_fn(e):
            ins = None
            for col in range(24):
                ins = e.matmul(banks[b_ada][:, col:col + 1], rowb[0:1, col * 128:(col + 1) * 128], one11[0:1, 0:1],
                               start=True, stop=True)
            return ins
        P.op("pe", adaT_fn, reads=["rowb%d" % j for j in range(6)] + ["one11"], writes=[B(b_ada)])
        P.op("dve", lambda e: e.tensor_tensor(out=ada, in0=banks[b_ada][:, 0:24], in1=adab, op=ALU.add),
             reads=[B(b_ada), "adab"], writes=["ada"])
        P.op("dve", lambda e: e.scalar_tensor_tensor(out=Acol, in0=ada[:, 8:16], scalar=1.0, in1=normg,
                                                     op0=ALU.add, op1=ALU.mult),
             reads=["ada", "normg"], writes=["Acol"])

        for t in range(NT):
            b = P.bank()
            b2 = P.bank()
            psA = banks[b].rearrange("p (k t) -> p k t", k=4)
            psB = banks[b2].rearrange("p (k t) -> p k t", k=4)

            def tr_fn(e, t_=t, psA_=psA, psB_=psB):
                ins = None
                for kc in range(KC):
                    dst_ = (psA_ if kc < 4 else psB_)[:, kc % 4, :]
                    ins = e.matmul(dst_, xn_all[:, t_, kc * 128:(kc + 1) * 128], ident, start=True, stop=True)
                return ins
            P.op("pe", tr_fn, reads=["xn%d" % t, "cst"], writes=[B(b), B(b2)])
            for kc in range(KC):
                dst = hT[:, kc, t * 128:(t + 1) * 128]
                psT = psA if kc < 4 else psB
                bk = b if kc < 4 else b2
                if kc < 4:
                    P.op("dve", (lambda d_, p_, k_: (lambda e: e.tensor_scalar(out=d_, in0=p_, scalar1=Acol[:, k_:k_ + 1],
                                                                              scalar2=Bcol[:, k_:k_ + 1],
                                                                              op0=ALU.mult, op1=ALU.add)))(dst, psT[:, kc % 4, :], kc),
                         reads=[B(bk), "Acol", "ada"], writes=["hT%d_%d" % (t, kc)])
                else:
                    P.op("act", (lambda d_, p_, k_: (lambda e: e.activation(out=d_, in_=p_, func=AF.Identity,
                                                                           bias=Bcol[:, k_:k_ + 1],
                                                                           scale=Acol[:, k_:k_ + 1])))(dst, psT[:, kc % 4, :], kc),
                         reads=[B(bk), "Acol", "ada"], writes=["hT%d_%d" % (t, kc)])
        if debug:
            P.barrier()
            P.op("sp", lambda e: e.dma_start(out=dbg["d_hT"], in_=hT.rearrange("p k t -> p (k t)")), slot="dbg0")
            P.op("sp", lambda e: e.dma_start(out=dbg["d_ada"], in_=ada), slot="dbg3")
        P.barrier()
        A.off = ph0

        if stop_after == 1:
            raise _Stop()
        off_wbig = A.off
        wbig[1] = A.alloc([128, KC, 512], BF16)
        V = [A.alloc([128, 16, 512], BF16) for _ in range(3)]
        off_qk = A.off
        qk = [[A.alloc([128, S], BF16) for _ in range(2)] for _ in range(3)]
        acc_n = A.alloc([128, S], F32)
        acc_d = A.alloc([128, S], F32)
        pT = [[A.alloc([128, 4, 128], BF16) for _ in range(2)] for _ in range(2)]
        bones_g = A.alloc([128, 6, 128], BF16)
        ginv = A.alloc([128, 8], F32)

        def tok_slice(g, blk):
            d = DIL[g]
            nb = 16 // d
            r, n = blk // nb, blk % nb
            st = 128 * n * d + r
            return slice(st, st + 127 * d + 1, d)

        P.op("dve", lambda e: e.reciprocal(out=ginv[:, 0:6], in_=gqk), reads=["gqk"], writes=["ginv"])
        P.op("dve", lambda e: e.tensor_tensor(out=ginv[:, 0:6], in0=ginv[:, 0:6], in1=ginv[:, 0:6], op=ALU.mult),
             reads=["ginv"], writes=["ginv"])
        for j in range(6):
            P.op("dve", (lambda j_: (lambda e: e.tensor_scalar(out=bones_g[:, j_, :], in0=bones, scalar1=ginv[:, j_:j_ + 1],
                                                               scalar2=None, op0=ALU.mult)))(j),
                 reads=["ginv", "cst"], writes=["bones_g%d" % j])

        seq = [(hp, g) for hp in range(4) for g in range(3)]
        worder = []
        for idx, (hp, g) in enumerate(seq):
            if idx == 0:
                worder += [(hp, g, 0), (hp, g, 1)]
            if idx + 1 < len(seq):
                nh, ng = seq[idx + 1]
                worder += [(nh, ng, 0), (nh, ng, 1)]
            if g == 2:
                worder += [(hp, -1, 0)]
        wloaded = {}
        wnext = [0]

        def wcol(item):
            hp_, g_, role_ = item
            if g_ < 0:
                return OFF_GATE_A + hp_ * 128
            return g_ * 1536 + role_ * 512 + hp_ * 128

        def get_w(item):
            k = worder.index(item)
            while wnext[0] <= min(k + 2, len(worder) - 1):
                wloaded[worder[wnext[0]]] = load_chunk(wcol(worder[wnext[0]]))
                wnext[0] += 1
            return wloaded[item]

        get_w(worder[0])
        posi = A.at(off_qk, [128, S], I32)
        ang = A.at(off_qk + 8192, [128, S], F32)
        yy = A.at(off_qk + 16384, [128, S], F32)
        ki = A.at(off_qk + 24576, [128, S], I32)
        kf = A.at(off_qk + 32768, [128, S], F32)
        P.op("sp", lambda e: e.dma_start(out=posi, in_=pos_d.partition_broadcast(128)),
             writes=["posi"], slot="posi")
        tab_ops = []
        tab_ops.append(lambda: P.op("dve", lambda e: e.tensor_copy(out=ang, in_=posi), reads=["posi"], writes=["ang"]))
        tab_ops.append(lambda: P.op("dve", lambda e: e.tensor_scalar(out=kf, in0=ang, scalar1=freq_lo, scalar2=None, op0=ALU.mult),
                                    reads=["ang", "freq"], writes=["kf"]))
        tab_ops.append(lambda: P.op("dve", lambda e: e.scalar_tensor_tensor(out=ang, in0=ang, scalar=freq, in1=kf,
                                                                            op0=ALU.mult, op1=ALU.add),
                                    reads=["ang", "kf", "freq"], writes=["ang"]))
        for tab, offs in ((Stab, 0.5), (Ctab, 0.75)):
            tab_ops.append((lambda o_: (lambda: P.op("dve", lambda e: e.tensor_scalar(out=yy, in0=ang, scalar1=INV2PI_HI, scalar2=o_,
                                                                                      op0=ALU.mult, op1=ALU.add),
                                                     reads=["ang"], writes=["yy"])))(offs))
            tab_ops.append(lambda: P.op("dve", lambda e: e.scalar_tensor_tensor(out=yy, in0=ang, scalar=INV2PI_LO, in1=yy,
                                                                                op0=ALU.mult, op1=ALU.add),
                                        reads=["ang", "yy"], writes=["yy"]))
            tab_ops.append(lambda: P.op("dve", lambda e: e.tensor_copy(out=ki, in_=yy), reads=["yy"], writes=["ki"]))
            tab_ops.append(lambda: P.op("dve", lambda e: e.tensor_copy(out=kf, in_=ki), reads=["ki"], writes=["kf"]))
            tab_ops.append(lambda: P.op("dve", lambda e: e.tensor_tensor(out=yy, in0=yy, in1=kf, op=ALU.subtract),
                                        reads=["yy", "kf"], writes=["yy"]))
            tab_ops.append(lambda: P.op("dve", lambda e: e.tensor_single_scalar(out=kf, in_=yy, scalar=0.0, op=ALU.is_lt),
                                        reads=["yy"], writes=["kf"]))
            tab_ops.append(lambda: P.op("dve", lambda e: e.tensor_tensor(out=yy, in0=yy, in1=kf, op=ALU.add),
                                        reads=["yy", "kf"], writes=["yy"]))
            tab_ops.append(lambda: P.op("dve", lambda e: e.tensor_scalar(out=yy, in0=yy, scalar1=TWO_PI, scalar2=-math.pi,
                                                                          op0=ALU.mult, op1=ALU.add),
                                        reads=["yy"], writes=["yy"]))
            tab_ops.append(lambda: P.op("dve", lambda e: e.tensor_scalar(out=yy, in0=yy, scalar1=-PI_LO, scalar2=PI_LO,
                                                                          op0=ALU.max, op1=ALU.min),
                                        reads=["yy"], writes=["yy"]))
            tab_ops.append((lambda t_: (lambda: P.op("act", lambda e: e.activation(out=t_, in_=yy, func=AF.Sin),
                                                     reads=["yy"], writes=["tab"])))(tab))

        for g in range(3):
            if g == 0:
                w, wkey = early_wv0[0]
            else:
                w, wkey = load_big(g * 1536 + 1024)
            for blk in range(16):
                sl = tok_slice(g, blk)
                b = P.bank()
                mm_group(banks[b], [(hT[:, kc, sl], w[:, kc, :]) for kc in range(KC)],
                         reads=[wkey], writes=[B(b)])
                if True:
                    P.op("act", (lambda g_, k_, b_: (lambda e: e.copy(out=V[g_][:, k_, :], in_=banks[b_])))(g, blk, b),
                         reads=[B(b)], writes=["V%d_%d" % (g, blk)])
                else:
                    P.op("dve", (lambda g_, k_, b_: (lambda e: e.tensor_copy(out=V[g_][:, k_, :], in_=banks[b_])))(g, blk, b),
                         reads=[B(b)], writes=["V%d_%d" % (g, blk)])
                if tab_ops:
                    tab_ops.pop(0)()
        while tab_ops:
            tab_ops.pop(0)()
        P.barrier()
        NBUF = 3
        save_off = A.off
        A.off = off_wbig
        def talloc(nbytes, dtype):
            if A.off < save_off and A.off + nbytes > off_wbig + 8192:
                A.off = save_off
            return A.alloc([128, 512], dtype)
        tmps = []
        for i in range(NBUF):
            tset = dict(zg=talloc(1024, BF16), sq=talloc(1024, BF16), zc=talloc(1024, BF16), zs=talloc(1024, BF16),
                        ln=talloc(2048, F32))
            tset["rs"] = tset["ln"]
            tmps.append(tset)
        ftmp = dict(t1=talloc(2048, F32), t2=talloc(2048, F32))
        if A.off < save_off:
            A.off = save_off
        assert A.off <= TOP8, ("phase 2 overlaps top slot", A.off, TOP8)
        wv = wbig[0]
        P.op("pool", lambda e: e.dma_start(out=wv, in_=win_v[:, :, OFF_Z_B + 512:OFF_Z_B + 1024]), writes=["wv"], slot="wb0")
        P.nb = 6
        P.bank_i = 0
        BO, BD = 6, 7

        cb_i = [0]

        def chunk_block(hp, g, role, tb):
            st = {}

            def stageA():
                w, wkey = get_w((hp, g, role))
                i = cb_i[0] % NBUF
                cb_i[0] += 1
                T = tmps[i]
                sf = "_%d" % i
                gcol = gqk[:, g * 2 + role:g * 2 + role + 1]
                bz = zT_block(w, wkey, tb)
                P.op("act", lambda e: e.activation(out=T["zg"], in_=banks[bz], func=AF.Copy, scale=gcol),
                     reads=[B(bz), "gqk"], writes=["zg" + sf])
                P.op("dve", lambda e: e.tensor_tensor(out=T["sq"], in0=T["zg"], in1=T["zg"], op=ALU.mult),
                     reads=["zg" + sf], writes=["sq" + sf])
                ts_ = slice(tb * 512, (tb + 1) * 512)
                P.op("dve", lambda e: e.tensor_tensor(out=T["zc"], in0=T["zg"], in1=Ctab[:, ts_], op=ALU.mult),
                     reads=["zg" + sf], writes=["zc" + sf])
                P.op("dve", lambda e: e.tensor_tensor(out=T["zs"], in0=T["zg"], in1=Stab[:, ts_], op=ALU.mult),
                     reads=["zg" + sf], writes=["zs" + sf])
                st["T"], st["sf"] = T, sf

            def stageB():
                T, sf = st["T"], st["sf"]
                dstT = qk[g][role]
                ts_ = slice(tb * 512, (tb + 1) * 512)
                bs = P.bank()
                br = P.bank()

                def sr_fn(e):
                    e.matmul(banks[bs], bones_g[:, g * 2 + role, :], T["sq"], start=True, stop=True)
                    e.matmul(banks[br], ident, T["zc"], start=True, stop=False)
                    return e.matmul(banks[br], perm, T["zs"], start=False, stop=True)
                P.op("pe", sr_fn, reads=["sq" + sf, "zc" + sf, "zs" + sf, "bones_g%d" % (g * 2 + role), "cst"], writes=[B(bs), B(br)])
                P.op("act", lambda e: e.activation(out=T["ln"], in_=banks[bs], func=AF.Ln, bias=epsc, scale=1.0),
                     reads=[B(bs), "epsc"], writes=["ln" + sf])
                P.op("act", lambda e: e.activation(out=T["rs"], in_=T["ln"], func=AF.Exp, scale=-0.5),
                     reads=["ln" + sf], writes=["ln" + sf])
                P.op("dve", lambda e: e.tensor_tensor(out=dstT[:, ts_], in0=banks[br], in1=T["rs"], op=ALU.mult),
                     reads=[B(br), "ln" + sf], writes=["qk%d%d_%d" % (g, role, tb)])
            return stageA, stageB

        def chunk_steps(hp, g):
            return [chunk_block(hp, g, ro, tb) for tb in range(4) for ro in range(2)]

        unit_ctr = [0]

        def att_units(hp, g):
            d = DIL[g]
            nb = 16 // d
            qT, kT = qk[g][0], qk[g][1]
            qkeys = ["qk%d%d_%d" % (g, ro, tb) for ro in range(2) for tb in range(4)]
            if g == 2:
                qgroups = [[r for r in range(4 * i, 4 * i + 4)] for i in range(4)]
            else:
                qgroups = [[r * nb + n for n in range(4 * i, 4 * i + 4)] for r in range(d) for i in range(nb // 4)]
            out = []
            for qg in qgroups:
                subs = []
                for slot_, blk in enumerate(qg):
                    n = blk % nb
                    if g != 2 and n > 0:
                        subs.append((blk - 1, blk, slot_, True, False, 1))
                        subs.append((blk, blk, slot_, False, True, 0))
                    else:
                        subs.append((blk, blk, slot_, True, True, 0))
                units = [subs[i:i + 4] for i in range(0, len(subs), 4)]
                for ui, un in enumerate(units):
                    par = unit_ctr[0] % 2
                    unit_ctr[0] += 1
                    last_of_group = (ui == len(units) - 1)

                    def S_stage(un=un, par=par):
                        types = [u_[5] for u_ in un]
                        nj = len(un)
                        bS = [P.bank(), P.bank()]
                        pss = [banks[bS[e_]].rearrange("p (j q) -> p j q", j=4) for e_ in range(2)]

                        def s_fn(e):
                            ins = None
                            for j, (kb, qb, _s, _f, _l, _t) in enumerate(un):
                                for e_ in range(2):
                                    rows = slice(64 * e_, 64 * e_ + 64)
                                    ins = e.matmul(pss[e_][:, j, :], kT[rows, tok_slice(g, kb)], qT[rows, tok_slice(g, qb)],
                                                   start=True, stop=True)
                            return ins
                        if g == 2:
                            rk = qkeys
                        else:
                            tbs_k = set((kb // 4) if g == 0 else (kb % nb) for (kb, qb, _s, _f, _l, _t) in un)
                            tbs_q = set((qb // 4) if g == 0 else (qb % nb) for (kb, qb, _s, _f, _l, _t) in un)
                            rk = ["qk%d1_%d" % (g, t_) for t_ in tbs_k] + ["qk%d0_%d" % (g, t_) for t_ in tbs_q]
                        P.op("pe", s_fn, reads=rk, writes=[B(bS[0]), B(bS[1])])
                        if all(t_ == 0 for t_ in types):
                            mk = mDD
                        elif all(types[j] == (j % 2) for j in range(nj)):
                            mk = mDP
                        elif all(types[j] == ((j + 1) % 2) for j in range(nj)):
                            mk = mPD
                        else:
                            raise AssertionError("mask pattern")
                        for e_ in range(2):
                            pt = pT[par][e_]
                            pkey = "pT%d%d" % (par, e_)
                            P.op("act", (lambda ps=pss[e_], pt=pt: (lambda e: e.activation(out=pt[:, 0:nj, :], in_=ps[:, 0:nj, :],
                                                                                           func=AF.Exp, scale=0.125)))(),
                                 reads=[B(bS[e_])], writes=[pkey])
                            P.op("dve",
                                 (lambda pt=pt, mk=mk: (lambda e: e.tensor_tensor(out=pt[:, 0:nj, :], in0=pt[:, 0:nj, :],
                                                                                  in1=mk[:, 0:nj, :], op=ALU.mult)))(),
                                 reads=[pkey], writes=[pkey])

                    def PV_stage(un=un, par=par):
                        po = banks[BO].rearrange("p (j q) -> p j q", j=4)
                        pd = banks[BD].rearrange("p (j q) -> p j q", j=4)

                        def pv_fn(e):
                            ins = None
                            for j, (kb, qb, sl_, f_, l_, _t) in enumerate(un):
                                for e_ in range(2):
                                    orow = slice(64 * e_, 64 * e_ + 64)
                                    hcol = slice((hp * 2 + e_) * 64, (hp * 2 + e_) * 64 + 64)
                                    e.matmul(po[orow, sl_, :], V[g][:, kb, hcol], pT[par][e_][:, j, :], start=f_, stop=l_)
                                for e_ in range(2):
                                    orow = slice(64 * e_, 64 * e_ + 64)
                                    ins = e.matmul(pd[orow, sl_, :], ones64, pT[par][e_][:, j, :], start=f_, stop=l_)
                            return ins
                        P.op("pe", pv_fn, reads=["pT%d0" % par, "pT%d1" % par, "cst"], writes=[B(BO), B(BD)])

                    def post(qg=qg):
                        if g == 0:
                            t0 = (qg[0] % nb) * 128
                            dn = acc_n[:, t0:t0 + 512]
                            dd = acc_d[:, t0:t0 + 512]
                            P.op("act", lambda e: e.copy(out=dn, in_=banks[BO]), reads=[B(BO)], writes=["accn"])
                            P.op("dve", lambda e: e.tensor_copy(out=dd, in_=banks[BD]), reads=[B(BD)], writes=["accd"])
                        else:
                            if g == 1:
                                r = qg[0] // nb
                                dn = acc_n[:, r:S:4]
                                dd = acc_d[:, r:S:4]
                                sn = banks[BO]
                                sd = banks[BD]
                            else:
                                r0 = qg[0]
                                dn = acc_n.rearrange("p (i r) -> p r i", r=16)[:, r0:r0 + 4, :]
                                dd = acc_d.rearrange("p (i r) -> p r i", r=16)[:, r0:r0 + 4, :]
                                sn = banks[BO].rearrange("p (j q) -> p j q", j=4)
                                sd = banks[BD].rearrange("p (j q) -> p j q", j=4)
                            P.op("dve", lambda e: e.tensor_tensor(out=dn, in0=sn, in1=dn, op=ALU.add),
                                 reads=[B(BO), "accn"], writes=["accn"])
                            P.op("dve", lambda e: e.tensor_tensor(out=dd, in0=sd, in1=dd, op=ALU.add),
                                 reads=[B(BD), "accd"], writes=["accd"])
                    out.append((S_stage, PV_stage, post if last_of_group else None))
            return out

        def finalize(hp):
            w, wkey = get_w((hp, -1, 0))
            for tb in range(4):
                i = cb_i[0] % NBUF
                cb_i[0] += 1
                T = dict(tmps[i])
                T.update(ftmp)
                sf = "_%d" % i
                ts_ = slice(tb * 512, (tb + 1) * 512)
                bz = zT_block(w, wkey, tb)
                P.op("act", (lambda T=T, bz=bz: (lambda e: e.activation(out=T["t2"], in_=banks[bz], func=AF.Exp, scale=-1.0)))(),
                     reads=[B(bz)], writes=["ft2"])
                P.op("dve", (lambda T=T, ts_=ts_: (lambda e: e.scalar_tensor_tensor(out=T["t1"], in0=T["t2"], scalar=1.0, in1=acc_d[:, ts_],
                                                                                    op0=ALU.add, op1=ALU.mult)))(),
                     reads=["ft2", "accd"], writes=["ft1"])
                P.op("act", (lambda T=T: (lambda e: e.activation(out=T["ln"], in_=T["t1"], func=AF.Ln)))(),
                     reads=["ft1"], writes=["ln" + sf])
                P.op("act", (lambda T=T: (lambda e: e.activation(out=T["rs"], in_=T["ln"], func=AF.Exp, scale=-1.0)))(),
                     reads=["ln" + sf], writes=["ln" + sf])
                P.op("dve", (lambda T=T, ts_=ts_: (lambda e: e.tensor_tensor(out=T["t1"], in0=acc_n[:, ts_], in1=T["rs"], op=ALU.mult)))(),
                     reads=["accn", "ln" + sf, "ft1"], writes=["ft1"])
                P.op("dve", (lambda T=T, ts_=ts_, bz=bz: (lambda e: e.tensor_tensor(out=y_aT[:, hp, ts_], in0=banks[bz], in1=T["t1"], op=ALU.mult)))(),
                     reads=[B(bz), "ft1"], writes=["ya%d_%d" % (hp, tb)])

        pendB = [None]

        def emit_chunk(ab):
            ab[0]()
            if pendB[0] is not None:
                pendB[0]()
            pendB[0] = ab[1]

        def flushB():
            if pendB[0] is not None:
                pendB[0]()
                pendB[0] = None

        for ab in chunk_steps(*seq[0]):
            emit_chunk(ab)
        flushB()
        for idx, (hp, g) in enumerate(seq):
            units = att_units(hp, g)
            nxt = chunk_steps(*seq[idx + 1]) if idx + 1 < len(seq) else []
            ci = 0
            units[0][0]()
            for u in range(len(units)):
                if u + 1 < len(units):
                    units[u + 1][0]()
                k = -(-(u + 1) * len(nxt) // len(units)) - ci
                for _ in range(k):
                    emit_chunk(nxt[ci])
                    ci += 1
                units[u][1]()
                if units[u][2] is not None:
                    units[u][2]()
            flushB()
            if g == 2:
                finalize(hp)
            if debug and hp == 0 and g == 2:
                P.barrier()
                for g2 in range(3):
                    P.op("sp", (lambda g_: (lambda e: e.dma_start(out=dbg["d_V"][:, g_ * 8192:(g_ + 1) * 8192],
                                                                  in_=V[g_].rearrange("p a b -> p (a b)"))))(g2), slot="dbgV%d" % g2)
                P.barrier()

        P.barrier()
        P.nb = 8
        P.bank_i = 0
        A.off = ph0

        if stop_after == 2:
            raise _Stop()
        y_bT = A.alloc([128, 4, S], BF16)
        wug = A.alloc([128, 8, KC, 128], BF16)
        sv_all = A.alloc([128, 4, S], F32)
        lng = A.alloc([128, 512], F32)
        lnb = A.alloc([128, 512], F32)
        vg1 = A.alloc([128, 4, 512], F32)
        vg = [vg1, vg1]
        vn_t = [A.alloc([128, 512], F32) for _ in range(2)]
        vl_t = [A.alloc([128, 512], BF16) for _ in range(4)]
        gu_t = [A.alloc([128, 512], F32) for _ in range(2)]
        gs_t = [A.alloc([128, 512], F32) for _ in range(2)]
        st3 = A.alloc([128, 64], F32)
        bst = [st3[:, 0:24].rearrange("p (j k) -> p j k", j=4), st3[:, 24:48].rearrange("p (j k) -> p j k", j=4)]
        mv = [st3[:, 48:56].rearrange("p (j k) -> p j k", j=4), st3[:, 56:64].rearrange("p (j k) -> p j k", j=4)]
        st3b = A.alloc([128, 16], F32)
        rsd = [st3b[:, 0:4], st3b[:, 4:8]]
        assert A.off <= TOP16, ("phase 3 overlaps top16", A.off, TOP16)
        for j in (0, 4, 1, 5, 2, 6, 3, 7):
            c0 = OFF_Z_B + j * 128 if j < 4 else OFF_GATE_B + (j - 4) * 128
            P.op("pool", (lambda j_, c_: (lambda e: e.dma_start(out=wug[:, j_], in_=win_v[:, :, c_:c_ + 128])))(j, c0),
                 writes=["wug%d" % j], slot="wug%d" % j)
        wba = A.at(TOP16, [128, 4, D], BF16)
        wbb = A.at(TOP16 + 8192, [128, 4, D], BF16)
        P.op("pool", lambda e: e.dma_start(out=wba, in_=wba_d.rearrange("(kc p) c -> p kc c", p=128)), writes=["wba"], slot="wba")
        P.op("pool", lambda e: e.dma_start(out=wbb, in_=wbb_d.rearrange("(kc p) c -> p kc c", p=128)), writes=["wbb"], slot="wbb")
        P.op("sp", lambda e: e.dma_start(out=lng, in_=lng_d.partition_broadcast(128)), writes=["lng"], slot="lng")
        P.op("sp", lambda e: e.dma_start(out=lnb, in_=lnb_d.partition_broadcast(128)), writes=["lnb"], slot="lnb")

        gate_bc = A.at(TOP8, [128, D], F32)
        gate_bf = A.at(TOP8 + 4096, [128, KC, 128], BF16)
        onesT = A.at(TOP8 + 6144, [128, 128], BF16)
        gres = A.at(TOP8 + 6400, [128, KC], F32)
        ghi = A.at(TOP8 + 6464, [128, KC], BF16)

        def gate_steps():
            bgt = [P.bank(), P.bank()]
            steps = []

            def prep0():
                P.op("dve", lambda e: e.memset(onesT, 1.0), writes=["onesT", "wv"])
                P.op("dve", lambda e: e.tensor_copy(out=ghi, in_=gatecol), writes=["ghi"])
                P.op("dve", lambda e: e.tensor_tensor(out=gres, in0=gatecol, in1=ghi, op=ALU.subtract), reads=["ghi"], writes=["gres"])
            steps.append(prep0)
            for half, (src, key) in enumerate(((ghi, "ghi"), (gres, "gres"))):
                def prep(src=src, key=key):
                    for kc in range(KC):
                        P.op("dve", (lambda k_: (lambda e: e.tensor_scalar(out=gate_bf[:, k_, :], in0=ident, scalar1=src[:, k_:k_ + 1],
                                                                           scalar2=None, op0=ALU.mult)))(kc),
                             reads=[key], writes=["gbf%d" % kc])
                steps.append(prep)

                def pe_ev(half=half):
                    for hf in range(2):
                        bb_ = bgt[hf]

                        def g_fn(e, hf_=hf, bb__=bb_):
                            ins = None
                            for j in range(4):
                                kc = hf_ * 4 + j
                                ins = e.matmul(banks[bb__][:, j * 128:(j + 1) * 128], onesT, gate_bf[:, kc, :], start=True, stop=True)
                            return ins
                        P.op("pe", g_fn, reads=["onesT"] + ["gbf%d" % k for k in range(KC)], writes=[B(bb_)])
                        dst = gate_bc[:, hf * 512:(hf + 1) * 512]
                        if half == 0:
                            P.op("dve", (lambda d_, b_: (lambda e: e.tensor_copy(out=d_, in_=banks[b_])))(dst, bb_),
                                 reads=[B(bb_)], writes=["gbc%d" % hf])
                        else:
                            P.op("dve", (lambda d_, b_: (lambda e: e.tensor_tensor(out=d_, in0=banks[b_], in1=d_, op=ALU.add)))(dst, bb_),
                                 reads=[B(bb_), "gbc%d" % hf], writes=["gbc%d" % hf])
                steps.append(pe_ev)
            return steps

        mw = {0: (load_chunk(OFF_MERGE_A), load_chunk(OFF_MERGE_B)),
              1: (load_chunk(OFF_MERGE_A + 128), load_chunk(OFF_MERGE_B + 128))}
        def V1(t):
            tbp = (t // 4) % 2
            j = t % 4
            b = P.bank()
            mm_group(banks[b], [(hT[:, kc, t * 128:(t + 1) * 128], wv[:, kc, :]) for kc in range(KC)],
                     reads=["wv"], writes=[B(b)])
            P.op("act", lambda e: e.activation(out=vg[tbp][:, j, :], in_=banks[b], func=AF.Gelu),
                 reads=[B(b)], writes=["vg_%d" % j])
            P.op("dve", lambda e: e.bn_stats(out=bst[tbp][:, j, :], in_=vg[tbp][:, j, :]),
                 reads=["vg_%d" % j], writes=["bst%d_%d" % (tbp, j)])
            P.op("dve", lambda e: e.bn_aggr(out=mv[tbp][:, j, :], in_=bst[tbp][:, j, :]),
                 reads=["bst%d_%d" % (tbp, j)], writes=["mv%d_%d" % (tbp, j)])

        def V2(tb):
            tbp = tb % 2
            P.op("act", lambda e: e.activation(out=rsd[tbp], in_=mv[tbp][:, :, 1], func=AF.Sqrt, bias=epsc, scale=1.0),
                 reads=["mv%d_%d" % (tbp, j) for j in range(4)] + ["epsc"], writes=["rsd%d" % tbp])
            P.op("dve", lambda e: e.reciprocal(out=rsd[tbp], in_=rsd[tbp]), reads=["rsd%d" % tbp], writes=["rsd%d" % tbp])

        def V3a(t):
            tbp = (t // 4) % 2
            j = t % 4
            s = t % 2
            P.op("dve", lambda e: e.tensor_scalar(out=vn_t[s], in0=vg[tbp][:, j, :], scalar1=mv[tbp][:, j, 0:1],
                                                  scalar2=rsd[tbp][:, j:j + 1], op0=ALU.subtract, op1=ALU.mult),
                 reads=["vg_%d" % j, "mv%d_%d" % (tbp, j), "rsd%d" % tbp], writes=["vn%d" % s])
            P.op("dve", lambda e: e.tensor_tensor(out=vn_t[s], in0=vn_t[s], in1=lng, op=ALU.mult),
                 reads=["vn%d" % s, "lng"], writes=["vn%d" % s])
            P.op("pool", lambda e: e.tensor_tensor(out=vl_t[j], in0=vn_t[s], in1=lnb, op=ALU.add),
                 reads=["vn%d" % s, "lnb"], writes=["vl%d" % j])

        def V3b(t):
            j = t % 4
            bsv = P.bank()
            psv = banks[bsv].rearrange("p (c q) -> p c q", c=4)

            def sv_fn(e):
                ins = None
                for gg in range(8):
                    ins = e.matmul(psv[64 * (gg % 2):64 * (gg % 2) + 64, gg // 2, :], vl_t[j][:, gg * 64:(gg + 1) * 64],
                                   WsT[:, gg, :], start=True, stop=True)
                return ins
            P.op("pe", sv_fn, reads=["vl%d" % j, "WsT"], writes=[B(bsv)])
            P.op("dve", lambda e: e.tensor_tensor(out=sv_all[:, :, t * 128:(t + 1) * 128], in0=psv, in1=bsp, op=ALU.add),
                 reads=[B(bsv), "bsp"], writes=["sv%d" % t])

        ug_i = [0]

        def UG(c, tb):
            i = ug_i[0] % 2
            ug_i[0] += 1
            ts_ = slice(tb * 512, (tb + 1) * 512)
            bu = P.bank()
            mm_group(banks[bu], [(wug[:, c, kc, :], hT[:, kc, ts_]) for kc in range(KC)], reads=["wug%d" % c], writes=[B(bu)])
            bg = P.bank()
            mm_group(banks[bg], [(wug[:, 4 + c, kc, :], hT[:, kc, ts_]) for kc in range(KC)], reads=["wug%d" % (4 + c)], writes=[B(bg)])
            P.op("act", lambda e: e.activation(out=gu_t[i], in_=banks[bu], func=AF.Gelu), reads=[B(bu)], writes=["gu%d" % i])

            def a2():
                P.op("act", lambda e: e.activation(out=gs_t[i], in_=banks[bg], func=AF.Silu), reads=[B(bg)], writes=["gs%d" % i])
                P.op("dve", lambda e: e.tensor_tensor(out=gu_t[i], in0=gu_t[i], in1=gs_t[i], op=ALU.mult),
                     reads=["gu%d" % i, "gs%d" % i], writes=["gu%d" % i])

            def b_():
                P.op("pool", lambda e: e.tensor_tensor(out=y_bT[:, c, ts_], in0=gu_t[i], in1=sv_all[:, c, ts_], op=ALU.mult),
                     reads=["gu%d" % i] + ["sv%d" % t for t in range(4 * tb, 4 * tb + 4)], writes=["yb%d_%d" % (c, tb)])
            return a2, b_

        for t in range(4):
            V1(t)
        V2(0)
        for t in range(4):
            V3a(t)
        for tb in range(4):
            pend = []
            if tb == 3:
                gsteps = gate_steps()
                gsteps[0]()
                gsteps[1]()
            for c in range(4):
                a2, b_ = UG(c, tb)
                if tb + 1 < 4:
                    V1(4 * (tb + 1) + c)
                a2()
                if c < 2:
                    pend.append(b_)
                    if c == 1:
                        for t in range(4 * tb, 4 * tb + 4):
                            V3b(t)
                        for f_ in pend:
                            f_()
                else:
                    b_()
                if tb == 3 and c == 1:
                    gsteps[2]()
                if tb == 3 and c == 2:
                    gsteps[3]()
                if tb == 3 and c == 3:
                    gsteps[4]()
            if tb + 1 < 4:
                V2(tb + 1)
                for t in range(4 * (tb + 1), 4 * (tb + 1) + 4):
                    V3a(t)
        if debug:
            P.barrier()
            P.op("sp", lambda e: e.dma_start(out=dbg["d_ya"], in_=y_aT.rearrange("p k t -> p (k t)")), slot="dbg4")
            P.op("sp", lambda e: e.dma_start(out=dbg["d_yb"], in_=y_bT.rearrange("p k t -> p (k t)")), slot="dbg5")
        P.barrier()
        A.off = ph0

        if stop_after == 3:
            raise _Stop()
        y_bT = A.alloc([128, 4, S], BF16)
        wout = A.alloc([128, KC, D], BF16)
        mg = A.alloc([128, KC, S], BF16)
        xs2 = [A.alloc([128, D], F32) for _ in range(2)]
        ot = [A.alloc([128, D], F32) for _ in range(2)]
        sga = A.alloc([128, 512], F32)
        sgb = A.alloc([128, 512], F32)
        m1 = A.alloc([128, 512], F32)
        m2 = A.alloc([128, 512], F32)
        for h in range(2):
            P.op("pool", (lambda h_: (lambda e: e.dma_start(out=wout[:, :, h_ * 512:(h_ + 1) * 512],
                                                            in_=wout_d.rearrange("(kc p) c -> p kc c", p=128)[:, :, h_ * 512:(h_ + 1) * 512])))(h),
                 writes=["wout%d" % h], slot="wout%d" % h)
        mtmp = [dict(sga=sga, sgb=sgb, m1=m1, m2=m2),
                dict(sga=A.alloc([128, 512], F32), sgb=A.alloc([128, 512], F32),
                     m1=A.alloc([128, 512], F32), m2=A.alloc([128, 512], F32))]
        assert A.off <= TOP16, ("phase 4 overlaps top16", A.off, TOP16)
        it4 = 0
        for fc in range(KC):
            if fc + 1 < KC and (fc + 1) not in mw:
                mw[fc + 1] = (load_chunk(OFF_MERGE_A + (fc + 1) * 128), load_chunk(OFF_MERGE_B + (fc + 1) * 128))
            (wma, wmakey), (wmb, wmbkey) = mw[fc]
            for tb in range(4):
                ts_ = slice(tb * 512, (tb + 1) * 512)
                M = mtmp[it4 % 2]
                sf = "_%d" % (it4 % 2)
                it4 += 1
                bga = zT_block(wma, wmakey, tb)
                bgb = zT_block(wmb, wmbkey, tb)
                bpa = P.bank()
                mm_group(banks[bpa], [(wba[:, kc, fc * 128:(fc + 1) * 128], y_aT[:, kc, ts_]) for kc in range(4)],
                         reads=["wba"], writes=[B(bpa)])
                bpb = P.bank()
                mm_group(banks[bpb], [(wbb[:, kc, fc * 128:(fc + 1) * 128], y_bT[:, kc, ts_]) for kc in range(4)],
                         reads=["wbb"], writes=[B(bpb)])
                P.op("act", (lambda b_, M=M: (lambda e: e.activation(out=M["sga"], in_=banks[b_], func=AF.Sigmoid)))(bga),
                     reads=[B(bga)], writes=["sga" + sf])
                P.op("act", (lambda b_, M=M: (lambda e: e.activation(out=M["sgb"], in_=banks[b_], func=AF.Sigmoid)))(bgb),
                     reads=[B(bgb)], writes=["sgb" + sf])
                P.op("dve", (lambda b_, M=M: (lambda e: e.tensor_tensor(out=M["m1"], in0=banks[b_], in1=M["sga"], op=ALU.mult)))(bpa),
                     reads=[B(bpa), "sga" + sf], writes=["m1" + sf])
                P.op("dve", (lambda b_, M=M: (lambda e: e.tensor_tensor(out=M["m2"], in0=banks[b_], in1=M["sgb"], op=ALU.mult)))(bpb),
                     reads=[B(bpb), "sgb" + sf], writes=["m2" + sf])
                P.op("pool", (lambda f_, s_, M=M: (lambda e: e.tensor_tensor(out=mg[:, f_, s_], in0=M["m1"], in1=M["m2"], op=ALU.add)))(fc, ts_),
                     reads=["m1" + sf, "m2" + sf], writes=["mg%d_%d" % (fc, tb)])
        if debug:
            P.barrier()
            P.op("sp", lambda e: e.dma_start(out=dbg["d_mg"], in_=mg.rearrange("p k t -> p (k t)")), slot="dbg6")
            P.barrier()
        xs2 = xs2 + [A.at(TOP16, [128, D], F32), A.at(TOP16 + 8192, [128, D], F32)]
        ot = ot + [A.at(TOP16 + 4096, [128, D], F32), A.at(TOP16 + 12288, [128, D], F32)]
        alias = {2: "wba", 3: "wbb"}
        first_x = set()
        first_o = set()

        def x2_load(t):
            s = t % 4
            wr = ["xs2_%d" % s]
            if s in alias and s not in first_x:
                first_x.add(s)
                wr.append(alias[s])
            P.op("sp", (lambda s_, t_: (lambda e: e.dma_start(out=xs2[s_], in_=x_d[t_ * 128:(t_ + 1) * 128, :])))(s, t),
                 writes=wr, slot="xs2_%d" % s)
        for t in range(4):
            x2_load(t)
        for t in range(NT):
            s = t % 4
            for hf in range(2):
                b = P.bank()
                mm_group(banks[b], [(mg[:, fc, t * 128:(t + 1) * 128], wout[:, fc, hf * 512:(hf + 1) * 512]) for fc in range(KC)],
                         reads=["wout%d" % hf] + ["mg%d_%d" % (fc, t // 4) for fc in range(KC)], writes=[B(b)])
                hs = slice(hf * 512, (hf + 1) * 512)
                wr = ["ot%d_%d" % (s, hf)]
                if s in alias and (s, hf) not in first_o:
                    first_o.add((s, hf))
                    wr.append(alias[s])
                P.op("dve", (lambda s_, b_, h_: (lambda e: e.tensor_tensor(out=ot[s_][:, h_], in0=banks[b_], in1=gate_bc[:, h_], op=ALU.mult)))(s, b, hs),
                     reads=[B(b), "gbc%d" % hf], writes=wr)
                P.op("pool", (lambda s_, h_: (lambda e: e.tensor_tensor(out=ot[s_][:, h_], in0=ot[s_][:, h_], in1=xs2[s_][:, h_], op=ALU.add)))(s, hs),
                     reads=["ot%d_%d" % (s, hf), "xs2_%d" % s], writes=["ot%d_%d" % (s, hf)])
            P.op("act", (lambda s_, t_: (lambda e: e.dma_start(out=out_d[t_ * 128:(t_ + 1) * 128, :], in_=ot[s_])))(s, t),
                 reads=["ot%d_0" % s, "ot%d_1" % s], writes=["osb%d" % s], slot="osb%d" % s)
            if t + 4 < NT:
                x2_load(t + 4)
        P.barrier()

    except _Stop:
        P.barrier()

    with ExitStack() as es:
        sems = {}
        for sk in P.count:
            sems[sk] = es.enter_context(nc.semaphore("s_" + sk))
        block = es.enter_context(nc.Block())

        def replay(engname, e):
            for (waits, fn, sk, inc) in P.ops[engname]:
                for (wk, wv) in waits:
                    e.wait_ge(sems[wk], wv)
                if fn is None:
                    continue
                ins = fn(e)
                ins.then_inc(sems[sk], inc)

        @block.tensor
        def _(e):
            replay("pe", e)

        @block.scalar
        def _(e):
            replay("act", e)

        @block.vector
        def _(e):
            replay("dve", e)

        @block.gpsimd
        def _(e):
            replay("pool", e)

        @block.sync
        def _(e):
            replay("sp", e)
    return nc


def _consts():
    c = np.zeros((128, 704), np.float32)
    p = np.arange(128)
    c[p, p] = 1.0
    c[:, 128:256] = ((p[:, None] // 64) == (p[None, :] // 64)).astype(np.float32) / 64.0
    perm = np.zeros((128, 128), np.float32)
    for h in range(2):
        for d in range(8):
            perm[64 * h + d + 8, 64 * h + d] = -1.0
            perm[64 * h + d, 64 * h + d + 8] = 1.0
    c[:, 256:384] = perm
    c[:, 384:512] = (p[None, :] >= p[:, None]).astype(np.float32)
    c[:, 512:640] = (p[:, None] >= p[None, :]).astype(np.float32)
    c[:, 640:704] = 1.0
    freq = np.zeros((128, 2), np.float32)
    fr64 = ROPE_THETA ** (-np.arange(0, 16, 2, dtype=np.float64) / 16.0)
    fr = fr64.astype(np.float32)
    frl = (fr64 - fr.astype(np.float64)).astype(np.float32)
    for h in range(2):
        for d in range(16):
            freq[64 * h + d, 0] = fr[d % 8]
            freq[64 * h + d, 1] = frl[d % 8]
    return c, freq


_NC_CACHE = {}


def make_in_maps(x, c, positions, norm_g, w_ada, b_ada, w_in, q_norm_g, k_norm_g,
                 sgu_ln_g, sgu_ln_b, w_spatial, b_spatial, w_branch_a, w_branch_b, w_out):
    f = np.float32
    cst, freq = _consts()
    gqk = np.zeros((128, 6), f)
    for g in range(3):
        gqk[:, 2 * g] = np.tile(np.asarray(q_norm_g[g], f), 2)
        gqk[:, 2 * g + 1] = np.tile(np.asarray(k_norm_g[g], f), 2)
    wsp = np.ascontiguousarray(np.transpose(np.asarray(w_spatial, f), (2, 0, 1)))
    bsp = np.repeat(np.asarray(b_spatial, f).reshape(4, 2, 128), 64, axis=1)
    bsp = np.ascontiguousarray(np.transpose(bsp, (1, 0, 2)))
    shared = {
        "w_ada": np.ascontiguousarray(w_ada, f),
        "adab": np.ascontiguousarray(np.asarray(b_ada, f).reshape(24, 128).T),
        "normg": np.ascontiguousarray(np.asarray(norm_g, f).reshape(8, 128).T),
        "w_in": np.ascontiguousarray(w_in, f),
        "gqk": gqk,
        "lng": np.ascontiguousarray(np.asarray(sgu_ln_g, f).reshape(1, 512)),
        "lnb": np.ascontiguousarray(np.asarray(sgu_ln_b, f).reshape(1, 512)),
        "wsp": wsp, "bsp": bsp,
        "w_ba": np.ascontiguousarray(w_branch_a, f),
        "w_bb": np.ascontiguousarray(w_branch_b, f),
        "w_out": np.ascontiguousarray(w_out, f),
        "freq": freq, "cst": cst,
    }
    maps = []
    for b in range(8):
        m = dict(shared)
        m["x"] = np.ascontiguousarray(x[b], f)
        m["cT"] = np.ascontiguousarray(np.asarray(c[b], f).reshape(8, 128).T)
        m["pos"] = np.ascontiguousarray(np.asarray(positions[b], np.int32).reshape(1, S))
        maps.append(m)
    return maps


def kernel(**inputs):
    inputs = {k: np.asarray(v) for k, v in inputs.items()}
    if "nc" not in _NC_CACHE:
        _NC_CACHE["nc"] = build_nc()
    nc = _NC_CACHE["nc"]
    in_maps = make_in_maps(**inputs)
    res = run_bass_kernel_spmd(nc, in_maps, core_ids=list(range(8)))
    out = np.stack([np.asarray(r["out"], np.float32) for r in res.results], axis=0)
    return out
```

```python
import math
from contextlib import ExitStack

import numpy as np
import concourse.bass as bass
import concourse.mybir as mybir
from concourse.bass_utils import run_bass_kernel_spmd

F32 = mybir.dt.float32
BF16 = mybir.dt.bfloat16
I32 = mybir.dt.int32
AF = mybir.ActivationFunctionType
ALU = mybir.AluOpType

D = 1024
S = 2048
NT = 16
KC = 8
IN_COLS = 8704
OFF_GATE_A = 4608
OFF_Z_B = 5120
OFF_GATE_B = 6144
OFF_MERGE_A = 6656
OFF_MERGE_B = 7680
EPS = 1e-6
DIL = (1, 4, 16)
ROPE_THETA = 500000.0

ENGS = ("pe", "act", "dve", "pool", "sp")


class Prog:
    def __init__(self):
        self.ops = {e: [] for e in ENGS}
        self.count = {}
        self.waited = {e: {} for e in ENGS}
        self.lastw = {}
        self.readers = {}
        self.bank_i = 0
        self.nb = 8

    def _need(self, eng, reads, writes):
        need = {}

        def add(sk, v, war=False):
            if sk == eng and eng == "pe":
                return
            if need.get(sk, 0) < v:
                need[sk] = v

        for k in reads:
            w = self.lastw.get(k)
            if w:
                add(*w)
        for k in writes:
            w = self.lastw.get(k)
            if w:
                add(*w)
            for r in self.readers.get(k, ()):
                add(r[0], r[1], war=True)
        out = []
        for sk, v in need.items():
            if self.waited[eng].get(sk, 0) >= v:
                continue
            self.waited[eng][sk] = v
            out.append((sk, v))
        return out

    def op(self, eng, fn, reads=(), writes=(), slot=None):
        waits = self._need(eng, reads, writes)
        if slot is None:
            sk, inc = eng, 1
        else:
            sk, inc = "dma_" + slot, 16
        self.count[sk] = self.count.get(sk, 0) + inc
        val = self.count[sk]
        self.ops[eng].append((waits, fn, sk, inc))
        for k in reads:
            self.readers.setdefault(k, []).append((sk, val))
        for k in writes:
            self.lastw[k] = (sk, val)
            self.readers[k] = []

    def barrier(self):
        for e in ENGS:
            waits = []
            for sk, v in self.count.items():
                if sk == e and e == "pe":
                    continue
                if self.waited[e].get(sk, 0) >= v:
                    continue
                self.waited[e][sk] = v
                waits.append((sk, v))
            if waits:
                self.ops[e].append((waits, None, None, 0))
        self.lastw = {}
        self.readers = {}

    def bank(self):
        i = self.bank_i
        self.bank_i = (i + 1) % self.nb
        return i


class Arena:
    def __init__(self, nc, nbytes):
        self.t = nc.alloc_sbuf_tensor("arena", [128, nbytes // 2], BF16)
        self.ap = self.t.ap()
        self.off = 0
        self.cap = nbytes
        self.peak = 0

    def at(self, off, shape, dtype):
        save, savep = self.off, self.peak
        self.off = off
        v = self.alloc(shape, dtype)
        self.off, self.peak = save, savep
        return v

    def alloc(self, shape, dtype):
        es = 2 if dtype == BF16 else 4
        n = 1
        for s in shape[1:]:
            n *= s
        nb = (n * es + 63) // 64 * 64
        o = self.off
        self.off += nb
        self.peak = max(self.peak, self.off)
        assert self.off <= self.cap, ("SBUF arena overflow", self.off, self.cap)
        v = self.ap[:, o // 2:(o + n * es) // 2]
        if dtype != BF16:
            v = v.bitcast(dtype)
        if len(shape) == 3:
            v = v.rearrange("p (a b) -> p a b", a=shape[1])
        elif len(shape) == 4:
            v = v.rearrange("p (a b c) -> p a b c", a=shape[1], b=shape[2])
        return v


class _Stop(Exception):
    pass


def build_nc(debug=False, stop_after=99):
    nc = bass.Bass("TRN2", target_bir_lowering=False)
    dt = nc.dram_tensor
    x_d = dt("x", [S, D], F32, kind="ExternalInput").ap()
    cT_d = dt("cT", [128, 8], F32, kind="ExternalInput").ap()
    pos_d = dt("pos", [1, S], I32, kind="ExternalInput").ap()
    wada_d = dt("w_ada", [D, 3 * D], F32, kind="ExternalInput").ap()
    adab_d = dt("adab", [128, 24], F32, kind="ExternalInput").ap()
    normg_d = dt("normg", [128, 8], F32, kind="ExternalInput").ap()
    win_d = dt("w_in", [D, IN_COLS], F32, kind="ExternalInput").ap()
    gqk_d = dt("gqk", [128, 6], F32, kind="ExternalInput").ap()
    lng_d = dt("lng", [1, 512], F32, kind="ExternalInput").ap()
    lnb_d = dt("lnb", [1, 512], F32, kind="ExternalInput").ap()
    wsp_d = dt("wsp", [128, 8, 128], F32, kind="ExternalInput").ap()
    bsp_d = dt("bsp", [128, 4, 128], F32, kind="ExternalInput").ap()
    wba_d = dt("w_ba", [512, D], F32, kind="ExternalInput").ap()
    wbb_d = dt("w_bb", [512, D], F32, kind="ExternalInput").ap()
    wout_d = dt("w_out", [D, D], F32, kind="ExternalInput").ap()
    freq_d = dt("freq", [128, 2], F32, kind="ExternalInput").ap()
    cst_d = dt("cst", [128, 704], F32, kind="ExternalInput").ap()
    out_d = dt("out", [S, D], F32, kind="ExternalOutput").ap()
    dbg = {}
    if debug:
        for nm, shp, ty in (("d_hT", [128, 8 * S], BF16), ("d_V", [128, 3 * 16 * 512], BF16),
                            ("d_qk", [128, 6 * S], BF16), ("d_ya", [128, 4 * S], BF16),
                            ("d_yb", [128, 4 * S], BF16), ("d_mg", [128, 8 * S], BF16),
                            ("d_tab", [128, 2 * S], BF16), ("d_ada", [128, 24], F32)):
            dbg[nm] = dt(nm, shp, ty, kind="ExternalOutput").ap()

    win_v = win_d.rearrange("(kc p) c -> p kc c", p=128)
    wada_v = wada_d.rearrange("(kc p) c -> p kc c", p=128)

    P = Prog()
    A = Arena(nc, 206 * 1024)
    banks = [nc.alloc_psum_tensor("bank%d" % i, [128, 512], F32).ap() for i in range(8)]

    def B(i):
        return "B%d" % i

    hT = A.alloc([128, KC, S], BF16)
    cst = A.alloc([128, 704], BF16)
    ident = cst[:, 0:128]
    bones = cst[:, 128:256]
    perm = cst[:, 256:384]
    mD = cst[:, 384:512]
    mP = cst[:, 512:640]
    ones64 = cst[:, 640:704]
    mDP = A.alloc([128, 4, 128], BF16)
    mDD = A.alloc([128, 4, 128], BF16)
    WsT = A.alloc([128, 8, 128], BF16)
    bsp = A.alloc([128, 4, 128], F32)
    Ctab = A.alloc([128, S], BF16)
    Stab = A.alloc([128, S], BF16)
    small = A.alloc([128, 128], F32)
    cT = small[:, 0:8]
    sc = small[:, 8:16]
    adab = small[:, 16:40]
    ada = small[:, 40:64]
    normg = small[:, 64:72]
    Acol = small[:, 72:80]
    gqk = small[:, 80:86]
    freq = small[:, 86:87]
    freq_lo = small[:, 87:88]
    ssq = small[:, 88:104]
    rstd_x = small[:, 104:120]
    bnst = small[:, 120:126]
    bnag = small[:, 126:128]
    small2 = A.alloc([128, 64], F32)
    srt = small2[:, 0:16]
    lnr = small2[:, 16:18]
    Bcol = ada[:, 0:8]
    gatecol = ada[:, 16:24]
    negpi = small2[:, 18:19]
    epsc = small2[:, 19:20]
    mPD = A.alloc([128, 4, 128], BF16)
    wchunk = [A.alloc([128, KC, 128], BF16) for _ in range(4)]
    y_aT = A.alloc([128, 4, S], BF16)
    TOP8 = A.cap - 8192
    TOP16 = A.cap - 8192 - 16384
    wbig = [A.at(TOP8, [128, KC, 512], BF16), None]
    TWO_PI = 2.0 * math.pi
    SHR = 1.0 - 2e-6
    PI_LO = 3.1415925
    INV2PI_HI = float(np.float32(1.0 / TWO_PI))
    INV2PI_LO = float(np.float32(1.0 / TWO_PI - INV2PI_HI))
    P.op("dve", lambda e: e.memset(negpi, -math.pi * SHR), writes=["negpi"])
    P.op("dve", lambda e: e.memset(epsc, EPS), writes=["epsc"])

    wc_i = [0]
    wb_i = [0]

    def load_chunk(c0):
        s = wc_i[0] % 4
        wc_i[0] += 1
        key = "wc%d" % s
        P.op("pool", lambda e: e.dma_start(out=wchunk[s], in_=win_v[:, :, c0:c0 + 128]),
             writes=[key], slot=key)
        return wchunk[s], key

    def load_big(c0):
        s = wb_i[0] % 2
        wb_i[0] += 1
        key = "wb%d" % s
        dst = wbig[s]
        P.op("pool", lambda e: e.dma_start(out=dst, in_=win_v[:, :, c0:c0 + 512]),
             writes=[key], slot=key)
        return dst, key

    def mm_group(out_ap, pairs, reads, writes):
        def fn(e):
            n = len(pairs)
            ins = None
            for i, (l, r) in enumerate(pairs):
                ins = e.matmul(out_ap, l, r, start=(i == 0), stop=(i == n - 1))
            return ins
        P.op("pe", fn, reads=reads, writes=writes)

    def zT_block(w, wkey, tb):
        b = P.bank()
        mm_group(banks[b], [(w[:, kc, :], hT[:, kc, tb * 512:(tb + 1) * 512]) for kc in range(KC)],
                 reads=[wkey], writes=[B(b)])
        return b

    try:
        P.op("pool", lambda e: e.dma_start(out=cst, in_=cst_d), writes=["cst"], slot="cst")
        P.op("pool", lambda e: e.dma_start(out=WsT, in_=wsp_d), writes=["WsT"], slot="wsp")
        for nm, dst, src in (("cT", cT, cT_d), ("adab", adab, adab_d), ("normg", normg, normg_d),
                             ("gqk", gqk, gqk_d), ("freq", small[:, 86:88], freq_d), ("bsp", bsp, bsp_d)):
            P.op("sp", (lambda d_, s_: (lambda e: e.dma_start(out=d_, in_=s_)))(dst, src),
                 writes=[nm], slot=nm)
        for j in range(4):
            P.op("dve", (lambda j_: (lambda e: e.tensor_copy(out=mDP[:, j_, :], in_=(mD if j_ % 2 == 0 else mP))))(j),
                 reads=["cst"], writes=["mDP%d" % j])
            P.op("dve", (lambda j_: (lambda e: e.tensor_copy(out=mDD[:, j_, :], in_=mD)))(j),
                 reads=["cst"], writes=["mDD%d" % j])
            P.op("dve", (lambda j_: (lambda e: e.tensor_copy(out=mPD[:, j_, :], in_=(mP if j_ % 2 == 0 else mD))))(j),
                 reads=["cst"], writes=["mPD%d" % j])
        for g in range(8):
            P.op("dve", (lambda g_: (lambda e: e.tensor_tensor(out=WsT[:, g_, :], in0=WsT[:, g_, :], in1=mD, op=ALU.mult)))(g),
                 reads=["cst", "WsT"], writes=["WsT"])

        ph0 = A.off
        xn_all = A.alloc([128, NT, D], BF16)
        xs = [A.alloc([128, D], F32) for _ in range(3)]
        scr = A.alloc([128, D], BF16)
        wada_s = [A.alloc([128, KC, 512], F32) for _ in range(2)]
        rowb = A.alloc([128, 3 * D], F32)
        wada_s.append(A.alloc([128, KC, 512], F32))
        one11 = small2[:, 20:21]
        P.op("dve", lambda e: e.memset(one11, 1.0), writes=["one11"])
        P.op("act", lambda e: e.activation(out=sc, in_=cT, func=AF.Silu), reads=["cT"], writes=["sc"])

        def x_load(t):
            s = t % 3
            P.op("pool", (lambda s_, t_: (lambda e: e.dma_start(out=xs[s_], in_=x_d[t_ * 128:(t_ + 1) * 128, :])))(s, t),
                 writes=["xs%d" % s], slot="xs%d" % s)
        for t in range(3):
            x_load(t)
        early_wv0 = []
        for t in range(NT):
            s = t % 3
            P.op("act", (lambda s_, t_: (lambda e: e.activation(out=scr, in_=xs[s_], func=AF.Square,
                                                                 accum_out=ssq[:, t_:t_ + 1])))(s, t),
                 reads=["xs%d" % s], writes=["scr", "ssq%d" % t])
            P.op("act", (lambda t_: (lambda e: e.activation(out=srt[:, t_:t_ + 1], in_=ssq[:, t_:t_ + 1], func=AF.Sqrt,
                                                             bias=epsc, scale=1.0 / D)))(t),
                 reads=["ssq%d" % t, "epsc"], writes=["srt%d" % t])
            P.op("dve", (lambda t_: (lambda e: e.reciprocal(out=rstd_x[:, t_:t_ + 1], in_=srt[:, t_:t_ + 1])))(t),
                 reads=["srt%d" % t], writes=["rstdx%d" % t])
            P.op("act", (lambda s_, t_: (lambda e: e.activation(out=xn_all[:, t_, :], in_=xs[s_], func=AF.Copy,
                                                                 scale=rstd_x[:, t_:t_ + 1])))(s, t),
                 reads=["xs%d" % s, "rstdx%d" % t], writes=["xn%d" % t])
            if t + 3 < NT:
                x_load(t + 3)
            elif not early_wv0:
                early_wv0.append(load_big(0 * 1536 + 1024))

        for j in range(6):
            s = j % 3
            key = "wada%d" % s
            P.op("sp", (lambda s_, j_: (lambda e: e.dma_start(out=wada_s[s_], in_=wada_v[:, :, j_ * 512:(j_ + 1) * 512])))(s, j),
                 writes=[key], slot=key)
            br_ = P.bank()
            mm_group(banks[br_][0:1, :], [(sc[:, kc:kc + 1], wada_s[s][:, kc, :]) for kc in range(KC)],
                     reads=[key, "sc"], writes=[B(br_)])
            P.op("dve", (lambda j_, b_: (lambda e: e.tensor_copy(out=rowb[0:1, j_ * 512:(j_ + 1) * 512], in_=banks[b_][0:1, :])))(j, br_),
                 reads=[B(br_)], writes=["rowb%d" % j])
        b_ada = P.bank()

        def adaT_fn(e):
            ins = None
            for col in range(24):
                ins = e.matmul(banks[b_ada][:, col:col + 1], rowb[0:1, col * 128:(col + 1) * 128], one11[0:1, 0:1],
                               start=True, stop=True)
            return ins
        P.op("pe", adaT_fn, reads=["rowb%d" % j for j in range(6)] + ["one11"], writes=[B(b_ada)])
        P.op("dve", lambda e: e.tensor_tensor(out=ada, in0=banks[b_ada][:, 0:24], in1=adab, op=ALU.add),
             reads=[B(b_ada), "adab"], writes=["ada"])
        P.op("dve", lambda e: e.scalar_tensor_tensor(out=Acol, in0=ada[:, 8:16], scalar=1.0, in1=normg,
                                                     op0=ALU.add, op1=ALU.mult),
             reads=["ada", "normg"], writes=["Acol"])

        for t in range(NT):
            b = P.bank()
            b2 = P.bank()
            psA = banks[b].rearrange("p (k t) -> p k t", k=4)
            psB = banks[b2].rearrange("p (k t) -> p k t", k=4)

            def tr_fn(e, t_=t, psA_=psA, psB_=psB):
                ins = None
                for kc in range(KC):
                    dst_ = (psA_ if kc < 4 else psB_)[:, kc % 4, :]
                    ins = e.matmul(dst_, xn_all[:, t_, kc * 128:(kc + 1) * 128], ident, start=True, stop=True)
                return ins
            P.op("pe", tr_fn, reads=["xn%d" % t, "cst"], writes=[B(b), B(b2)])
            for kc in range(KC):
                dst = hT[:, kc, t * 128:(t + 1) * 128]
                psT = psA if kc < 4 else psB
                bk = b if kc < 4 else b2
                if kc < 4:
                    P.op("dve", (lambda d_, p_, k_: (lambda e: e.tensor_scalar(out=d_, in0=p_, scalar1=Acol[:, k_:k_ + 1],
                                                                              scalar2=Bcol[:, k_:k_ + 1],
                                                                              op0=ALU.mult, op1=ALU.add)))(dst, psT[:, kc % 4, :], kc),
                         reads=[B(bk), "Acol", "ada"], writes=["hT%d_%d" % (t, kc)])
                else:
                    P.op("act", (lambda d_, p_, k_: (lambda e: e.activation(out=d_, in_=p_, func=AF.Identity,
                                                                           bias=Bcol[:, k_:k_ + 1],
                                                                           scale=Acol[:, k_:k_ + 1])))(dst, psT[:, kc % 4, :], kc),
                         reads=[B(bk), "Acol", "ada"], writes=["hT%d_%d" % (t, kc)])
        if debug:
            P.barrier()
            P.op("sp", lambda e: e.dma_start(out=dbg["d_hT"], in_=hT.rearrange("p k t -> p (k t)")), slot="dbg0")
            P.op("sp", lambda e: e.dma_start(out=dbg["d_ada"], in_=ada), slot="dbg3")
        P.barrier()
        A.off = ph0

        if stop_after == 1:
            raise _Stop()
        off_wbig = A.off
        wbig[1] = A.alloc([128, KC, 512], BF16)
        V = [A.alloc([128, 16, 512], BF16) for _ in range(3)]
        off_qk = A.off
        qk = [[A.alloc([128, S], BF16) for _ in range(2)] for _ in range(3)]
        acc_n = A.alloc([128, S], F32)
        acc_d = A.alloc([128, S], F32)
        pT = [[A.alloc([128, 4, 128], BF16) for _ in range(2)] for _ in range(2)]
        bones_g = A.alloc([128, 6, 128], BF16)
        ginv = A.alloc([128, 8], F32)

        def tok_slice(g, blk):
            d = DIL[g]
            nb = 16 // d
            r, n = blk // nb, blk % nb
            st = 128 * n * d + r
            return slice(st, st + 127 * d + 1, d)

        P.op("dve", lambda e: e.reciprocal(out=ginv[:, 0:6], in_=gqk), reads=["gqk"], writes=["ginv"])
        P.op("dve", lambda e: e.tensor_tensor(out=ginv[:, 0:6], in0=ginv[:, 0:6], in1=ginv[:, 0:6], op=ALU.mult),
             reads=["ginv"], writes=["ginv"])
        for j in range(6):
            P.op("dve", (lambda j_: (lambda e: e.tensor_scalar(out=bones_g[:, j_, :], in0=bones, scalar1=ginv[:, j_:j_ + 1],
                                                               scalar2=None, op0=ALU.mult)))(j),
                 reads=["ginv", "cst"], writes=["bones_g%d" % j])

        seq = [(hp, g) for hp in range(4) for g in range(3)]
        worder = []
        for idx, (hp, g) in enumerate(seq):
            if idx == 0:
                worder += [(hp, g, 0), (hp, g, 1)]
            if idx + 1 < len(seq):
                nh, ng = seq[idx + 1]
                worder += [(nh, ng, 0), (nh, ng, 1)]
            if g == 2:
                worder += [(hp, -1, 0)]
        wloaded = {}
        wnext = [0]

        def wcol(item):
            hp_, g_, role_ = item
            if g_ < 0:
                return OFF_GATE_A + hp_ * 128
            return g_ * 1536 + role_ * 512 + hp_ * 128

        def get_w(item):
            k = worder.index(item)
            while wnext[0] <= min(k + 2, len(worder) - 1):
                wloaded[worder[wnext[0]]] = load_chunk(wcol(worder[wnext[0]]))
                wnext[0] += 1
            return wloaded[item]

        get_w(worder[0])
        posi = A.at(off_qk, [128, S], I32)
        ang = A.at(off_qk + 8192, [128, S], F32)
        yy = A.at(off_qk + 16384, [128, S], F32)
        ki = A.at(off_qk + 24576, [128, S], I32)
        kf = A.at(off_qk + 32768, [128, S], F32)
        P.op("sp", lambda e: e.dma_start(out=posi, in_=pos_d.partition_broadcast(128)),
             writes=["posi"], slot="posi")
        tab_ops = []
        tab_ops.append(lambda: P.op("dve", lambda e: e.tensor_copy(out=ang, in_=posi), reads=["posi"], writes=["ang"]))
        tab_ops.append(lambda: P.op("dve", lambda e: e.tensor_scalar(out=kf, in0=ang, scalar1=freq_lo, scalar2=None, op0=ALU.mult),
                                    reads=["ang", "freq"], writes=["kf"]))
        tab_ops.append(lambda: P.op("dve", lambda e: e.scalar_tensor_tensor(out=ang, in0=ang, scalar=freq, in1=kf,
                                                                            op0=ALU.mult, op1=ALU.add),
                                    reads=["ang", "kf", "freq"], writes=["ang"]))
        for tab, offs in ((Stab, 0.5), (Ctab, 0.75)):
            tab_ops.append((lambda o_: (lambda: P.op("dve", lambda e: e.tensor_scalar(out=yy, in0=ang, scalar1=INV2PI_HI, scalar2=o_,
                                                                                      op0=ALU.mult, op1=ALU.add),
                                                     reads=["ang"], writes=["yy"])))(offs))
            tab_ops.append(lambda: P.op("dve", lambda e: e.scalar_tensor_tensor(out=yy, in0=ang, scalar=INV2PI_LO, in1=yy,
                                                                                op0=ALU.mult, op1=ALU.add),
                                        reads=["ang", "yy"], writes=["yy"]))
            tab_ops.append(lambda: P.op("dve", lambda e: e.tensor_copy(out=ki, in_=yy), reads=["yy"], writes=["ki"]))
            tab_ops.append(lambda: P.op("dve", lambda e: e.tensor_copy(out=kf, in_=ki), reads=["ki"], writes=["kf"]))
            tab_ops.append(lambda: P.op("dve", lambda e: e.tensor_tensor(out=yy, in0=yy, in1=kf, op=ALU.subtract),
                                        reads=["yy", "kf"], writes=["yy"]))
            tab_ops.append(lambda: P.op("dve", lambda e: e.tensor_single_scalar(out=kf, in_=yy, scalar=0.0, op=ALU.is_lt),
                                        reads=["yy"], writes=["kf"]))
            tab_ops.append(lambda: P.op("dve", lambda e: e.tensor_tensor(out=yy, in0=yy, in1=kf, op=ALU.add),
                                        reads=["yy", "kf"], writes=["yy"]))
            tab_ops.append(lambda: P.op("dve", lambda e: e.tensor_scalar(out=yy, in0=yy, scalar1=TWO_PI, scalar2=-math.pi,
                                                                          op0=ALU.mult, op1=ALU.add),
                                        reads=["yy"], writes=["yy"]))
            tab_ops.append(lambda: P.op("dve", lambda e: e.tensor_scalar(out=yy, in0=yy, scalar1=-PI_LO, scalar2=PI_LO,
                                                                          op0=ALU.max, op1=ALU.min),
                                        reads=["yy"], writes=["yy"]))
            tab_ops.append((lambda t_: (lambda: P.op("act", lambda e: e.activation(out=t_, in_=yy, func=AF.Sin),
                                                     reads=["yy"], writes=["tab"])))(tab))

        for g in range(3):
            if g == 0:
                w, wkey = early_wv0[0]
            else:
                w, wkey = load_big(g * 1536 + 1024)
            for blk in range(16):
                sl = tok_slice(g, blk)
                b = P.bank()
                mm_group(banks[b], [(hT[:, kc, sl], w[:, kc, :]) for kc in range(KC)],
                         reads=[wkey], writes=[B(b)])
                if True:
                    P.op("act", (lambda g_, k_, b_: (lambda e: e.copy(out=V[g_][:, k_, :], in_=banks[b_])))(g, blk, b),
                         reads=[B(b)], writes=["V%d_%d" % (g, blk)])
                else:
                    P.op("dve", (lambda g_, k_, b_: (lambda e: e.tensor_copy(out=V[g_][:, k_, :], in_=banks[b_])))(g, blk, b),
                         reads=[B(b)], writes=["V%d_%d" % (g, blk)])
                if tab_ops:
                    tab_ops.pop(0)()
        while tab_ops:
            tab_ops.pop(0)()
        P.barrier()
        NBUF = 3
        save_off = A.off
        A.off = off_wbig
        def talloc(nbytes, dtype):
            if A.off < save_off and A.off + nbytes > off_wbig + 8192:
                A.off = save_off
            return A.alloc([128, 512], dtype)
        tmps = []
        for i in range(NBUF):
            tset = dict(zg=talloc(1024, BF16), sq=talloc(1024, BF16), zc=talloc(1024, BF16), zs=talloc(1024, BF16),
                        ln=talloc(2048, F32))
            tset["rs"] = tset["ln"]
            tmps.append(tset)
        ftmp = dict(t1=talloc(2048, F32), t2=talloc(2048, F32))
        if A.off < save_off:
            A.off = save_off
        assert A.off <= TOP8, ("phase 2 overlaps top slot", A.off, TOP8)
        wv = wbig[0]
        P.op("pool", lambda e: e.dma_start(out=wv, in_=win_v[:, :, OFF_Z_B + 512:OFF_Z_B + 1024]), writes=["wv"], slot="wb0")
        P.nb = 6
        P.bank_i = 0
        BO, BD = 6, 7

        cb_i = [0]

        def chunk_block(hp, g, role, tb):
            st = {}

            def stageA():
                w, wkey = get_w((hp, g, role))
                i = cb_i[0] % NBUF
                cb_i[0] += 1
                T = tmps[i]
                sf = "_%d" % i
                gcol = gqk[:, g * 2 + role:g * 2 + role + 1]
                bz = zT_block(w, wkey, tb)
                P.op("act", lambda e: e.activation(out=T["zg"], in_=banks[bz], func=AF.Copy, scale=gcol),
                     reads=[B(bz), "gqk"], writes=["zg" + sf])
                P.op("dve", lambda e: e.tensor_tensor(out=T["sq"], in0=T["zg"], in1=T["zg"], op=ALU.mult),
                     reads=["zg" + sf], writes=["sq" + sf])
                ts_ = slice(tb * 512, (tb + 1) * 512)
                P.op("dve", lambda e: e.tensor_tensor(out=T["zc"], in0=T["zg"], in1=Ctab[:, ts_], op=ALU.mult),
                     reads=["zg" + sf], writes=["zc" + sf])
                P.op("dve", lambda e: e.tensor_tensor(out=T["zs"], in0=T["zg"], in1=Stab[:, ts_], op=ALU.mult),
                     reads=["zg" + sf], writes=["zs" + sf])
                st["T"], st["sf"] = T, sf

            def stageB():
                T, sf = st["T"], st["sf"]
                dstT = qk[g][role]
                ts_ = slice(tb * 512, (tb + 1) * 512)
                bs = P.bank()
                br = P.bank()

                def sr_fn(e):
                    e.matmul(banks[bs], bones_g[:, g * 2 + role, :], T["sq"], start=True, stop=True)
                    e.matmul(banks[br], ident, T["zc"], start=True, stop=False)
                    return e.matmul(banks[br], perm, T["zs"], start=False, stop=True)
                P.op("pe", sr_fn, reads=["sq" + sf, "zc" + sf, "zs" + sf, "bones_g%d" % (g * 2 + role), "cst"], writes=[B(bs), B(br)])
                P.op("act", lambda e: e.activation(out=T["ln"], in_=banks[bs], func=AF.Ln, bias=epsc, scale=1.0),
                     reads=[B(bs), "epsc"], writes=["ln" + sf])
                P.op("act", lambda e: e.activation(out=T["rs"], in_=T["ln"], func=AF.Exp, scale=-0.5),
                     reads=["ln" + sf], writes=["ln" + sf])
                P.op("dve", lambda e: e.tensor_tensor(out=dstT[:, ts_], in0=banks[br], in1=T["rs"], op=ALU.mult),
                     reads=[B(br), "ln" + sf], writes=["qk%d%d_%d" % (g, role, tb)])
            return stageA, stageB

        def chunk_steps(hp, g):
            return [chunk_block(hp, g, ro, tb) for tb in range(4) for ro in range(2)]

        unit_ctr = [0]

        def att_units(hp, g):
            d = DIL[g]
            nb = 16 // d
            qT, kT = qk[g][0], qk[g][1]
            qkeys = ["qk%d%d_%d" % (g, ro, tb) for ro in range(2) for tb in range(4)]
            if g == 2:
                qgroups = [[r for r in range(4 * i, 4 * i + 4)] for i in range(4)]
            else:
                qgroups = [[r * nb + n for n in range(4 * i, 4 * i + 4)] for r in range(d) for i in range(nb // 4)]
            out = []
            for qg in qgroups:
                subs = []
                for slot_, blk in enumerate(qg):
                    n = blk % nb
                    if g != 2 and n > 0:
                        subs.append((blk - 1, blk, slot_, True, False, 1))
                        subs.append((blk, blk, slot_, False, True, 0))
                    else:
                        subs.append((blk, blk, slot_, True, True, 0))
                units = [subs[i:i + 4] for i in range(0, len(subs), 4)]
                for ui, un in enumerate(units):
                    par = unit_ctr[0] % 2
                    unit_ctr[0] += 1
                    last_of_group = (ui == len(units) - 1)

                    def S_stage(un=un, par=par):
                        types = [u_[5] for u_ in un]
                        nj = len(un)
                        bS = [P.bank(), P.bank()]
                        pss = [banks[bS[e_]].rearrange("p (j q) -> p j q", j=4) for e_ in range(2)]

                        def s_fn(e):
                            ins = None
                            for j, (kb, qb, _s, _f, _l, _t) in enumerate(un):
                                for e_ in range(2):
                                    rows = slice(64 * e_, 64 * e_ + 64)
                                    ins = e.matmul(pss[e_][:, j, :], kT[rows, tok_slice(g, kb)], qT[rows, tok_slice(g, qb)],
                                                   start=True, stop=True)
                            return ins
                        if g == 2:
                            rk = qkeys
                        else:
                            tbs_k = set((kb // 4) if g == 0 else (kb % nb) for (kb, qb, _s, _f, _l, _t) in un)
                            tbs_q = set((qb // 4) if g == 0 else (qb % nb) for (kb, qb, _s, _f, _l, _t) in un)
                            rk = ["qk%d1_%d" % (g, t_) for t_ in tbs_k] + ["qk%d0_%d" % (g, t_) for t_ in tbs_q]
                        P.op("pe", s_fn, reads=rk, writes=[B(bS[0]), B(bS[1])])
                        if all(t_ == 0 for t_ in types):
                            mk = mDD
                        elif all(types[j] == (j % 2) for j in range(nj)):
                            mk = mDP
                        elif all(types[j] == ((j + 1) % 2) for j in range(nj)):
                            mk = mPD
                        else:
                            raise AssertionError("mask pattern")
                        for e_ in range(2):
                            pt = pT[par][e_]
                            pkey = "pT%d%d" % (par, e_)
                            P.op("act", (lambda ps=pss[e_], pt=pt: (lambda e: e.activation(out=pt[:, 0:nj, :], in_=ps[:, 0:nj, :],
                                                                                           func=AF.Exp, scale=0.125)))(),
                                 reads=[B(bS[e_])], writes=[pkey])
                            P.op("dve",
                                 (lambda pt=pt, mk=mk: (lambda e: e.tensor_tensor(out=pt[:, 0:nj, :], in0=pt[:, 0:nj, :],
                                                                                  in1=mk[:, 0:nj, :], op=ALU.mult)))(),
                                 reads=[pkey], writes=[pkey])

                    def PV_stage(un=un, par=par):
                        po = banks[BO].rearrange("p (j q) -> p j q", j=4)
                        pd = banks[BD].rearrange("p (j q) -> p j q", j=4)

                        def pv_fn(e):
                            ins = None
                            for j, (kb, qb, sl_, f_, l_, _t) in enumerate(un):
                                for e_ in range(2):
                                    orow = slice(64 * e_, 64 * e_ + 64)
                                    hcol = slice((hp * 2 + e_) * 64, (hp * 2 + e_) * 64 + 64)
                                    e.matmul(po[orow, sl_, :], V[g][:, kb, hcol], pT[par][e_][:, j, :], start=f_, stop=l_)
                                for e_ in range(2):
                                    orow = slice(64 * e_, 64 * e_ + 64)
                                    ins = e.matmul(pd[orow, sl_, :], ones64, pT[par][e_][:, j, :], start=f_, stop=l_)
                            return ins
                        P.op("pe", pv_fn, reads=["pT%d0" % par, "pT%d1" % par, "cst"], writes=[B(BO), B(BD)])

                    def post(qg=qg):
                        if g == 0:
                            t0 = (qg[0] % nb) * 128
                            dn = acc_n[:, t0:t0 + 512]
                            dd = acc_d[:, t0:t0 + 512]
                            P.op("act", lambda e: e.copy(out=dn, in_=banks[BO]), reads=[B(BO)], writes=["accn"])
                            P.op("dve", lambda e: e.tensor_copy(out=dd, in_=banks[BD]), reads=[B(BD)], writes=["accd"])
                        else:
                            if g == 1:
                                r = qg[0] // nb
                                dn = acc_n[:, r:S:4]
                                dd = acc_d[:, r:S:4]
                                sn = banks[BO]
                                sd = banks[BD]
                            else:
                                r0 = qg[0]
                                dn = acc_n.rearrange("p (i r) -> p r i", r=16)[:, r0:r0 + 4, :]
                                dd = acc_d.rearrange("p (i r) -> p r i", r=16)[:, r0:r0 + 4, :]
                                sn = banks[BO].rearrange("p (j q) -> p j q", j=4)
                                sd = banks[BD].rearrange("p (j q) -> p j q", j=4)
                            P.op("dve", lambda e: e.tensor_tensor(out=dn, in0=sn, in1=dn, op=ALU.add),
                                 reads=[B(BO), "accn"], writes=["accn"])
                            P.op("dve", lambda e: e.tensor_tensor(out=dd, in0=sd, in1=dd, op=ALU.add),
                                 reads=[B(BD), "accd"], writes=["accd"])
                    out.append((S_stage, PV_stage, post if last_of_group else None))
            return out

        def finalize(hp):
            w, wkey = get_w((hp, -1, 0))
            for tb in range(4):
                i = cb_i[0] % NBUF
                cb_i[0] += 1
                T = dict(tmps[i])
                T.update(ftmp)
                sf = "_%d" % i
                ts_ = slice(tb * 512, (tb + 1) * 512)
                bz = zT_block(w, wkey, tb)
                P.op("act", (lambda T=T, bz=bz: (lambda e: e.activation(out=T["t2"], in_=banks[bz], func=AF.Exp, scale=-1.0)))(),
                     reads=[B(bz)], writes=["ft2"])
                P.op("dve", (lambda T=T, ts_=ts_: (lambda e: e.scalar_tensor_tensor(out=T["t1"], in0=T["t2"], scalar=1.0, in1=acc_d[:, ts_],
                                                                                    op0=ALU.add, op1=ALU.mult)))(),
                     reads=["ft2", "accd"], writes=["ft1"])
                P.op("act", (lambda T=T: (lambda e: e.activation(out=T["ln"], in_=T["t1"], func=AF.Ln)))(),
                     reads=["ft1"], writes=["ln" + sf])
                P.op("act", (lambda T=T: (lambda e: e.activation(out=T["rs"], in_=T["ln"], func=AF.Exp, scale=-1.0)))(),
                     reads=["ln" + sf], writes=["ln" + sf])
                P.op("dve", (lambda T=T, ts_=ts_: (lambda e: e.tensor_tensor(out=T["t1"], in0=acc_n[:, ts_], in1=T["rs"], op=ALU.mult)))(),
                     reads=["accn", "ln" + sf, "ft1"], writes=["ft1"])
                P.op("dve", (lambda T=T, ts_=ts_, bz=bz: (lambda e: e.tensor_tensor(out=y_aT[:, hp, ts_], in0=banks[bz], in1=T["t1"], op=ALU.mult)))(),
                     reads=[B(bz), "ft1"], writes=["ya%d_%d" % (hp, tb)])

        pendB = [None]

        def emit_chunk(ab):
            ab[0]()
            if pendB[0] is not None:
                pendB[0]()
            pendB[0] = ab[1]

        def flushB():
            if pendB[0] is not None:
                pendB[0]()
                pendB[0] = None

        for ab in chunk_steps(*seq[0]):
            emit_chunk(ab)
        flushB()
        for idx, (hp, g) in enumerate(seq):
            units = att_units(hp, g)
            nxt = chunk_steps(*seq[idx + 1]) if idx + 1 < len(seq) else []
            ci = 0
            units[0][0]()
            for u in range(len(units)):
                if u + 1 < len(units):
                    units[u + 1][0]()
                k = -(-(u + 1) * len(nxt) // len(units)) - ci
                for _ in range(k):
                    emit_chunk(nxt[ci])
                    ci += 1
                units[u][1]()
                if units[u][2] is not None:
                    units[u][2]()
            flushB()
            if g == 2:
                finalize(hp)
            if debug and hp == 0 and g == 2:
                P.barrier()
                for g2 in range(3):
                    P.op("sp", (lambda g_: (lambda e: e.dma_start(out=dbg["d_V"][:, g_ * 8192:(g_ + 1) * 8192],
                                                                  in_=V[g_].rearrange("p a b -> p (a b)"))))(g2), slot="dbgV%d" % g2)
                P.barrier()

        P.barrier()
        P.nb = 8
        P.bank_i = 0
        A.off = ph0

        if stop_after == 2:
            raise _Stop()
        y_bT = A.alloc([128, 4, S], BF16)
        wug = A.alloc([128, 8, KC, 128], BF16)
        sv_all = A.alloc([128, 4, S], F32)
        lng = A.alloc([128, 512], F32)
        lnb = A.alloc([128, 512], F32)
        vg1 = A.alloc([128, 4, 512], F32)
        vg = [vg1, vg1]
        vn_t = [A.alloc([128, 512], F32) for _ in range(2)]
        vl_t = [A.alloc([128, 512], BF16) for _ in range(4)]
        gu_t = [A.alloc([128, 512], F32) for _ in range(2)]
        gs_t = [A.alloc([128, 512], F32) for _ in range(2)]
        st3 = A.alloc([128, 64], F32)
        bst = [st3[:, 0:24].rearrange("p (j k) -> p j k", j=4), st3[:, 24:48].rearrange("p (j k) -> p j k", j=4)]
        mv = [st3[:, 48:56].rearrange("p (j k) -> p j k", j=4), st3[:, 56:64].rearrange("p (j k) -> p j k", j=4)]
        st3b = A.alloc([128, 16], F32)
        rsd = [st3b[:, 0:4], st3b[:, 4:8]]
        assert A.off <= TOP16, ("phase 3 overlaps top16", A.off, TOP16)
        for j in (0, 4, 1, 5, 2, 6, 3, 7):
            c0 = OFF_Z_B + j * 128 if j < 4 else OFF_GATE_B + (j - 4) * 128
            P.op("pool", (lambda j_, c_: (lambda e: e.dma_start(out=wug[:, j_], in_=win_v[:, :, c_:c_ + 128])))(j, c0),
                 writes=["wug%d" % j], slot="wug%d" % j)
        wba = A.at(TOP16, [128, 4, D], BF16)
        wbb = A.at(TOP16 + 8192, [128, 4, D], BF16)
        P.op("pool", lambda e: e.dma_start(out=wba, in_=wba_d.rearrange("(kc p) c -> p kc c", p=128)), writes=["wba"], slot="wba")
        P.op("pool", lambda e: e.dma_start(out=wbb, in_=wbb_d.rearrange("(kc p) c -> p kc c", p=128)), writes=["wbb"], slot="wbb")
        P.op("sp", lambda e: e.dma_start(out=lng, in_=lng_d.partition_broadcast(128)), writes=["lng"], slot="lng")
        P.op("sp", lambda e: e.dma_start(out=lnb, in_=lnb_d.partition_broadcast(128)), writes=["lnb"], slot="lnb")

        gate_bc = A.at(TOP8, [128, D], F32)
        gate_bf = A.at(TOP8 + 4096, [128, KC, 128], BF16)
        onesT = A.at(TOP8 + 6144, [128, 128], BF16)
        gres = A.at(TOP8 + 6400, [128, KC], F32)
        ghi = A.at(TOP8 + 6464, [128, KC], BF16)

        def gate_steps():
            bgt = [P.bank(), P.bank()]
            steps = []

            def prep0():
                P.op("dve", lambda e: e.memset(onesT, 1.0), writes=["onesT", "wv"])
                P.op("dve", lambda e: e.tensor_copy(out=ghi, in_=gatecol), writes=["ghi"])
                P.op("dve", lambda e: e.tensor_tensor(out=gres, in0=gatecol, in1=ghi, op=ALU.subtract), reads=["ghi"], writes=["gres"])
            steps.append(prep0)
            for half, (src, key) in enumerate(((ghi, "ghi"), (gres, "gres"))):
                def prep(src=src, key=key):
                    for kc in range(KC):
                        P.op("dve", (lambda k_: (lambda e: e.tensor_scalar(out=gate_bf[:, k_, :], in0=ident, scalar1=src[:, k_:k_ + 1],
                                                                           scalar2=None, op0=ALU.mult)))(kc),
                             reads=[key], writes=["gbf%d" % kc])
                steps.append(prep)

                def pe_ev(half=half):
                    for hf in range(2):
                        bb_ = bgt[hf]

                        def g_fn(e, hf_=hf, bb__=bb_):
                            ins = None
                            for j in range(4):
                                kc = hf_ * 4 + j
                                ins = e.matmul(banks[bb__][:, j * 128:(j + 1) * 128], onesT, gate_bf[:, kc, :], start=True, stop=True)
                            return ins
                        P.op("pe", g_fn, reads=["onesT"] + ["gbf%d" % k for k in range(KC)], writes=[B(bb_)])
                        dst = gate_bc[:, hf * 512:(hf + 1) * 512]
                        if half == 0:
                            P.op("dve", (lambda d_, b_: (lambda e: e.tensor_copy(out=d_, in_=banks[b_])))(dst, bb_),
                                 reads=[B(bb_)], writes=["gbc%d" % hf])
                        else:
                            P.op("dve", (lambda d_, b_: (lambda e: e.tensor_tensor(out=d_, in0=banks[b_], in1=d_, op=ALU.add)))(dst, bb_),
                                 reads=[B(bb_), "gbc%d" % hf], writes=["gbc%d" % hf])
                steps.append(pe_ev)
            return steps

        mw = {0: (load_chunk(OFF_MERGE_A), load_chunk(OFF_MERGE_B)),
              1: (load_chunk(OFF_MERGE_A + 128), load_chunk(OFF_MERGE_B + 128))}
        def V1(t):
            tbp = (t // 4) % 2
            j = t % 4
            b = P.bank()
            mm_group(banks[b], [(hT[:, kc, t * 128:(t + 1) * 128], wv[:, kc, :]) for kc in range(KC)],
                     reads=["wv"], writes=[B(b)])
            P.op("act", lambda e: e.activation(out=vg[tbp][:, j, :], in_=banks[b], func=AF.Gelu),
                 reads=[B(b)], writes=["vg_%d" % j])
            P.op("dve", lambda e: e.bn_stats(out=bst[tbp][:, j, :], in_=vg[tbp][:, j, :]),
                 reads=["vg_%d" % j], writes=["bst%d_%d" % (tbp, j)])
            P.op("dve", lambda e: e.bn_aggr(out=mv[tbp][:, j, :], in_=bst[tbp][:, j, :]),
                 reads=["bst%d_%d" % (tbp, j)], writes=["mv%d_%d" % (tbp, j)])

        def V2(tb):
            tbp = tb % 2
            P.op("act", lambda e: e.activation(out=rsd[tbp], in_=mv[tbp][:, :, 1], func=AF.Sqrt, bias=epsc, scale=1.0),
                 reads=["mv%d_%d" % (tbp, j) for j in range(4)] + ["epsc"], writes=["rsd%d" % tbp])
            P.op("dve", lambda e: e.reciprocal(out=rsd[tbp], in_=rsd[tbp]), reads=["rsd%d" % tbp], writes=["rsd%d" % tbp])

        def V3a(t):
            tbp = (t // 4) % 2
            j = t % 4
            s = t % 2
            P.op("dve", lambda e: e.tensor_scalar(out=vn_t[s], in0=vg[tbp][:, j, :], scalar1=mv[tbp][:, j, 0:1],
                                                  scalar2=rsd[tbp][:, j:j + 1], op0=ALU.subtract, op1=ALU.mult),
                 reads=["vg_%d" % j, "mv%d_%d" % (tbp, j), "rsd%d" % tbp], writes=["vn%d" % s])
            P.op("dve", lambda e: e.tensor_tensor(out=vn_t[s], in0=vn_t[s], in1=lng, op=ALU.mult),
                 reads=["vn%d" % s, "lng"], writes=["vn%d" % s])
            P.op("pool", lambda e: e.tensor_tensor(out=vl_t[j], in0=vn_t[s], in1=lnb, op=ALU.add),
                 reads=["vn%d" % s, "lnb"], writes=["vl%d" % j])

        def V3b(t):
            j = t % 4
            bsv = P.bank()
            psv = banks[bsv].rearrange("p (c q) -> p c q", c=4)

            def sv_fn(e):
                ins = None
                for gg in range(8):
                    ins = e.matmul(psv[64 * (gg % 2):64 * (gg % 2) + 64, gg // 2, :], vl_t[j][:, gg * 64:(gg + 1) * 64],
                                   WsT[:, gg, :], start=True, stop=True)
                return ins
            P.op("pe", sv_fn, reads=["vl%d" % j, "WsT"], writes=[B(bsv)])
            P.op("dve", lambda e: e.tensor_tensor(out=sv_all[:, :, t * 128:(t + 1) * 128], in0=psv, in1=bsp, op=ALU.add),
                 reads=[B(bsv), "bsp"], writes=["sv%d" % t])

        ug_i = [0]

        def UG(c, tb):
            i = ug_i[0] % 2
            ug_i[0] += 1
            ts_ = slice(tb * 512, (tb + 1) * 512)
            bu = P.bank()
            bg = P.bank()

            def ug_fn(e):
                ins = None
                for kc in range(KC):
                    e.matmul(banks[bu], wug[:, c, kc, :], hT[:, kc, ts_], start=(kc == 0), stop=(kc == KC - 1))
                for kc in range(KC):
                    ins = e.matmul(banks[bg], wug[:, 4 + c, kc, :], hT[:, kc, ts_], start=(kc == 0), stop=(kc == KC - 1))
                return ins
            P.op("pe", ug_fn, reads=["wug%d" % c, "wug%d" % (4 + c)], writes=[B(bu), B(bg)])
            P.op("act", lambda e: e.activation(out=gu_t[i], in_=banks[bu], func=AF.Gelu), reads=[B(bu)], writes=["gu%d" % i])

            def a2():
                P.op("act", lambda e: e.activation(out=gs_t[i], in_=banks[bg], func=AF.Silu), reads=[B(bg)], writes=["gs%d" % i])
                P.op("dve", lambda e: e.tensor_tensor(out=gu_t[i], in0=gu_t[i], in1=gs_t[i], op=ALU.mult),
                     reads=["gu%d" % i, "gs%d" % i], writes=["gu%d" % i])

            def b_():
                P.op("pool", lambda e: e.tensor_tensor(out=y_bT[:, c, ts_], in0=gu_t[i], in1=sv_all[:, c, ts_], op=ALU.mult),
                     reads=["gu%d" % i] + ["sv%d" % t for t in range(4 * tb, 4 * tb + 4)], writes=["yb%d_%d" % (c, tb)])
            return a2, b_

        for t in range(4):
            V1(t)
        V2(0)
        for t in range(4):
            V3a(t)
        for tb in range(4):
            pend = []
            if tb == 3:
                gsteps = gate_steps()
                gsteps[0]()
                gsteps[1]()
            for c in range(4):
                a2, b_ = UG(c, tb)
                if tb + 1 < 4:
                    V1(4 * (tb + 1) + c)
                a2()
                if c < 2:
                    pend.append(b_)
                    if c == 1:
                        for t in range(4 * tb, 4 * tb + 4):
                            V3b(t)
                        for f_ in pend:
                            f_()
                else:
                    b_()
                if tb == 3 and c == 1:
                    gsteps[2]()
                if tb == 3 and c == 2:
                    gsteps[3]()
                if tb == 3 and c == 3:
                    gsteps[4]()
            if tb + 1 < 4:
                V2(tb + 1)
                for t in range(4 * (tb + 1), 4 * (tb + 1) + 4):
                    V3a(t)
        if debug:
            P.barrier()
            P.op("sp", lambda e: e.dma_start(out=dbg["d_ya"], in_=y_aT.rearrange("p k t -> p (k t)")), slot="dbg4")
            P.op("sp", lambda e: e.dma_start(out=dbg["d_yb"], in_=y_bT.rearrange("p k t -> p (k t)")), slot="dbg5")
        P.barrier()
        A.off = ph0

        if stop_after == 3:
            raise _Stop()
        y_bT = A.alloc([128, 4, S], BF16)
        wout = A.alloc([128, KC, D], BF16)
        mg = A.alloc([128, KC, S], BF16)
        xs2 = [A.alloc([128, D], F32) for _ in range(2)]
        ot = [A.alloc([128, D], F32) for _ in range(2)]
        sga = A.alloc([128, 512], F32)
        sgb = A.alloc([128, 512], F32)
        m1 = A.alloc([128, 512], F32)
        m2 = A.alloc([128, 512], F32)
        for h in range(2):
            P.op("pool", (lambda h_: (lambda e: e.dma_start(out=wout[:, :, h_ * 512:(h_ + 1) * 512],
                                                            in_=wout_d.rearrange("(kc p) c -> p kc c", p=128)[:, :, h_ * 512:(h_ + 1) * 512])))(h),
                 writes=["wout%d" % h], slot="wout%d" % h)
        mtmp = [dict(sga=sga, sgb=sgb, m1=m1, m2=m2),
                dict(sga=A.alloc([128, 512], F32), sgb=A.alloc([128, 512], F32),
                     m1=A.alloc([128, 512], F32), m2=A.alloc([128, 512], F32))]
        assert A.off <= TOP16, ("phase 4 overlaps top16", A.off, TOP16)
        it4 = 0
        for fc in range(KC):
            if fc + 1 < KC and (fc + 1) not in mw:
                mw[fc + 1] = (load_chunk(OFF_MERGE_A + (fc + 1) * 128), load_chunk(OFF_MERGE_B + (fc + 1) * 128))
            (wma, wmakey), (wmb, wmbkey) = mw[fc]
            for tb in range(4):
                ts_ = slice(tb * 512, (tb + 1) * 512)
                M = mtmp[it4 % 2]
                sf = "_%d" % (it4 % 2)
                it4 += 1
                bga, bgb, bpa, bpb = P.bank(), P.bank(), P.bank(), P.bank()

                def merge_fn(e, wma=wma, wmb=wmb, ts_=ts_, fc=fc, bga=bga, bgb=bgb, bpa=bpa, bpb=bpb):
                    ins = None
                    for kc in range(KC):
                        e.matmul(banks[bga], wma[:, kc, :], hT[:, kc, ts_], start=(kc == 0), stop=(kc == KC - 1))
                    for kc in range(KC):
                        e.matmul(banks[bgb], wmb[:, kc, :], hT[:, kc, ts_], start=(kc == 0), stop=(kc == KC - 1))
                    for kc in range(4):
                        e.matmul(banks[bpa], wba[:, kc, fc * 128:(fc + 1) * 128], y_aT[:, kc, ts_], start=(kc == 0), stop=(kc == 3))
                    for kc in range(4):
                        ins = e.matmul(banks[bpb], wbb[:, kc, fc * 128:(fc + 1) * 128], y_bT[:, kc, ts_], start=(kc == 0), stop=(kc == 3))
                    return ins
                P.op("pe", merge_fn, reads=[wmakey, wmbkey, "wba", "wbb"], writes=[B(bga), B(bgb), B(bpa), B(bpb)])
                P.op("act", (lambda b_, M=M: (lambda e: e.activation(out=M["sga"], in_=banks[b_], func=AF.Sigmoid)))(bga),
                     reads=[B(bga)], writes=["sga" + sf])
                P.op("act", (lambda b_, M=M: (lambda e: e.activation(out=M["sgb"], in_=banks[b_], func=AF.Sigmoid)))(bgb),
                     reads=[B(bgb)], writes=["sgb" + sf])
                P.op("dve", (lambda b_, M=M: (lambda e: e.tensor_tensor(out=M["m1"], in0=banks[b_], in1=M["sga"], op=ALU.mult)))(bpa),
                     reads=[B(bpa), "sga" + sf], writes=["m1" + sf])
                P.op("dve", (lambda b_, M=M: (lambda e: e.tensor_tensor(out=M["m2"], in0=banks[b_], in1=M["sgb"], op=ALU.mult)))(bpb),
                     reads=[B(bpb), "sgb" + sf], writes=["m2" + sf])
                P.op("pool", (lambda f_, s_, M=M: (lambda e: e.tensor_tensor(out=mg[:, f_, s_], in0=M["m1"], in1=M["m2"], op=ALU.add)))(fc, ts_),
                     reads=["m1" + sf, "m2" + sf], writes=["mg%d_%d" % (fc, tb)])
        if debug:
            P.barrier()
            P.op("sp", lambda e: e.dma_start(out=dbg["d_mg"], in_=mg.rearrange("p k t -> p (k t)")), slot="dbg6")
            P.barrier()
        xs2 = xs2 + [A.at(TOP16, [128, D], F32), A.at(TOP16 + 8192, [128, D], F32)]
        ot = ot + [A.at(TOP16 + 4096, [128, D], F32), A.at(TOP16 + 12288, [128, D], F32)]
        alias = {2: "wba", 3: "wbb"}
        first_x = set()
        first_o = set()

        def x2_load(t):
            s = t % 4
            wr = ["xs2_%d" % s]
            if s in alias and s not in first_x:
                first_x.add(s)
                wr.append(alias[s])
            P.op("sp", (lambda s_, t_: (lambda e: e.dma_start(out=xs2[s_], in_=x_d[t_ * 128:(t_ + 1) * 128, :])))(s, t),
                 writes=wr, slot="xs2_%d" % s)
        for t in range(4):
            x2_load(t)
        for t in range(NT):
            s = t % 4
            for hf in range(2):
                b = P.bank()
                mm_group(banks[b], [(mg[:, fc, t * 128:(t + 1) * 128], wout[:, fc, hf * 512:(hf + 1) * 512]) for fc in range(KC)],
                         reads=["wout%d" % hf] + ["mg%d_%d" % (fc, t // 4) for fc in range(KC)], writes=[B(b)])
                hs = slice(hf * 512, (hf + 1) * 512)
                wr = ["ot%d_%d" % (s, hf)]
                if s in alias and (s, hf) not in first_o:
                    first_o.add((s, hf))
                    wr.append(alias[s])
                P.op("dve", (lambda s_, b_, h_: (lambda e: e.tensor_tensor(out=ot[s_][:, h_], in0=banks[b_], in1=gate_bc[:, h_], op=ALU.mult)))(s, b, hs),
                     reads=[B(b), "gbc%d" % hf], writes=wr)
                P.op("pool", (lambda s_, h_: (lambda e: e.tensor_tensor(out=ot[s_][:, h_], in0=ot[s_][:, h_], in1=xs2[s_][:, h_], op=ALU.add)))(s, hs),
                     reads=["ot%d_%d" % (s, hf), "xs2_%d" % s], writes=["ot%d_%d" % (s, hf)])
            P.op("act", (lambda s_, t_: (lambda e: e.dma_start(out=out_d[t_ * 128:(t_ + 1) * 128, :], in_=ot[s_])))(s, t),
                 reads=["ot%d_0" % s, "ot%d_1" % s], writes=["osb%d" % s], slot="osb%d" % s)
            if t + 4 < NT:
                x2_load(t + 4)
        P.barrier()

    except _Stop:
        P.barrier()

    with ExitStack() as es:
        sems = {}
        for sk in P.count:
            sems[sk] = es.enter_context(nc.semaphore("s_" + sk))
        block = es.enter_context(nc.Block())

        def replay(engname, e):
            for (waits, fn, sk, inc) in P.ops[engname]:
                for (wk, wv) in waits:
                    e.wait_ge(sems[wk], wv)
                if fn is None:
                    continue
                ins = fn(e)
                ins.then_inc(sems[sk], inc)

        @block.tensor
        def _(e):
            replay("pe", e)

        @block.scalar
        def _(e):
            replay("act", e)

        @block.vector
        def _(e):
            replay("dve", e)

        @block.gpsimd
        def _(e):
            replay("pool", e)

        @block.sync
        def _(e):
            replay("sp", e)
    return nc


def _consts():
    c = np.zeros((128, 704), np.float32)
    p = np.arange(128)
    c[p, p] = 1.0
    c[:, 128:256] = ((p[:, None] // 64) == (p[None, :] // 64)).astype(np.float32) / 64.0
    perm = np.zeros((128, 128), np.float32)
    for h in range(2):
        for d in range(8):
            perm[64 * h + d + 8, 64 * h + d] = -1.0
            perm[64 * h + d, 64 * h + d + 8] = 1.0
    c[:, 256:384] = perm
    c[:, 384:512] = (p[None, :] >= p[:, None]).astype(np.float32)
    c[:, 512:640] = (p[:, None] >= p[None, :]).astype(np.float32)
    c[:, 640:704] = 1.0
    freq = np.zeros((128, 2), np.float32)
    fr64 = ROPE_THETA ** (-np.arange(0, 16, 2, dtype=np.float64) / 16.0)
    fr = fr64.astype(np.float32)
    frl = (fr64 - fr.astype(np.float64)).astype(np.float32)
    for h in range(2):
        for d in range(16):
            freq[64 * h + d, 0] = fr[d % 8]
            freq[64 * h + d, 1] = frl[d % 8]
    return c, freq


_NC_CACHE = {}


def make_in_maps(x, c, positions, norm_g, w_ada, b_ada, w_in, q_norm_g, k_norm_g,
                 sgu_ln_g, sgu_ln_b, w_spatial, b_spatial, w_branch_a, w_branch_b, w_out):
    f = np.float32
    cst, freq = _consts()
    gqk = np.zeros((128, 6), f)
    for g in range(3):
        gqk[:, 2 * g] = np.tile(np.asarray(q_norm_g[g], f), 2)
        gqk[:, 2 * g + 1] = np.tile(np.asarray(k_norm_g[g], f), 2)
    wsp = np.ascontiguousarray(np.transpose(np.asarray(w_spatial, f), (2, 0, 1)))
    bsp = np.repeat(np.asarray(b_spatial, f).reshape(4, 2, 128), 64, axis=1)
    bsp = np.ascontiguousarray(np.transpose(bsp, (1, 0, 2)))
    shared = {
        "w_ada": np.ascontiguousarray(w_ada, f),
        "adab": np.ascontiguousarray(np.asarray(b_ada, f).reshape(24, 128).T),
        "normg": np.ascontiguousarray(np.asarray(norm_g, f).reshape(8, 128).T),
        "w_in": np.ascontiguousarray(w_in, f),
        "gqk": gqk,
        "lng": np.ascontiguousarray(np.asarray(sgu_ln_g, f).reshape(1, 512)),
        "lnb": np.ascontiguousarray(np.asarray(sgu_ln_b, f).reshape(1, 512)),
        "wsp": wsp, "bsp": bsp,
        "w_ba": np.ascontiguousarray(w_branch_a, f),
        "w_bb": np.ascontiguousarray(w_branch_b, f),
        "w_out": np.ascontiguousarray(w_out, f),
        "freq": freq, "cst": cst,
    }
    maps = []
    for b in range(8):
        m = dict(shared)
        m["x"] = np.ascontiguousarray(x[b], f)
        m["cT"] = np.ascontiguousarray(np.asarray(c[b], f).reshape(8, 128).T)
        m["pos"] = np.ascontiguousarray(np.asarray(positions[b], np.int32).reshape(1, S))
        maps.append(m)
    return maps


def kernel(**inputs):
    inputs = {k: np.asarray(v) for k, v in inputs.items()}
    if "nc" not in _NC_CACHE:
        _NC_CACHE["nc"] = build_nc()
    nc = _NC_CACHE["nc"]
    in_maps = make_in_maps(**inputs)
    res = run_bass_kernel_spmd(nc, in_maps, core_ids=list(range(8)))
    out = np.stack([np.asarray(r["out"], np.float32) for r in res.results], axis=0)
    return out
```
